# Optimizing a Trainium2 kernel written in Bass

```python
import math
import jax, jax.numpy as jnp
from jax import lax
import numpy as np

D_MODEL = 2048
BATCH = 32
SEQ = 256
DEPTH = 2
DEC_BATCH = 8
DEC_SEQ = 4096
PAST_LEN = 512

GRID_W = 64
MIX_W = D_MODEL // 4
HEAD_DIM = 64
D_FF = 4 * D_MODEL
NORM_EPS = 1e-6

GLA_H = MIX_W // HEAD_DIM
GLA_DK = HEAD_DIM // 2
GLA_DV = HEAD_DIM
GLA_RANK = 16
GLA_TAU = 16.0
GLA_CHUNK = 64

DIFF_H = MIX_W // HEAD_DIM
DIFF_DK = HEAD_DIM // 2
DIFF_DV = HEAD_DIM
Q_BLOCK = 128
ROPE_BASE = 10000.0

SSD_H = MIX_W // HEAD_DIM
SSD_P = HEAD_DIM
SSD_N = 64
SSD_G = 2
SSD_CONV = 3
SSD_CHUNK = 128

HY_CH = MIX_W
HY_SHORT = 3
HY_BANDS = 16
HY_EMB = 2 * HY_BANDS + 1
HY_HID = 64

GLA_QK = GLA_H * GLA_DK
GLA_V = GLA_H * GLA_DV
DIFF_QK = DIFF_H * 2 * DIFF_DK
DIFF_V = DIFF_H * DIFF_DV
SSD_DI = SSD_H * SSD_P
SSD_BC = SSD_G * SSD_N
SSD_XBC = SSD_DI + 2 * SSD_BC
HY_W = 3 * HY_CH
IN_SIZES = (GLA_QK, GLA_QK, GLA_V, GLA_V, 2 * GLA_RANK,
            DIFF_QK, DIFF_QK, DIFF_V,
            SSD_DI, SSD_XBC, 2 * SSD_H,
            HY_W)
IN_W = sum(IN_SIZES)

kernel_name = 'hybrid_prefix_diffusion_step'

f32 = jnp.float32


def rms_norm(x, g):
    xf = x.astype(f32)
    y = xf * lax.rsqrt(jnp.mean(xf * xf, axis=-1, keepdims=True) + NORM_EPS)
    return (y * g.astype(f32)).astype(x.dtype)


def split_cols(t, sizes):
    offs = np.cumsum(sizes)[:-1].tolist()
    return jnp.split(t, offs, axis=-1)


def flip(t):
    return jnp.flip(t, axis=1)


def dwconv_centred(x, w, b):
    K = w.shape[0]
    L = x.shape[1]
    xp = jnp.pad(x, ((0, 0), (K // 2, K // 2), (0, 0)))
    return sum(xp[:, i:i + L] * w[i] for i in range(K)) + b


def adaln(cvec, w, b):
    m = jax.nn.silu(cvec) @ w + b
    return jnp.split(m[:, None, :], 6, axis=-1)


def axial_rope(L):
    rows = L // GRID_W
    r, col = jnp.meshgrid(jnp.arange(rows), jnp.arange(GRID_W), indexing='ij')
    r = r.reshape(-1).astype(f32)
    col = col.reshape(-1).astype(f32)
    nf = DIFF_DK // 4
    inv = ROPE_BASE ** (-jnp.arange(nf, dtype=f32) / nf)
    ang = jnp.concatenate([r[:, None] * inv, col[:, None] * inv], axis=-1)
    return jnp.cos(ang), jnp.sin(ang)


def apply_rope(x, cos, sin):
    half = x.shape[-1] // 2
    x1, x2 = x[..., :half], x[..., half:]
    c = cos[:, None, None, :]
    s = sin[:, None, None, :]
    return jnp.concatenate([x1 * c - x2 * s, x1 * s + x2 * c], axis=-1).astype(x.dtype)


def gla_scan(q, k, v, log_a, s0):
    B, L, H, dk = q.shape
    dv = v.shape[-1]
    C = GLA_CHUNK
    nc = L // C
    qc = q.astype(f32).reshape(B, nc, C, H, dk)
    kc = k.astype(f32).reshape(B, nc, C, H, dk)
    vc = v.astype(f32).reshape(B, nc, C, H, dv)
    gc = log_a.astype(f32).reshape(B, nc, C, H, dk)
    b = jnp.cumsum(gc, axis=2)
    btot = b[:, :, -1]
    q_dec = qc * jnp.exp(b)
    att = jnp.einsum('bnihd,bnjhd->bnhij', q_dec, kc * jnp.exp(-b))
    att = jnp.where(jnp.tril(jnp.ones((C, C), bool)), att, 0.0)
    o_intra = jnp.einsum('bnhij,bnjhv->bnihv', att, vc)
    kv = jnp.einsum('bnjhd,bnjhv->bnhdv', kc * jnp.exp(btot[:, :, None] - b), vc)

    def step(S, xs):
        qd, kvc, bt = xs
        o = jnp.einsum('bihd,bhdv->bihv', qd, S)
        return jnp.exp(bt)[..., None] * S + kvc, o

    S_fin, o_inter = lax.scan(step, s0.astype(f32),
                              (jnp.moveaxis(q_dec, 1, 0), jnp.moveaxis(kv, 1, 0), jnp.moveaxis(btot, 1, 0)))
    o = o_intra + jnp.moveaxis(o_inter, 0, 1)
    return o.reshape(B, L, H, dv).astype(v.dtype), S_fin


def ssd_scan(x, dt, a, bm, cm, s0):
    B, L, H, P = x.shape
    N = bm.shape[-1]
    C = SSD_CHUNK
    nc = L // C
    rep = H // bm.shape[2]
    bh = jnp.repeat(bm, rep, axis=2).astype(f32).reshape(B, nc, C, H, N)
    ch = jnp.repeat(cm, rep, axis=2).astype(f32).reshape(B, nc, C, H, N)
    xc = x.astype(f32).reshape(B, nc, C, H, P)
    dtc = dt.astype(f32).reshape(B, nc, C, H)
    cum = jnp.cumsum(dtc * a, axis=2)
    causal = jnp.tril(jnp.ones((C, C), bool))[None, None, :, :, None]
    seg = cum[:, :, :, None, :] - cum[:, :, None, :, :]
    lmat = jnp.exp(jnp.where(causal, seg, -jnp.inf))
    w = jnp.einsum('bnihd,bnjhd->bnijh', ch, bh) * lmat * dtc[:, :, None, :, :]
    y_intra = jnp.einsum('bnijh,bnjhp->bnihp', w, xc)
    tail = jnp.exp(cum[:, :, -1:, :] - cum) * dtc
    st = jnp.einsum('bnjhd,bnjh,bnjhp->bnhpd', bh, tail, xc)
    chunk_decay = jnp.exp(cum[:, :, -1])
    c_dec = ch * jnp.exp(cum)[..., None]

    def step(S, xs):
        cd, stc, dec = xs
        y = jnp.einsum('bihd,bhpd->bihp', cd, S)
        return dec[:, :, None, None] * S + stc, y

    S_fin, y_inter = lax.scan(step, s0.astype(f32),
                              (jnp.moveaxis(c_dec, 1, 0), jnp.moveaxis(st, 1, 0), jnp.moveaxis(chunk_decay, 1, 0)))
    y = (y_intra + jnp.moveaxis(y_inter, 0, 1)).reshape(B, L, H, P)
    return y.astype(x.dtype), S_fin


def diff_attention(q, k, v, lam):
    B, Lq, H, _, dk = q.shape
    nb = Lq // Q_BLOCK
    qb = jnp.moveaxis(q.reshape(B, nb, Q_BLOCK, H, 2, dk), 1, 0)
    kf = k.astype(f32) * dk ** -0.5
    vf = v.astype(f32)

    def block(qblk):
        s = jnp.einsum('bqhcd,bkhcd->bhcqk', qblk.astype(f32), kf)
        pr = jax.nn.softmax(s, axis=-1)
        amap = pr[:, :, 0] - lam * pr[:, :, 1]
        return jnp.einsum('bhqk,bkhd->bqhd', amap, vf)

    o = lax.map(block, qb)
    return jnp.moveaxis(o, 0, 1).reshape(B, Lq, H, v.shape[-1]).astype(v.dtype)


def hyena_filters(L, p):
    t = (jnp.arange(L, dtype=f32) / L)[:, None]
    bands = jnp.arange(1, HY_BANDS + 1, dtype=f32)
    ang = 2.0 * math.pi * t * bands
    z = jnp.concatenate([t, jnp.cos(ang), jnp.sin(ang)], axis=-1)
    hdn = jnp.sin(p['hy_sin_w'] * (z @ p['hy_f_w1'] + p['hy_f_b1']))
    hdn = jnp.sin(p['hy_sin_w'] * (hdn @ p['hy_f_w2'] + p['hy_f_b2']))
    h = (hdn @ p['hy_f_w3'] + p['hy_f_b3']).astype(f32) * jnp.exp(-t * jnp.abs(p['hy_decay'].astype(f32)))
    h = h.reshape(L, 2, 2, HY_CH)
    return h / (jnp.sum(jnp.abs(h), axis=(0, 2), keepdims=True) + NORM_EPS)


def fft_long_conv(u, hf, hb, skip):
    L = u.shape[1]
    kern = jnp.concatenate([hf, jnp.zeros_like(hf[:1]), hb[:0:-1]], axis=0)
    kf = jnp.fft.rfft(kern, n=2 * L, axis=0)
    uf = jnp.fft.rfft(u.astype(f32), n=2 * L, axis=1)
    y = jnp.fft.irfft(uf * kf[None], n=2 * L, axis=1)[:, :L]
    return (y + u.astype(f32) * skip.astype(f32)).astype(u.dtype)


def mixer_block(u, p, lam_init, rope, ctx):
    B, L, _ = u.shape
    (gq, gk, gv, gg, gr, dq, dkk, dvv, sz, sxbc, sdt, hy) = split_cols(u @ p['w_in'], IN_SIZES)
    if ctx is None:
        gla_s0 = jnp.zeros((B, 2, GLA_H, GLA_DK, GLA_DV), f32)
        ssd_s0 = jnp.zeros((B, 2, SSD_H, SSD_P, SSD_N), f32)
    else:
        ctx_k, ctx_v, gla_s0, ssd_s0 = ctx

    q = gq.reshape(B, L, GLA_H, GLA_DK) * GLA_DK ** -0.5
    k = gk.reshape(B, L, GLA_H, GLA_DK)
    v = gv.reshape(B, L, GLA_H, GLA_DV)
    logits = jnp.einsum('bldr,drk->bldk', gr.reshape(B, L, 2, GLA_RANK), p['gla_gate_w']) + p['gla_gate_b']
    log_a = (jax.nn.log_sigmoid(logits.astype(f32)) / GLA_TAU).reshape(B, L, 2, GLA_H, GLA_DK)
    o_f, gsf = gla_scan(q, k, v, log_a[:, :, 0], gla_s0[:, 0])
    o_b, gsb = gla_scan(flip(q), flip(k), flip(v), flip(log_a[:, :, 1]), gla_s0[:, 1])
    o_gla = rms_norm(o_f + flip(o_b), p['gla_norm_g']).reshape(B, L, GLA_V) * jax.nn.silu(gg)

    q = dq.reshape(B, L, DIFF_H, 2, DIFF_DK)
    k = dkk.reshape(B, L, DIFF_H, 2, DIFF_DK)
    v = dvv.reshape(B, L, DIFF_H, DIFF_DV)
    if rope is not None:
        q = apply_rope(q, rope[0], rope[1])
        k = apply_rope(k, rope[0], rope[1])
    if ctx is None:
        keys, vals = k, v
    else:
        keys = jnp.concatenate([k, ctx_k.reshape(B, -1, DIFF_H, 2, DIFF_DK).astype(k.dtype)], axis=1)
        vals = jnp.concatenate([v, ctx_v.astype(v.dtype)], axis=1)
    lp = p['diff_lambda'].astype(f32)
    lam = jnp.exp(jnp.sum(lp[0] * lp[1])) - jnp.exp(jnp.sum(lp[2] * lp[3])) + lam_init
    o_diff = rms_norm(diff_attention(q, keys, vals, lam), p['diff_norm_g']) * (1.0 - lam_init)
    o_diff = o_diff.reshape(B, L, DIFF_V)

    xbc = jax.nn.silu(dwconv_centred(sxbc, p['ssd_conv_w'], p['ssd_conv_b']))
    xs, bm, cm = split_cols(xbc, (SSD_DI, SSD_BC, SSD_BC))
    xs = xs.reshape(B, L, SSD_H, SSD_P)
    bm = bm.reshape(B, L, SSD_G, SSD_N)
    cm = cm.reshape(B, L, SSD_G, SSD_N)
    dt = jax.nn.softplus(sdt.reshape(B, L, 2, SSD_H).astype(f32) + p['ssd_dt_bias'].astype(f32))
    a = -jnp.exp(p['ssd_a_log'].astype(f32))
    y_f, ssf = ssd_scan(xs, dt[:, :, 0], a[0], bm, cm, ssd_s0[:, 0])
    y_b, ssb = ssd_scan(flip(xs), flip(dt[:, :, 1]), a[1], flip(bm), flip(cm), ssd_s0[:, 1])
    y = y_f + flip(y_b) + xs * p['ssd_d'][:, None]
    o_ssd = rms_norm(y.reshape(B, L, SSD_DI) * jax.nn.silu(sz), p['ssd_norm_g'])

    hv, hx1, hx2 = split_cols(dwconv_centred(hy, p['hy_conv_w'], p['hy_conv_b']), (HY_CH, HY_CH, HY_CH))
    filt = hyena_filters(L, p)
    zz = hx1 * fft_long_conv(hv, filt[:, 0, 0], filt[:, 0, 1], p['hy_skip'][0])
    o_hy = hx2 * fft_long_conv(zz, filt[:, 1, 0], filt[:, 1, 1], p['hy_skip'][1])

    out = jnp.concatenate([o_gla, o_diff, o_ssd, o_hy], axis=-1) @ p['w_out']
    new_ctx = (k.reshape(B, L, DIFF_H, 2 * DIFF_DK), v,
               jnp.stack([gsf, gsb], axis=1), jnp.stack([ssf, ssb], axis=1))
    return out, new_ctx


def trunk_layer(x, cvec, p, lam_init, rope, ctx):
    sh1, sc1, g1, sh2, sc2, g2 = adaln(cvec, p['ada_w'], p['ada_b'])
    u = rms_norm(x, p['norm1_g']) * (1.0 + sc1) + sh1
    mix, new_ctx = mixer_block(u, p, lam_init, rope, ctx)
    x = x + g1 * mix
    u = rms_norm(x, p['norm2_g']) * (1.0 + sc2) + sh2
    x = x + g2 * (jnp.square(jax.nn.relu(u @ p['mlp_w1'])) @ p['mlp_w2'])
    return x, new_ctx


def setup_inputs(seed: int = 0) -> dict:
    key = jax.random.key(seed)
    ks = iter(jax.random.split(key, 48))

    def nrm(shape, scale):
        return jax.random.normal(next(ks), shape, f32) * scale

    def unif(shape, lo, hi):
        return jax.random.uniform(next(ks), shape, f32, lo, hi)

    dt0 = jnp.exp(unif((DEPTH, 2, SSD_H), math.log(1e-3), math.log(1e-1)))
    return {
        'x_prompt': nrm((BATCH, SEQ, D_MODEL), 1.0),
        'x_sample': nrm((DEC_BATCH, DEC_SEQ, D_MODEL), 1.0),
        'cache_diff_k': nrm((DEC_BATCH, DEPTH, PAST_LEN, DIFF_H, 2 * DIFF_DK), 1.0),
        'cache_diff_v': nrm((DEC_BATCH, DEPTH, PAST_LEN, DIFF_H, DIFF_DV), 1.0),
        'state_gla': nrm((DEC_BATCH, DEPTH, 2, GLA_H, GLA_DK, GLA_DV), 0.5),
        'state_ssd': nrm((DEC_BATCH, DEPTH, 2, SSD_H, SSD_P, SSD_N), 0.5),
        'c': nrm((DEC_BATCH, D_MODEL), 1.0),
        'c_ctx': nrm((D_MODEL,), 1.0),
        'ada_w': nrm((DEPTH, D_MODEL, 6 * D_MODEL), D_MODEL ** -0.5),
        'ada_b': nrm((DEPTH, 6 * D_MODEL), 0.02),
        'norm1_g': 1.0 + nrm((DEPTH, D_MODEL), 0.02),
        'norm2_g': 1.0 + nrm((DEPTH, D_MODEL), 0.02),
        'w_in': nrm((DEPTH, D_MODEL, IN_W), D_MODEL ** -0.5),
        'w_out': nrm((DEPTH, D_MODEL, D_MODEL), D_MODEL ** -0.5),
        'gla_gate_w': nrm((DEPTH, 2, GLA_RANK, GLA_QK), GLA_RANK ** -0.5),
        'gla_gate_b': nrm((DEPTH, 2, GLA_QK), 0.1),
        'gla_norm_g': 1.0 + nrm((DEPTH, GLA_DV), 0.02),
        'diff_lambda': nrm((DEPTH, 4, DIFF_DK), 0.1),
        'diff_norm_g': 1.0 + nrm((DEPTH, DIFF_DV), 0.02),
        'ssd_conv_w': nrm((DEPTH, SSD_CONV, SSD_XBC), SSD_CONV ** -0.5),
        'ssd_conv_b': nrm((DEPTH, SSD_XBC), 0.02),
        'ssd_dt_bias': dt0 + jnp.log(-jnp.expm1(-dt0)),
        'ssd_a_log': jnp.log(unif((DEPTH, 2, SSD_H), 1.0, 16.0)),
        'ssd_d': 1.0 + nrm((DEPTH, SSD_H), 0.1),
        'ssd_norm_g': 1.0 + nrm((DEPTH, SSD_DI), 0.02),
        'hy_conv_w': nrm((DEPTH, HY_SHORT, HY_W), HY_SHORT ** -0.5),
        'hy_conv_b': nrm((DEPTH, HY_W), 0.02),
        'hy_f_w1': nrm((DEPTH, HY_EMB, HY_HID), HY_EMB ** -0.5),
        'hy_f_b1': nrm((DEPTH, HY_HID), 0.1),
        'hy_f_w2': nrm((DEPTH, HY_HID, HY_HID), HY_HID ** -0.5),
        'hy_f_b2': nrm((DEPTH, HY_HID), 0.1),
        'hy_f_w3': nrm((DEPTH, HY_HID, 4 * HY_CH), HY_HID ** -0.5),
        'hy_f_b3': nrm((DEPTH, 4 * HY_CH), 0.1),
        'hy_sin_w': 1.0 + nrm((DEPTH, HY_HID), 0.1),
        'hy_decay': unif((DEPTH, 4 * HY_CH), 3.0, 15.0),
        'hy_skip': nrm((DEPTH, 2, HY_CH), 0.1),
        'mlp_w1': nrm((DEPTH, D_MODEL, D_FF), D_MODEL ** -0.5),
        'mlp_w2': nrm((DEPTH, D_FF, D_MODEL), D_FF ** -0.5),
        'final_g': 1.0 + nrm((D_MODEL,), 0.02),
    }


def reference(x_prompt, x_sample, cache_diff_k, cache_diff_v, state_gla, state_ssd, c, c_ctx,
              ada_w, ada_b, norm1_g, norm2_g, w_in, w_out,
              gla_gate_w, gla_gate_b, gla_norm_g, diff_lambda, diff_norm_g,
              ssd_conv_w, ssd_conv_b, ssd_dt_bias, ssd_a_log, ssd_d, ssd_norm_g,
              hy_conv_w, hy_conv_b, hy_f_w1, hy_f_b1, hy_f_w2, hy_f_b2, hy_f_w3, hy_f_b3,
              hy_sin_w, hy_decay, hy_skip, mlp_w1, mlp_w2, final_g):
    rope = axial_rope(x_sample.shape[1])
    hp = x_prompt
    hs = x_sample
    ks_, vs_, gs_, ss_ = [], [], [], []
    for l in range(DEPTH):
        p = dict(ada_w=ada_w[l], ada_b=ada_b[l], norm1_g=norm1_g[l], norm2_g=norm2_g[l],
                 w_in=w_in[l], w_out=w_out[l],
                 gla_gate_w=gla_gate_w[l], gla_gate_b=gla_gate_b[l], gla_norm_g=gla_norm_g[l],
                 diff_lambda=diff_lambda[l], diff_norm_g=diff_norm_g[l],
                 ssd_conv_w=ssd_conv_w[l], ssd_conv_b=ssd_conv_b[l], ssd_dt_bias=ssd_dt_bias[l],
                 ssd_a_log=ssd_a_log[l], ssd_d=ssd_d[l], ssd_norm_g=ssd_norm_g[l],
                 hy_conv_w=hy_conv_w[l], hy_conv_b=hy_conv_b[l], hy_f_w1=hy_f_w1[l], hy_f_b1=hy_f_b1[l],
                 hy_f_w2=hy_f_w2[l], hy_f_b2=hy_f_b2[l], hy_f_w3=hy_f_w3[l], hy_f_b3=hy_f_b3[l],
                 hy_sin_w=hy_sin_w[l], hy_decay=hy_decay[l], hy_skip=hy_skip[l],
                 mlp_w1=mlp_w1[l], mlp_w2=mlp_w2[l])
        lam_init = 0.8 - 0.6 * math.exp(-0.3 * l)
        hp, (k_l, v_l, g_l, s_l) = trunk_layer(hp, c_ctx[None], p, lam_init, None, None)
        ks_.append(k_l)
        vs_.append(v_l)
        gs_.append(g_l)
        ss_.append(s_l)
        hs, _ = trunk_layer(hs, c, p, lam_init, rope,
                            (cache_diff_k[:, l], cache_diff_v[:, l], state_gla[:, l], state_ssd[:, l]))
    y_prompt = rms_norm(hp, final_g)
    y_sample = rms_norm(hs, final_g)
    new_diff_k = jnp.stack(ks_, axis=1)
    new_diff_v = jnp.stack(vs_, axis=1)
    new_gla = jnp.stack(gs_, axis=1)
    new_ssd = jnp.stack(ss_, axis=1)
    return (y_prompt, y_sample, new_diff_k, new_diff_v, new_gla, new_ssd)
```

```python
import contextlib
import math
import numpy as np
import concourse.bass as bass
import concourse.mybir as mybir
from concourse.bass_utils import run_bass_kernel_spmd

F32 = mybir.dt.float32
BF16 = mybir.dt.bfloat16
AF = mybir.ActivationFunctionType
ALU = mybir.AluOpType

NCORES = 8
D = 2048
DEPTH = 2
LS = 4096
LP = 256
NPR = 4
T = LS + NPR * LP
G = 512
NG = T // G
IN_W = 5936
D_FF = 8192
EPS = 1e-6

SAME_ENGINE_SYNC = True
NCD = dict(allow_slow_non_contiguous=True)
ENABLE_SSD = True
ENABLE_HY = True


class Buf:
    __slots__ = ("name", "w", "r")

    def __init__(self, name=""):
        self.name = name
        self.w = None
        self.r = []


class Sched:
    ENGS = ("tensor", "vector", "scalar", "gpsimd", "sync")

    def __init__(self, nc, n_dma_sems=12):
        self.nc = nc
        self.prog = {e: [] for e in self.ENGS}
        self.cnt = {e: 0 for e in self.ENGS}
        self.known = {e: {} for e in self.ENGS}
        self.sems = {}
        self._ctx = []
        for e in self.ENGS:
            self._mk("c_" + e)
        self.dma_ring = {}
        for q in ("sync", "gpsimd", "scalar"):
            keys = []
            for i in range(n_dma_sems):
                k = "d_%s_%d" % (q, i)
                self._mk(k)
                keys.append(k)
            self.dma_ring[q] = [keys, 0, {k: 0 for k in keys}]

    def _mk(self, key):
        cm = self.nc.semaphore(key)
        h = cm.__enter__()
        self._ctx.append(cm)
        self.sems[key] = h

    def _need(self, eng, tok, waits, force=False):
        if tok is None:
            return
        k, v = tok
        if (not force) and k == "c_" + eng and (eng == "tensor" or not SAME_ENGINE_SYNC):
            return
        if self.known[eng].get(k, 0) >= v:
            return
        waits[k] = max(waits.get(k, 0), v)

    def _deps(self, eng, reads, writes, force=False):
        waits = {}
        for b in reads:
            self._need(eng, b.w, waits, force)
        for b in writes:
            self._need(eng, b.w, waits, force)
            for t in b.r:
                self._need(eng, t, waits, force)
        return waits

    def _emit_waits(self, eng, waits):
        for k, v in waits.items():
            self.known[eng][k] = v
            h = self.sems[k]
            self.prog[eng].append(lambda E, h=h, v=v: E.wait_ge(h, v))

    def _mark(self, tok, reads, writes):
        for b in reads:
            b.r.append(tok)
            if len(b.r) > 64:
                b.r = b.r[-64:] if False else b.r
        for b in writes:
            b.w = tok
            b.r = []

    def op(self, eng, fn, reads=(), writes=(), sreads=()):
        waits = self._deps(eng, reads, writes)
        for k, v in self._deps(eng, sreads, (), force=True).items():
            waits[k] = max(waits.get(k, 0), v)
        reads = list(reads) + list(sreads)
        self._emit_waits(eng, waits)
        self.cnt[eng] += 1
        v = self.cnt[eng]
        h = self.sems["c_" + eng]
        self.prog[eng].append(lambda E, fn=fn, h=h: fn(E).then_inc(h, 1))
        self._mark(("c_" + eng, v), reads, writes)

    def dma(self, q, out, in_, reads=(), writes=(), **kw):
        keys, pos, vals = self.dma_ring[q]
        k = keys[pos % len(keys)]
        self.dma_ring[q][1] = pos + 1
        waits = self._deps(q, reads, writes, force=True)
        if vals[k] > 0 and self.known[q].get(k, 0) < vals[k]:
            waits[k] = max(waits.get(k, 0), vals[k])
        self._emit_waits(q, waits)
        vals[k] += 16
        v = vals[k]
        h = self.sems[k]
        self.prog[q].append(
            lambda E, out=out, in_=in_, h=h, kw=kw: E.dma_start(out=out, in_=in_, **kw).then_inc(h, 16))
        self._mark((k, v), reads, writes)

    def barrier(self):
        for e in self.ENGS:
            waits = {}
            for o in self.ENGS:
                if o != e and self.cnt[o] > 0:
                    self._need(e, ("c_" + o, self.cnt[o]), waits)
            for q in self.dma_ring:
                for k, v in self.dma_ring[q][2].items():
                    if v > 0:
                        self._need(e, (k, v), waits)
            self._emit_waits(e, waits)

    def emit(self):
        nc = self.nc
        with nc.Block() as block:
            for e in self.ENGS:
                lst = self.prog[e]
                if not lst:
                    continue

                def body(E, lst=lst):
                    for f in lst:
                        f(E)
                getattr(block, e)(body)
        for cm in reversed(self._ctx):
            cm.__exit__(None, None, None)


class Tile:
    def __init__(self, t, name=""):
        self.t = t
        self.b = Buf(name)


_UID = [0]


class Ring:
    def __init__(self, es, nc, name, n, shape, dtype, psum=False):
        self.tiles = []
        _UID[0] += 1
        name = "%s_%d_" % (name, _UID[0])
        for i in range(n):
            if psum:
                t = es.enter_context(nc.psum_tensor("p_%s%d" % (name, i), shape, dtype))
            else:
                t = es.enter_context(nc.sbuf_tensor("s_%s%d" % (name, i), shape, dtype))
            self.tiles.append(Tile(t, "%s%d" % (name, i)))
        self.i = 0

    def next(self):
        t = self.tiles[self.i % len(self.tiles)]
        self.i += 1
        return t


class RingView:
    def __init__(self, tiles):
        self.tiles = list(tiles)
        self.i = 0

    def next(self):
        t = self.tiles[self.i % len(self.tiles)]
        self.i += 1
        return t


def dap(t, offset, dims):
    return bass.AP(t, offset, [list(d) for d in dims])


class Prog:
    def __init__(self, n_layers=DEPTH, mixers=True, debug=False):
        self.debug = debug
        self.n_layers = n_layers
        self.mixers = mixers
        nc = self.nc = bass.Bass("TRN2", target_bir_lowering=False)
        self.S = Sched(nc)
        self.inp = {}
        self.out = {}

    def din(self, name, shape):
        t = self.nc.dram_tensor(name, list(shape), F32, kind="ExternalInput")
        self.inp[name] = t
        return t

    def dout(self, name, shape):
        t = self.nc.dram_tensor(name, list(shape), F32, kind="ExternalOutput")
        self.out[name] = t
        return t

    def dscr(self, name, shape, dtype=F32):
        return self.nc.dram_tensor(name, list(shape), dtype, kind="Internal")

    def wtile(self, W, base, ld, k0, n0, ncols, nk=16):
        wt = self.wring.next()
        src = dap(W, base + k0 * ld + n0, [[ld, 128], [128 * ld, nk], [1, ncols]])
        self.S.dma("gpsimd", wt.t[:, 0:nk, 0:ncols], src, writes=[wt.b])
        return wt

    def evac(self, i, out, in_, reads, writes):
        if i % 2 == 0:
            self.S.op("scalar", lambda E: E.activation(out=out, in_=in_, func=AF.Copy), reads=reads, writes=writes)
        else:
            self.S.op("vector", lambda E: E.tensor_copy(out=out, in_=in_), reads=reads, writes=writes)

    def build(self):
        nc, S = self.nc, self.S
        L = self.n_layers
        xin = self.din("xin", [T, D])
        c2 = self.din("c2", [2, D])
        ada_w = self.din("ada_w", [DEPTH, D, 6 * D])
        ada_b = self.din("ada_b", [DEPTH, 6 * D])
        norm1_g = self.din("norm1_g", [DEPTH, D])
        norm2_g = self.din("norm2_g", [DEPTH, D])
        w_in = self.din("w_in", [DEPTH, D, IN_W])
        w_out = self.din("w_out", [DEPTH, D, D])
        mlp_w1 = self.din("mlp_w1", [DEPTH, D, D_FF])
        mlp_w2 = self.din("mlp_w2", [DEPTH, D_FF, D])
        final_g = self.din("final_g", [D])
        ident_d = self.din("ident", [128, 128])
        if self.mixers:
            self.cache_k = self.din("cache_k", [DEPTH, 512, 512])
            self.cache_v = self.din("cache_v", [DEPTH, 512, 512])
            self.diff_lambda = self.din("diff_lambda", [DEPTH, 4, 32])
            self.diff_norm_g = self.din("diff_norm_g", [DEPTH, 64])
            self.ropec = self.din("ropec", [128, LS])
            self.ropes = self.din("ropes", [128, LS])
            self.roper = self.din("roper", [128, 128])
            self.rowmask = self.din("rowmask", [128, 4])
            self.gla_tri = self.din("gla_tri", [6, 128, 128])
            self.gla_bmask = self.din("gla_bmask", [128, 256])
            self.gla_gate_w = self.din("gla_gate_w", [DEPTH, 2, 16, 256])
            self.gla_gate_b = self.din("gla_gate_b", [DEPTH, 2, 256])
            self.gla_norm_g = self.din("gla_norm_g", [DEPTH, 64])
            self.state_gla = self.din("state_gla", [DEPTH, 2, 8, 32, 64])
            self.ngla_d = self.dout("ngla", [NPR, DEPTH, 2, 8, 32, 64])
            if ENABLE_SSD:
              self.ssd_tri = self.din("ssd_tri", [6, 128, 128])
              self.ssd_gmask = self.din("ssd_gmask", [128, 2])
              self.ssd_conv_w = self.din("ssd_conv_w", [DEPTH, 3, 768])
              self.ssd_conv_b = self.din("ssd_conv_b", [DEPTH, 768])
              self.ssd_dt_bias = self.din("ssd_dt_bias", [DEPTH, 2, 8])
              self.ssd_a_log = self.din("ssd_a_log", [DEPTH, 2, 8])
              self.ssd_d = self.din("ssd_d", [DEPTH, 8])
              self.ssd_norm_g = self.din("ssd_norm_g", [DEPTH, 512])
              self.state_ssd = self.din("state_ssd", [DEPTH, 2, 8, 64, 64])
              self.nssd_d = self.dout("nssd", [NPR, DEPTH, 2, 8, 64, 64])
            if ENABLE_HY:
                self.hy_z_s = self.din("hy_z_s", [33, LS])
                self.hy_z_p = self.din("hy_z_p", [33, LP])
                self.hy_conv_w = self.din("hy_conv_w", [DEPTH, 3, 1536])
                self.hy_conv_b = self.din("hy_conv_b", [DEPTH, 1536])
                self.hy_f_w1 = self.din("hy_f_w1", [DEPTH, 33, 64])
                self.hy_f_b1 = self.din("hy_f_b1", [DEPTH, 64])
                self.hy_f_w2 = self.din("hy_f_w2", [DEPTH, 64, 64])
                self.hy_f_b2 = self.din("hy_f_b2", [DEPTH, 64])
                self.hy_f_w3 = self.din("hy_f_w3", [DEPTH, 64, 2048])
                self.hy_f_b3 = self.din("hy_f_b3", [DEPTH, 2048])
                self.hy_sin_w = self.din("hy_sin_w", [DEPTH, 64])
                self.hy_decay = self.din("hy_decay", [DEPTH, 2048])
                self.hy_skip = self.din("hy_skip", [DEPTH, 2, 512])
            self.nk_d = self.dout("nk", [NPR, DEPTH, LP, 512])
            self.nv_d = self.dout("nv", [NPR, DEPTH, LP, 512])
        self.ident = None
        y_d = self.dout("y", [T, D])
        xT_d = self.dscr("xT", [D, T])
        pT_d = self.dscr("pT", [IN_W, T])
        oT_d = self.dscr("oT", [D, T])
        self.obufs = []

        with contextlib.ExitStack() as es:
            ident = Tile(es.enter_context(nc.sbuf_tensor("s_ident", [128, 128], F32)))
            onesb = Tile(es.enter_context(nc.sbuf_tensor("s_onesb", [128, 128], BF16)))
            modT = Tile(es.enter_context(nc.sbuf_tensor("s_modT", [128, 96, 2], F32)))
            sc1 = Tile(es.enter_context(nc.sbuf_tensor("s_sc1", [128, 16, 2], F32)))
            sc2 = Tile(es.enter_context(nc.sbuf_tensor("s_sc2", [128, 16, 2], F32)))
            cT = Tile(es.enter_context(nc.sbuf_tensor("s_cT", [128, 16, 2], BF16)))
            cTf = Tile(es.enter_context(nc.sbuf_tensor("s_cTf", [128, 16, 2], F32)))
            gvec = Tile(es.enter_context(nc.sbuf_tensor("s_gvec", [128, 16], F32)))
            adab = Tile(es.enter_context(nc.sbuf_tensor("s_adab", [128, 96], F32)))
            fing = Tile(es.enter_context(nc.sbuf_tensor("s_fing", [128, 16], F32)))
            self.ps = Ring(es, nc, "ps", 6, [128, 512], F32, psum=True)
            psr = Ring(es, nc, "psr", 2, [128, 512], F32, psum=True)
            self.psr = psr

            S.dma("sync", ident.t[:], ident_d.ap(), writes=[ident.b])
            self.ident = ident
            S.op("vector", lambda E: E.memset(onesb.t[:], 1.0), writes=[onesb.b])
            for cv in range(2):
                S.dma("sync", cTf.t[:, :, cv], dap(c2, cv * D, [[1, 128], [128, 16]]), writes=[cTf.b], **NCD)
            S.dma("sync", fing.t[:], dap(final_g, 0, [[1, 128], [128, 16]]), writes=[fing.b], **NCD)
            S.op("scalar", lambda E: E.activation(out=cT.t[:], in_=cTf.t[:], func=AF.Silu), reads=[cTf.b], writes=[cT.b])

            with contextlib.ExitStack() as e0:
                xtok = Ring(e0, nc, "xtok", 2, [128, D], F32)
                xTt = Ring(e0, nc, "xTt", 2, [128, 16, G], F32)
                xTb = Buf("xT_d")
                for g in range(NG):
                    dst = xTt.next()
                    for s in range(4):
                        tt = g * 4 + s
                        xt = xtok.next()
                        S.dma("sync", xt.t[:], xin.ap()[tt * 128:(tt + 1) * 128, :], writes=[xt.b])
                        for q in range(4):
                            p = self.ps.next()
                            for j in range(4):
                                fc = q * 4 + j
                                S.op("tensor", lambda E, p=p, xt=xt, fc=fc, j=j: E.transpose(
                                    out=p.t[:, j * 128:(j + 1) * 128], in_=xt.t[:, fc * 128:(fc + 1) * 128],
                                    identity=ident.t[:]), reads=[xt.b, ident.b], writes=[p.b])
                            self.evac(q, dst.t[:, q * 4:(q + 1) * 4, s * 128:(s + 1) * 128],
                                      p.t[:, :].rearrange("p (a b) -> p a b", a=4), [p.b], [dst.b])
                    S.dma("sync", dap(xT_d, g * G, [[T, 128], [128 * T, 16], [1, G]]), dst.t[:],
                          reads=[dst.b], writes=[xTb])
            S.barrier()

            for l in range(L):
                ewa = contextlib.ExitStack()
                self.wring = Ring(ewa, nc, "wt", 3, [128, 16, 512], BF16)
                S.dma("sync", adab.t[:], dap(ada_b, l * 6 * D, [[1, 128], [128, 96]]), writes=[adab.b], **NCD)
                for nb in range(24):
                    wt = self.wtile(ada_w, l * D * 6 * D, 6 * D, 0, nb * 512, 512)
                    p = self.ps.next()
                    for j in range(4):
                        for kc in range(16):
                            S.op("tensor", lambda E, p=p, wt=wt, j=j, kc=kc: E.matmul(
                                p.t[:, j * 2:(j + 1) * 2], wt.t[:, kc, j * 128:(j + 1) * 128], cT.t[:, kc, :],
                                start=(kc == 0), stop=(kc == 15)), reads=[wt.b, cT.b], writes=[p.b])
                    S.op("vector", lambda E, p=p, nb=nb: E.tensor_tensor(
                        out=modT.t[:, nb * 4:(nb + 1) * 4, :],
                        in0=p.t[:, 0:8].rearrange("p (a b) -> p a b", a=4),
                        in1=adab.t[:, nb * 4:(nb + 1) * 4].unsqueeze(2).broadcast_to([128, 4, 2]),
                        op=ALU.add), reads=[p.b, adab.b], writes=[modT.b])
                for (sct, ng, j6) in ((sc1, norm1_g, 1), (sc2, norm2_g, 4)):
                    S.dma("sync", gvec.t[:], dap(ng, l * D, [[1, 128], [128, 16]]), writes=[gvec.b], **NCD)
                    S.op("vector", lambda E, sct=sct, j6=j6: E.tensor_scalar(
                        out=sct.t[:], in0=modT.t[:, j6 * 16:(j6 + 1) * 16, :], scalar1=1.0, scalar2=math.sqrt(D),
                        op0=ALU.add, op1=ALU.mult), reads=[modT.b], writes=[sct.b])
                    S.op("vector", lambda E, sct=sct: E.tensor_tensor(
                        out=sct.t[:], in0=sct.t[:], in1=gvec.t[:].unsqueeze(2).broadcast_to([128, 16, 2]),
                        op=ALU.mult), reads=[gvec.b, sct.b], writes=[sct.b])
                S.barrier()

                with contextlib.ExitStack() as eb:
                    xg = Ring(eb, nc, "xg", 2, [128, 16, G], F32)
                    uT = Ring(eb, nc, "uT", 2, [128, 16, G], BF16)
                    sq = Ring(eb, nc, "sq", 2, [128, G], BF16)
                    tmp = Ring(eb, nc, "tmp", 2, [128, G], F32)
                    rstd = Ring(eb, nc, "rstd", 2, [128, G], F32)
                    stg = Ring(eb, nc, "stg", 3, [128, G], F32)
                    pTb = Buf("pT_d")
                    for g in range(NG):
                        cv = 0 if g < LS // G else 1
                        x = xg.next()
                        S.dma("sync", x.t[:], dap(xT_d, g * G, [[T, 128], [128 * T, 16], [1, G]]), writes=[x.b])
                        u = uT.next()
                        self.norm_mod(x, u, sq, tmp, rstd, psr, onesb, sc1, modT, 0, cv)
                        ei = 0
                        for nb in range(12):
                            ncols = min(512, IN_W - nb * 512)
                            wt = self.wtile(w_in, l * D * IN_W, IN_W, 0, nb * 512, ncols)
                            for j in range((ncols + 127) // 128):
                                M = min(128, ncols - j * 128)
                                p = self.ps.next()
                                for kc in range(16):
                                    S.op("tensor", lambda E, p=p, wt=wt, u=u, j=j, kc=kc, M=M: E.matmul(
                                        p.t[0:M, :], wt.t[:, kc, j * 128:j * 128 + M], u.t[:, kc, :],
                                        start=(kc == 0), stop=(kc == 15)), reads=[wt.b, u.b], writes=[p.b])
                                st = stg.next()
                                self.evac(ei, st.t[0:M, :], p.t[0:M, :], [p.b], [st.b])
                                ei += 1
                                r0 = nb * 512 + j * 128
                                S.dma("sync", pT_d.ap()[r0:r0 + M, g * G:(g + 1) * G], st.t[0:M, :],
                                      reads=[st.b], writes=[pTb])
                        if self.mixers and cv == 1:
                            for (c0, od) in ((2080, self.nk_d), (2592, self.nv_d)):
                                wt = self.wtile(w_in, l * D * IN_W, IN_W, 0, c0, 512)
                                for s4 in range(4):
                                    p = self.ps.next()
                                    for kc in range(16):
                                        S.op("tensor", lambda E, p=p, wt=wt, u=u, s4=s4, kc=kc: E.matmul(
                                            p.t[:, :], u.t[:, kc, s4 * 128:(s4 + 1) * 128], wt.t[:, kc, :],
                                            start=(kc == 0), stop=(kc == 15)), reads=[wt.b, u.b], writes=[p.b])
                                    st = stg.next()
                                    self.evac(ei, st.t[:, :], p.t[:, :], [p.b], [st.b])
                                    ei += 1
                                    tokp = (g - LS // G) * G + s4 * 128
                                    pi, pos = tokp // LP, tokp % LP
                                    kvb = Buf("kv_out")
                                    S.dma("sync", dap(od, ((pi * DEPTH + l) * LP + pos) * 512, [[512, 128], [1, 512]]),
                                          st.t[:, :], reads=[st.b], writes=[kvb])
                S.barrier()
                ewa.close()

                self.mixer_phase(l, pT_d, oT_d)
                S.barrier()

                with contextlib.ExitStack() as ee:
                    self.wring = Ring(ee, nc, "wt", 3, [128, 16, 512], BF16)
                    xg = Ring(ee, nc, "xg", 1, [128, 16, G], F32)
                    oT = Ring(ee, nc, "oTt", 1, [128, 16, G], BF16)
                    uT = Ring(ee, nc, "uT", 1, [128, 16, G], BF16)
                    hT = Ring(ee, nc, "hT", 1, [128, 64, G], BF16)
                    sq = Ring(ee, nc, "sq", 2, [128, G], BF16)
                    tmp = Ring(ee, nc, "tmp", 2, [128, G], F32)
                    rstd = Ring(ee, nc, "rstd", 1, [128, G], F32)
                    rl = Ring(ee, nc, "rl", 3, [128, G], BF16)
                    ytok = Ring(ee, nc, "ytok", 2, [128, D], F32)
                    xTb = Buf("xT_d")
                    last = (l == L - 1)
                    for g in range(NG):
                        cv = 0 if g < LS // G else 1
                        x = xg.next()
                        S.dma("sync", x.t[:], dap(xT_d, g * G, [[T, 128], [128 * T, 16], [1, G]]),
                              reads=[], writes=[x.b])
                        o = oT.next()
                        S.dma("gpsimd", o.t[:], dap(oT_d, g * G, [[T, 128], [128 * T, 16], [1, G]]), writes=[o.b])
                        for nb in range(4):
                            wt = self.wtile(w_out, l * D * D, D, 0, nb * 512, 512)
                            for j in range(4):
                                fc = nb * 4 + j
                                p = self.ps.next()
                                for kc in range(16):
                                    S.op("tensor", lambda E, p=p, wt=wt, o=o, j=j, kc=kc: E.matmul(
                                        p.t[:, :], wt.t[:, kc, j * 128:(j + 1) * 128], o.t[:, kc, :],
                                        start=(kc == 0), stop=(kc == 15)), reads=[wt.b, o.b], writes=[p.b])
                                S.op("vector", lambda E, p=p, x=x, fc=fc, cv=cv: E.scalar_tensor_tensor(
                                    out=x.t[:, fc, :], in0=p.t[:, :], scalar=modT.t[:, 2 * 16 + fc, cv:cv + 1],
                                    in1=x.t[:, fc, :], op0=ALU.mult, op1=ALU.add),
                                    reads=[p.b, x.b, modT.b], writes=[x.b])
                        u = uT.next()
                        self.norm_mod(x, u, sq, tmp, rstd, psr, onesb, sc2, modT, 3, cv)
                        h = hT.next()
                        for nb in range(16):
                            wt = self.wtile(mlp_w1, l * D * D_FF, D_FF, 0, nb * 512, 512)
                            for j in range(4):
                                ffc = nb * 4 + j
                                p = self.ps.next()
                                for kc in range(16):
                                    S.op("tensor", lambda E, p=p, wt=wt, u=u, j=j, kc=kc: E.matmul(
                                        p.t[:, :], wt.t[:, kc, j * 128:(j + 1) * 128], u.t[:, kc, :],
                                        start=(kc == 0), stop=(kc == 15)), reads=[wt.b, u.b], writes=[p.b])
                                r = rl.next()
                                S.op("scalar", lambda E, p=p, r=r: E.activation(out=r.t[:], in_=p.t[:, :], func=AF.Relu),
                                     reads=[p.b], writes=[r.b])
                                S.op("gpsimd", lambda E, r=r, h=h, ffc=ffc: E.tensor_tensor(
                                    out=h.t[:, ffc, :], in0=r.t[:], in1=r.t[:], op=ALU.mult),
                                    reads=[r.b], writes=[h.b])
                        for nb in range(4):
                            pp = [self.ps.next() for _ in range(4)]
                            for kq in range(4):
                                wt = self.wtile(mlp_w2, l * D_FF * D, D, kq * 2048, nb * 512, 512)
                                for j in range(4):
                                    p = pp[j]
                                    for kc in range(16):
                                        S.op("tensor", lambda E, p=p, wt=wt, h=h, j=j, kc=kc, kq=kq: E.matmul(
                                            p.t[:, :], wt.t[:, kc, j * 128:(j + 1) * 128], h.t[:, kq * 16 + kc, :],
                                            start=(kq == 0 and kc == 0), stop=(kq == 3 and kc == 15)),
                                            reads=[wt.b, h.b], writes=[p.b])
                            for j in range(4):
                                fc = nb * 4 + j
                                p = pp[j]
                                S.op("vector", lambda E, p=p, x=x, fc=fc, cv=cv: E.scalar_tensor_tensor(
                                    out=x.t[:, fc, :], in0=p.t[:, :], scalar=modT.t[:, 5 * 16 + fc, cv:cv + 1],
                                    in1=x.t[:, fc, :], op0=ALU.mult, op1=ALU.add),
                                    reads=[p.b, x.b, modT.b], writes=[x.b])
                        if not last:
                            S.dma("sync", dap(xT_d, g * G, [[T, 128], [128 * T, 16], [1, G]]), x.t[:],
                                  reads=[x.b], writes=[xTb])
                        else:
                            self.final_norm_store(x, u, sq, tmp, rstd, psr, onesb, fing, ident, ytok, y_d, g)
                S.barrier()

            S.emit()
        return nc

    def sumsq_bcast(self, x, sq, psr, onesb):
        S = self.S
        pr = psr.next()
        for fc in range(16):
            s = sq.next()
            S.op("scalar", lambda E, s=s, x=x, fc=fc: E.activation(out=s.t[:], in_=x.t[:, fc, :], func=AF.Square),
                 reads=[x.b], writes=[s.b])
            S.op("tensor", lambda E, pr=pr, s=s, fc=fc: E.matmul(pr.t[:, :], onesb.t[:], s.t[:],
                                                                 start=(fc == 0), stop=(fc == 15)),
                 reads=[s.b, onesb.b], writes=[pr.b])
        return pr

    def rsqrt(self, rs, pr, scale, bias):
        S = self.S
        S.op("scalar", lambda E: E.activation(out=rs.t[:], in_=pr.t[:, :], func=AF.Sqrt, scale=scale, bias=bias),
             reads=[pr.b], writes=[rs.b])
        S.op("vector", lambda E: E.reciprocal(out=rs.t[:], in_=rs.t[:]), reads=[rs.b], writes=[rs.b])

    def norm_mod(self, x, u, sq, tmp, rstd, psr, onesb, sct, modT, jsh, cv):
        S = self.S
        pr = self.sumsq_bcast(x, sq, psr, onesb)
        rs = rstd.next()
        self.rsqrt(rs, pr, 1.0, D * EPS)
        for fc in range(16):
            t = tmp.next()
            S.op("vector", lambda E, t=t, fc=fc: E.tensor_tensor(out=t.t[:], in0=x.t[:, fc, :], in1=rs.t[:], op=ALU.mult),
                 reads=[x.b, rs.b], writes=[t.b])
            S.op("scalar", lambda E, t=t, fc=fc: E.activation(
                out=u.t[:, fc, :], in_=t.t[:], func=AF.Identity,
                scale=sct.t[:, fc, cv:cv + 1], bias=modT.t[:, jsh * 16 + fc, cv:cv + 1]),
                reads=[t.b, sct.b, modT.b], writes=[u.b])

    def final_norm_store(self, x, u, sq, tmp, rstd, psr, onesb, fing, ident, ytok, y_d, g):
        S = self.S
        pr = self.sumsq_bcast(x, sq, psr, onesb)
        rs = rstd.next()
        self.rsqrt(rs, pr, 1.0 / D, EPS)
        for fc in range(16):
            S.op("vector", lambda E, fc=fc: E.scalar_tensor_tensor(
                out=x.t[:, fc, :], in0=x.t[:, fc, :], scalar=fing.t[:, fc:fc + 1], in1=rs.t[:],
                op0=ALU.mult, op1=ALU.mult), reads=[x.b, rs.b, fing.b], writes=[x.b])
        ybuf = Buf("y_d")
        self.obufs.append(ybuf)
        for s in range(4):
            yt = ytok.next()
            for q in range(4):
                p = self.ps.next()
                for j in range(4):
                    fc = q * 4 + j
                    S.op("tensor", lambda E, p=p, fc=fc, j=j, s=s: E.transpose(
                        out=p.t[:, j * 128:(j + 1) * 128], in_=x.t[:, fc, s * 128:(s + 1) * 128],
                        identity=ident.t[:]), reads=[x.b, ident.b], writes=[p.b])
                self.evac(q, yt.t[:, q * 512:(q + 1) * 512], p.t[:, :], [p.b], [yt.b])
            r0 = g * G + s * 128
            S.dma("sync", y_d.ap()[r0:r0 + 128, :], yt.t[:], reads=[yt.b], writes=[ybuf])

    def mixer_phase(self, l, pT_d, oT_d):
        nc, S = self.nc, self.S
        if not self.mixers:
            with contextlib.ExitStack() as em:
                cp = Ring(em, nc, "cp", 2, [128, 16, G], F32)
                ob = Buf("oT_d")
                for g in range(NG):
                    t = cp.next()
                    S.dma("sync", t.t[:], dap(pT_d, g * G, [[T, 128], [128 * T, 16], [1, G]]), writes=[t.b])
                    S.dma("sync", dap(oT_d, g * G, [[T, 128], [128 * T, 16], [1, G]]), t.t[:], reads=[t.b], writes=[ob])
            return
        if l == 0:
            with contextlib.ExitStack() as em:
                z = Tile(em.enter_context(nc.sbuf_tensor("s_zero_fill", [128, 16, G], F32)))
                ob = Buf("oT_d")
                S.op("vector", lambda E: E.memset(z.t[:], 0.0), writes=[z.b])
                for g in range(NG):
                    S.dma("sync", dap(oT_d, g * G, [[T, 128], [128 * T, 16], [1, G]]), z.t[:], reads=[z.b], writes=[ob])
                S.barrier()
        self.attn_phase(l, pT_d, oT_d)
        S.barrier()
        self.gla_phase(l, pT_d, oT_d)
        S.barrier()
        if ENABLE_SSD:
            self.ssd_phase(l, pT_d, oT_d)
            S.barrier()
        if ENABLE_HY:
            self.hyena_phase(l, pT_d, oT_d)
            S.barrier()
        if self.debug and l == 0:
            dbg = self.dout("dbg_oT", [D, T])
            dbp = self.dout("dbg_pT", [IN_W, T])
            db = Buf("dbg")
            for r in range(16):
                S.dma("sync", dbg.ap()[r * 128:(r + 1) * 128, :], oT_d.ap()[r * 128:(r + 1) * 128, :], writes=[db])
            for r in range(0, IN_W, 128):
                e = min(IN_W, r + 128)
                S.dma("sync", dbp.ap()[r:e, :], pT_d.ap()[r:e, :], writes=[db])
            S.barrier()

    def attn_phase(self, l, pT_d, oT_d):
        nc, S = self.nc, self.S
        QR, KR, VR = 1568, 2080, 2592
        lam_init = 0.8 - 0.6 * math.exp(-0.3 * l)
        with contextlib.ExitStack() as ea:
            NKT = (LS + 512) // 128
            KT = Tile(ea.enter_context(nc.sbuf_tensor("s_KT%d" % l, [128, 4, LS + 512], BF16)))
            VA = Tile(ea.enter_context(nc.sbuf_tensor("s_VA%d" % l, [128, NKT, 8, 65], BF16)))
            cosT = Tile(ea.enter_context(nc.sbuf_tensor("s_cosT%d" % l, [128, LS], F32)))
            sinT = Tile(ea.enter_context(nc.sbuf_tensor("s_sinT%d" % l, [128, LS], F32)))
            Rm = Tile(ea.enter_context(nc.sbuf_tensor("s_Rm%d" % l, [128, 128], F32)))
            lamb = Tile(ea.enter_context(nc.sbuf_tensor("s_lamb%d" % l, [128, 4, 32], F32)))
            lamp = Tile(ea.enter_context(nc.sbuf_tensor("s_lamp%d" % l, [128, 2, 32], F32)))
            lam2 = Tile(ea.enter_context(nc.sbuf_tensor("s_lam2%d" % l, [128, 2], F32)))
            lam = Tile(ea.enter_context(nc.sbuf_tensor("s_lam%d" % l, [128, 1], F32)))
            gnb = Tile(ea.enter_context(nc.sbuf_tensor("s_gnb%d" % l, [128, 64], F32)))
            QT = Ring(ea, nc, "QT", 2, [128, 4, G], BF16)
            ld = Ring(ea, nc, "ld", 2, [128, G], F32)
            t1 = Ring(ea, nc, "t1", 1, [128, G], F32)
            t2 = Ring(ea, nc, "t2", 1, [128, G], F32)
            PT = Ring(ea, nc, "PT", 3, [128, G], BF16)
            osb = Ring(ea, nc, "osb", 1, [128, 4, 8, 64], F32)
            osq = Ring(ea, nc, "osq", 1, [128, 4, 8, 64], F32)
            rr = Ring(ea, nc, "rr", 4, [128, 2, 4], F32)
            tt = Ring(ea, nc, "tt", 4, [128, 4, 64], F32)
            ssum = Ring(ea, nc, "ssum", 2, [128, 32], F32)
            ckl = Ring(ea, nc, "ckl", 2, [128, 512], F32)
            stg = Ring(ea, nc, "astg", 2, [128, G], F32)
            ident = self.ident
            ob = Buf("oT_d")
            accr = RingView(self.ps.tiles[0:4])
            QM = Ring(ea, nc, "QM", 1, [128, 4, 4, G], BF16)
            rmask = Tile(ea.enter_context(nc.sbuf_tensor("s_rmask%d" % l, [128, 4], F32)))
            S.dma("sync", rmask.t[:], self.rowmask.ap(), writes=[rmask.b])
            scr = RingView(self.ps.tiles[4:6] + self.psr.tiles)

            S.dma("sync", cosT.t[:], self.ropec.ap(), writes=[cosT.b])
            S.dma("sync", sinT.t[:], self.ropes.ap(), writes=[sinT.b])
            S.dma("sync", Rm.t[:], self.roper.ap(), writes=[Rm.b])
            S.dma("sync", lamb.t[:], dap(self.diff_lambda, l * 128, [[0, 128], [32, 4], [1, 32]]), writes=[lamb.b])
            S.dma("sync", gnb.t[:], dap(self.diff_norm_g, l * 64, [[0, 128], [1, 64]]), writes=[gnb.b])
            S.op("vector", lambda E: E.tensor_tensor(
                out=lamp.t[:], in0=lamb.t[:].rearrange("p (a b) d -> p a b d", b=2)[:, :, 0, :],
                in1=lamb.t[:].rearrange("p (a b) d -> p a b d", b=2)[:, :, 1, :], op=ALU.mult),
                reads=[lamb.b], writes=[lamp.b])
            S.op("vector", lambda E: E.tensor_reduce(out=lam2.t[:], in_=lamp.t[:], axis=mybir.AxisListType.X, op=ALU.add),
                 reads=[lamp.b], writes=[lam2.b])
            S.op("scalar", lambda E: E.activation(out=lam2.t[:], in_=lam2.t[:], func=AF.Exp), reads=[lam2.b], writes=[lam2.b])
            S.op("vector", lambda E: E.tensor_tensor(out=lam.t[:], in0=lam2.t[:, 0:1], in1=lam2.t[:, 1:2], op=ALU.subtract),
                 reads=[lam2.b], writes=[lam.b])
            S.op("vector", lambda E: E.tensor_scalar(out=lam.t[:], in0=lam.t[:], scalar1=lam_init, scalar2=None, op0=ALU.add),
                 reads=[lam.b], writes=[lam.b])
            S.op("vector", lambda E: E.tensor_scalar(out=gnb.t[:], in0=gnb.t[:], scalar1=1.0 - lam_init, scalar2=None,
                                                     op0=ALU.mult), reads=[gnb.b], writes=[gnb.b])
            S.op("gpsimd", lambda E: E.memset(VA.t[:, :, :, 64:65], 1.0), writes=[VA.b])

            def load_rot(row0, tok0, pos0, W, rope, dst_ap, dst_b):
                x = ld.next()
                S.dma("sync", x.t[:, 0:W], pT_d.ap()[row0:row0 + 128, tok0:tok0 + W], writes=[x.b])
                if not rope:
                    S.op("gpsimd", lambda E: E.tensor_copy(out=dst_ap, in_=x.t[:, 0:W]), reads=[x.b], writes=[dst_b])
                    return
                p = scr.next()
                S.op("tensor", lambda E: E.matmul(p.t[:, 0:W], Rm.t[:], x.t[:, 0:W], start=True, stop=True),
                     reads=[x.b, Rm.b], writes=[p.b])
                a = t1.next()
                b = t2.next()
                S.op("gpsimd", lambda E: E.tensor_tensor(out=a.t[:, 0:W], in0=x.t[:, 0:W], in1=cosT.t[:, pos0:pos0 + W],
                                                         op=ALU.mult), reads=[x.b, cosT.b], writes=[a.b])
                S.op("vector", lambda E: E.tensor_tensor(out=b.t[:, 0:W], in0=p.t[:, 0:W], in1=sinT.t[:, pos0:pos0 + W],
                                                         op=ALU.mult), reads=[p.b, sinT.b], writes=[b.b])
                S.op("vector", lambda E: E.tensor_tensor(out=dst_ap, in0=a.t[:, 0:W], in1=b.t[:, 0:W], op=ALU.add),
                     reads=[a.b, b.b], writes=[dst_b])

            seqs = [(0, LS, True)] + [(LS + i * LP, LP, False) for i in range(NPR)]
            def do_seq(tok0, Lq, samp):
                W = min(G, Lq)
                nkt = Lq // 128 + (4 if samp else 0)
                for kg in range(Lq // W):
                    for ch in range(4):
                        load_rot(KR + ch * 128, tok0 + kg * W, kg * W, W, samp,
                                 KT.t[:, ch, kg * W:(kg + 1) * W], KT.b)
                    for ch in range(4):
                        x = ld.next()
                        S.dma("sync", x.t[:, 0:W], pT_d.ap()[VR + ch * 128:VR + (ch + 1) * 128,
                                                            tok0 + kg * W:tok0 + (kg + 1) * W], writes=[x.b])
                        p = scr.next()
                        for s4 in range(W // 128):
                            S.op("tensor", lambda E, p=p, x=x, s4=s4: E.transpose(
                                out=p.t[:, s4 * 128:(s4 + 1) * 128], in_=x.t[:, s4 * 128:(s4 + 1) * 128],
                                identity=ident.t[:]), reads=[x.b, ident.b], writes=[p.b])
                        for s4 in range(W // 128):
                            kt = kg * (W // 128) + s4
                            self.evac(s4, VA.t[:, kt, 2 * ch:2 * ch + 2, 0:64],
                                      p.t[:, s4 * 128:(s4 + 1) * 128].rearrange("p (a b) -> p a b", a=2),
                                      [p.b], [VA.b])
                if samp:
                    for k4 in range(4):
                        ck = ckl.next()
                        S.dma("sync", ck.t[:], dap(self.cache_k, (l * 512 + k4 * 128) * 512, [[512, 128], [1, 512]]),
                              writes=[ck.b])
                        p = scr.next()
                        for ch in range(4):
                            S.op("tensor", lambda E, p=p, ck=ck, ch=ch: E.transpose(
                                out=p.t[:, ch * 128:(ch + 1) * 128], in_=ck.t[:, ch * 128:(ch + 1) * 128],
                                identity=ident.t[:]), reads=[ck.b, ident.b], writes=[p.b])
                        self.evac(k4, KT.t[:, :, LS + k4 * 128:LS + (k4 + 1) * 128],
                                  p.t[:, :].rearrange("p (a b) -> p a b", a=4), [p.b], [KT.b])
                        S.dma("gpsimd", VA.t[:, LS // 128 + k4, :, 0:64],
                              dap(self.cache_v, (l * 512 + k4 * 128) * 512, [[512, 128], [64, 8], [1, 64]]),
                              writes=[VA.b])
                def do_qg(qg):
                    q = QT.next()
                    for ch in range(4):
                        load_rot(QR + ch * 128, tok0 + qg * W, qg * W, W, samp, q.t[:, ch, 0:W], q.b)
                    qm = QM.next()
                    for ch in range(4):
                        for grp in range(4):
                            S.op("gpsimd" if (ch + grp) % 2 else "vector", lambda E, qm=qm, q=q, ch=ch, grp=grp: E.tensor_scalar(
                                out=qm.t[:, ch, grp, 0:W], in0=q.t[:, ch, 0:W], scalar1=rmask.t[:, grp:grp + 1], scalar2=None,
                                op0=ALU.mult), reads=[q.b, rmask.b], writes=[qm.b])
                    o = osb.next()
                    ns = W // 128
                    def do_head(h):
                        acc = [accr.next(), accr.next()]
                        ch = h // 2
                        for c in range(2):
                            r0 = ((h % 2) * 2 + c) * 32
                            for kt in range(nkt):
                                kcol = kt * 128 if kt < Lq // 128 else LS + (kt - Lq // 128) * 128
                                pS = scr.next()
                                S.op("tensor", lambda E, pS=pS, qm=qm, r0=r0, ch=ch, kcol=kcol: E.matmul(
                                    pS.t[:, 0:W], KT.t[:, ch, kcol:kcol + 128], qm.t[:, ch, r0 // 32, 0:W],
                                    start=True, stop=True),
                                    reads=[KT.b, qm.b], writes=[pS.b])
                                pt = PT.next()
                                S.op("scalar", lambda E, pS=pS, pt=pt: E.activation(
                                    out=pt.t[:, 0:W], in_=pS.t[:, 0:W], func=AF.Exp, scale=32.0 ** -0.5),
                                    reads=[pS.b], writes=[pt.b])
                                vkt = kt if kt < Lq // 128 else LS // 128 + (kt - Lq // 128)
                                for s4 in range(ns):
                                    S.op("tensor", lambda E, a=acc[c], pt=pt, s4=s4, vkt=vkt, h=h, kt=kt: E.matmul(
                                        a.t[:, s4 * 65:(s4 + 1) * 65], pt.t[:, s4 * 128:(s4 + 1) * 128], VA.t[:, vkt, h, :],
                                        start=(kt == 0 and s4 == 0), stop=(kt == nkt - 1)),
                                        reads=[pt.b, VA.b], writes=[acc[c].b])
                        r = rr.next()
                        for c in range(2):
                            S.op("vector", lambda E, r=r, c=c, a=acc[c]: E.reciprocal(
                                out=r.t[:, c, 0:ns], in_=a.t[:, 0:ns * 65].rearrange("p (s e) -> p s e", e=65)[:, :, 64]),
                                reads=[acc[c].b], writes=[r.b])
                        S.op("vector", lambda E, r=r: E.tensor_scalar(out=r.t[:, 1, 0:ns], in0=r.t[:, 1, 0:ns],
                                                                      scalar1=lam.t[:, 0:1], scalar2=None, op0=ALU.mult),
                             reads=[r.b, lam.b], writes=[r.b])
                        ta = tt.next()
                        tb = tt.next()
                        a0v = acc[0].t[:, 0:ns * 65].rearrange("p (s e) -> p s e", e=65)[:, :, 0:64]
                        a1v = acc[1].t[:, 0:ns * 65].rearrange("p (s e) -> p s e", e=65)[:, :, 0:64]
                        S.op("vector", lambda E, ta=ta, r=r, a0v=a0v: E.tensor_tensor(
                            out=ta.t[:, 0:ns, :], in0=a0v, in1=r.t[:, 0, 0:ns].unsqueeze(2).broadcast_to([128, ns, 64]),
                            op=ALU.mult), reads=[acc[0].b, r.b], writes=[ta.b])
                        S.op("vector", lambda E, tb=tb, r=r, a1v=a1v: E.tensor_tensor(
                            out=tb.t[:, 0:ns, :], in0=a1v, in1=r.t[:, 1, 0:ns].unsqueeze(2).broadcast_to([128, ns, 64]),
                            op=ALU.mult), reads=[acc[1].b, r.b], writes=[tb.b])
                        S.op("gpsimd", lambda E, ta=ta, tb=tb, o=o, h=h: E.tensor_tensor(
                            out=o.t[:, 0:ns, h, :], in0=ta.t[:, 0:ns, :], in1=tb.t[:, 0:ns, :], op=ALU.subtract),
                            reads=[ta.b, tb.b], writes=[o.b])
                    for h in range(8):
                        do_head(h)
                    dbgon = self.debug and l == 0 and tok0 == LS and qg == 0
                    if dbgon:
                        d1 = self.dout("dbg_o_pre", [128, 4 * 8 * 64])
                        d2 = self.dout("dbg_sm", [128, 32])
                        d3 = self.dout("dbg_o_post", [128, 4 * 8 * 64])
                        d4 = self.dout("dbg_sm2", [128, 32])
                        S.dma("sync", d1.ap(), o.t[:].rearrange("p a b c -> p (a b c)"), reads=[o.b], writes=[Buf()])
                    oq = osq.next()
                    S.op("gpsimd", lambda E, oq=oq, o=o: E.tensor_tensor(out=oq.t[:, 0:ns], in0=o.t[:, 0:ns], in1=o.t[:, 0:ns],
                                                                         op=ALU.mult), reads=[o.b], writes=[oq.b])
                    sm = ssum.next()
                    S.op("vector", lambda E, sm=sm, oq=oq: E.tensor_reduce(
                        out=sm.t[:, 0:ns * 8], in_=oq.t[:, 0:ns].rearrange("p s h d -> p (s h) d"),
                        axis=mybir.AxisListType.X, op=ALU.add), reads=[oq.b], writes=[sm.b])
                    if dbgon:
                        S.dma("sync", d2.ap(), sm.t[:], reads=[sm.b], writes=[Buf()])
                    S.op("scalar", lambda E, sm=sm: E.activation(out=sm.t[:, 0:ns * 8], in_=sm.t[:, 0:ns * 8], func=AF.Sqrt,
                                                                 scale=1.0 / 64, bias=EPS), reads=[sm.b], writes=[sm.b])
                    S.op("vector", lambda E, sm=sm: E.reciprocal(out=sm.t[:, 0:ns * 8], in_=sm.t[:, 0:ns * 8]),
                         reads=[sm.b], writes=[sm.b])
                    S.op("vector", lambda E, sm=sm, o=o: E.tensor_tensor(
                        out=o.t[:, 0:ns].rearrange("p s h d -> p (s h) d"),
                        in0=o.t[:, 0:ns].rearrange("p s h d -> p (s h) d"),
                        in1=sm.t[:, 0:ns * 8].unsqueeze(2).broadcast_to([128, ns * 8, 64]), op=ALU.mult),
                        reads=[o.b, sm.b], writes=[o.b])
                    S.op("gpsimd", lambda E, o=o: E.tensor_tensor(
                        out=o.t[:, 0:ns].rearrange("p s h d -> p (s h) d"),
                        in0=o.t[:, 0:ns].rearrange("p s h d -> p (s h) d"),
                        in1=gnb.t[:].unsqueeze(1).broadcast_to([128, ns * 8, 64]), op=ALU.mult),
                        reads=[o.b, gnb.b], writes=[o.b])
                    if dbgon:
                        S.dma("sync", d4.ap(), sm.t[:], reads=[sm.b], writes=[Buf()])
                        S.dma("sync", d3.ap(), o.t[:].rearrange("p a b c -> p (a b c)"), reads=[o.b], writes=[Buf()])
                    for hp in range(4):
                        p = scr.next()
                        for s4 in range(ns):
                            S.op("tensor", lambda E, p=p, o=o, s4=s4, hp=hp: E.transpose(
                                out=p.t[:, s4 * 128:(s4 + 1) * 128],
                                in_=o.t[:, s4, 2 * hp:2 * hp + 2, :].rearrange("p h d -> p (h d)"),
                                identity=ident.t[:]), reads=[o.b, ident.b], writes=[p.b])
                        st = stg.next()
                        self.evac(hp, st.t[:, 0:W], p.t[:, 0:W], [p.b], [st.b])
                        r0 = 512 + hp * 128
                        S.dma("sync", oT_d.ap()[r0:r0 + 128, tok0 + qg * W:tok0 + (qg + 1) * W], st.t[:, 0:W],
                              reads=[st.b], writes=[ob])
                for qg in range(Lq // W):
                    do_qg(qg)
            for (tok0, Lq, samp) in seqs:
                do_seq(tok0, Lq, samp)

    def gla_phase(self, l, pT_d, oT_d):
        nc, S = self.nc, self.S
        ident = self.ident
        with contextlib.ExitStack() as ea:
            def sb(name, shape, dt=F32):
                return Tile(ea.enter_context(nc.sbuf_tensor("s_g%s%d" % (name, l), shape, dt)))
            TRI = sb("tri", [128, 6, 128])
            GW = sb("gw", [128, 512])
            gng = sb("gng", [128, 64])
            bmask = sb("bmask", [128, 256])
            rmask = sb("rmask", [128, 4])
            OF = sb("OF", [128, LS // 128, 512])
            S32 = [[sb("S32_%d%d" % (d, hc), [128, 64]) for hc in range(2)] for d in range(2)]
            Sbf = [[sb("Sbf_%d%d" % (d, hc), [128, 64], BF16) for hc in range(2)] for d in range(2)]
            grT = Ring(ea, nc, "grT", 2, [128, 128], F32)
            qk = Ring(ea, nc, "gqk", 2, [128, 4, 128], F32)
            vT = Ring(ea, nc, "gvT", 2, [128, 4, 128], F32)
            ktok = Ring(ea, nc, "gktok", 2, [128, 256], F32)
            vtok = Ring(ea, nc, "gvtok", 2, [128, 512], BF16)
            ex = Ring(ea, nc, "gex", 2, [128, 256], F32)
            la = Ring(ea, nc, "gla", 2, [128, 256], F32)
            E1 = Ring(ea, nc, "gE1", 4, [128, 128], F32)
            E2 = Ring(ea, nc, "gE2", 2, [128, 128], F32)
            qd = Ring(ea, nc, "gqd", 2, [128, 128], F32)
            qm = Ring(ea, nc, "gqm", 4, [128, 4, 128], BF16)
            kd = Ring(ea, nc, "gkd", 4, [128, 128], BF16)
            ek = Ring(ea, nc, "gek", 2, [128, 256], F32)
            khat = Ring(ea, nc, "gkhat", 2, [128, 256], BF16)
            at = Ring(ea, nc, "gat", 3, [128, 128], BF16)
            km = Ring(ea, nc, "gkm", 2, [128, 256], F32)
            kvd = Ring(ea, nc, "gkvd", 2, [128, 64], F32)
            osum = Ring(ea, nc, "gosum", 2, [128, 512], F32)
            osq = Ring(ea, nc, "gosq", 1, [128, 512], F32)
            ss = Ring(ea, nc, "gss", 2, [128, 8], F32)
            ggT = Ring(ea, nc, "gggT", 2, [128, 4, 128], F32)
            res = Ring(ea, nc, "gres", 2, [128, 4, 128], F32)
            pA = RingView(self.ps.tiles[0:3])
            pO = RingView(self.ps.tiles[3:5])
            pX = RingView([self.ps.tiles[5]] + self.psr.tiles)
            ob = Buf("oT_d")

            S.dma("sync", TRI.t[:], dap(self.gla_tri, 0, [[128, 128], [128 * 128, 6], [1, 128]]), writes=[TRI.b])
            S.dma("sync", bmask.t[:], self.gla_bmask.ap(), writes=[bmask.b])
            S.dma("sync", rmask.t[:], self.rowmask.ap(), writes=[rmask.b])
            S.dma("sync", gng.t[:], dap(self.gla_norm_g, l * 64, [[0, 128], [1, 64]]), writes=[gng.b])
            S.op("vector", lambda E: E.memset(GW.t[:], 0.0), writes=[GW.b])
            for d in range(2):
                S.dma("sync", GW.t[d * 16:(d + 1) * 16, d * 256:(d + 1) * 256],
                      dap(self.gla_gate_w, (l * 2 + d) * 16 * 256, [[256, 16], [1, 256]]), writes=[GW.b])
            S.dma("sync", GW.t[32:33, :], dap(self.gla_gate_b, l * 512, [[512, 1], [1, 512]]), writes=[GW.b])
            for g_ in grT.tiles:
                S.op("vector", lambda E, g_=g_: E.memset(g_.t[32:33, :], 1.0), writes=[g_.b])

            def do_tile(tok0, tt, d, samp):
                t0 = tok0 + tt * 128
                x = qk.next()
                S.dma("sync", x.t[:], dap(pT_d, t0, [[T, 128], [128 * T, 4], [1, 128]]), writes=[x.b])
                v = vT.next()
                S.dma("sync", v.t[:], dap(pT_d, 512 * T + t0, [[T, 128], [128 * T, 4], [1, 128]]), writes=[v.b])
                g = grT.next()
                S.dma("sync", g.t[0:32, :], dap(pT_d, 1536 * T + t0, [[T, 32], [1, 128]]), writes=[g.b])
                p = pX.next()
                for c in range(2):
                    S.op("tensor", lambda E, c=c: E.transpose(out=p.t[:, c * 128:(c + 1) * 128], in_=x.t[:, 2 + c, :],
                                                               identity=ident.t[:]), reads=[x.b, ident.b], writes=[p.b])
                kt_ = ktok.next()
                S.op("vector", lambda E: E.tensor_copy(out=kt_.t[:], in_=p.t[:, 0:256]), reads=[p.b], writes=[kt_.b])
                p2 = pX.next()
                for c in range(4):
                    S.op("tensor", lambda E, c=c: E.transpose(out=p2.t[:, c * 128:(c + 1) * 128], in_=v.t[:, c, :],
                                                               identity=ident.t[:]), reads=[v.b, ident.b], writes=[p2.b])
                vt_ = vtok.next()
                S.op("scalar", lambda E: E.activation(out=vt_.t[:], in_=p2.t[:, :], func=AF.Copy), reads=[p2.b], writes=[vt_.b])
                pl = pX.next()
                S.op("tensor", lambda E: E.matmul(pl.t[:, 0:256], g.t[0:33, :], GW.t[0:33, d * 256:(d + 1) * 256],
                                                  start=True, stop=True), reads=[g.b, GW.b], writes=[pl.b])
                e_ = ex.next()
                S.op("scalar", lambda E: E.activation(out=e_.t[:], in_=pl.t[:, 0:256], func=AF.Exp, scale=-1.0),
                     reads=[pl.b], writes=[e_.b])
                la_ = la.next()
                S.op("scalar", lambda E: E.activation(out=la_.t[:], in_=e_.t[:], func=AF.Ln, bias=1.0),
                     reads=[e_.b], writes=[la_.b])
                qms, kds, e1s = [], [], []
                for hc in range(2):
                    pb = pX.next()
                    S.op("tensor", lambda E, hc=hc, pb=pb: E.matmul(pb.t[:, 0:128], la_.t[:, hc * 128:(hc + 1) * 128],
                                                                    TRI.t[:, d, :], start=True, stop=True),
                         reads=[la_.b, TRI.b], writes=[pb.b])
                    e1 = E1.next()
                    e2 = E2.next()
                    S.op("scalar", lambda E, e1=e1, pb=pb: E.activation(out=e1.t[:], in_=pb.t[:, 0:128], func=AF.Exp),
                         reads=[pb.b], writes=[e1.b])
                    S.op("scalar", lambda E, e2=e2, pb=pb: E.activation(out=e2.t[:], in_=pb.t[:, 0:128], func=AF.Exp, scale=-1.0),
                         reads=[pb.b], writes=[e2.b])
                    q_ = qd.next()
                    S.op("vector", lambda E, q_=q_, e1=e1, hc=hc: E.scalar_tensor_tensor(
                        out=q_.t[:], in0=x.t[:, hc, :], scalar=32.0 ** -0.5, in1=e1.t[:], op0=ALU.mult, op1=ALU.mult),
                        reads=[x.b, e1.b], writes=[q_.b])
                    m_ = qm.next()
                    for grp in range(4):
                        S.op("gpsimd" if grp % 2 else "vector", lambda E, m_=m_, q_=q_, grp=grp: E.tensor_scalar(
                            out=m_.t[:, grp, :], in0=q_.t[:], scalar1=rmask.t[:, grp:grp + 1], scalar2=None, op0=ALU.mult),
                            reads=[q_.b, rmask.b], writes=[m_.b])
                    k_ = kd.next()
                    S.op("gpsimd", lambda E, k_=k_, e2=e2, hc=hc: E.tensor_tensor(out=k_.t[:], in0=x.t[:, 2 + hc, :], in1=e2.t[:],
                                                                                  op=ALU.mult), reads=[x.b, e2.b], writes=[k_.b])
                    qms.append(m_)
                    kds.append(k_)
                    e1s.append(e1)
                pk = pX.next()
                S.op("tensor", lambda E: E.matmul(pk.t[:, 0:256], TRI.t[:, 2 + d, :], la_.t[:], start=True, stop=True),
                     reads=[la_.b, TRI.b], writes=[pk.b])
                ek_ = ek.next()
                S.op("scalar", lambda E: E.activation(out=ek_.t[:], in_=pk.t[:, 0:256], func=AF.Exp), reads=[pk.b], writes=[ek_.b])
                kh = khat.next()
                S.op("vector", lambda E: E.tensor_tensor(out=kh.t[:], in0=kt_.t[:], in1=ek_.t[:], op=ALU.mult),
                     reads=[kt_.b, ek_.b], writes=[kh.b])
                po = pO.next()
                for h in range(8):
                    hc, grp = h // 4, h % 4
                    pa = pA.next()
                    S.op("tensor", lambda E, pa=pa, hc=hc, grp=grp: E.matmul(
                        pa.t[:, 0:128], kds[hc].t[:], qms[hc].t[:, grp, :], start=True, stop=True),
                        reads=[kds[hc].b, qms[hc].b], writes=[pa.b])
                    a_ = at.next()
                    S.op("vector", lambda E, pa=pa, a_=a_: E.tensor_tensor(out=a_.t[:], in0=pa.t[:, 0:128], in1=TRI.t[:, 4 + d, :],
                                                                           op=ALU.mult), reads=[pa.b, TRI.b], writes=[a_.b])
                    S.op("tensor", lambda E, a_=a_, h=h: E.matmul(
                        po.t[:, h * 64:(h + 1) * 64], a_.t[:], vt_.t[:, h * 64:(h + 1) * 64], start=(h == 0), stop=False),
                        reads=[a_.b, vt_.b], writes=[po.b])
                    S.op("tensor", lambda E, h=h, hc=hc, grp=grp: E.matmul(
                        po.t[:, h * 64:(h + 1) * 64], qms[hc].t[:, grp, :], Sbf[d][hc].t[:], start=False, stop=True),
                        reads=[qms[hc].b, Sbf[d][hc].b], writes=[po.b])
                if d == 0:
                    S.op("scalar", lambda E: E.activation(out=OF.t[:, tt, :], in_=po.t[:, :], func=AF.Copy),
                         reads=[po.b], writes=[OF.b])
                for hc in range(2):
                    ps_ = pX.next()
                    S.op("tensor", lambda E, ps_=ps_, hc=hc: E.matmul(
                        ps_.t[:, 0:256], kh.t[:, hc * 128:(hc + 1) * 128], vt_.t[:, hc * 256:(hc + 1) * 256],
                        start=True, stop=True), reads=[kh.b, vt_.b], writes=[ps_.b])
                    km_ = km.next()
                    S.op("vector", lambda E, ps_=ps_, km_=km_: E.tensor_tensor(out=km_.t[:], in0=ps_.t[:, 0:256], in1=bmask.t[:],
                                                                               op=ALU.mult), reads=[ps_.b, bmask.b], writes=[km_.b])
                    kv_ = kvd.next()
                    S.op("vector", lambda E, km_=km_, kv_=kv_: E.tensor_reduce(
                        out=kv_.t[:], in_=km_.t[:].rearrange("p (h d) -> p d h", h=4), axis=mybir.AxisListType.X, op=ALU.add),
                        reads=[km_.b], writes=[kv_.b])
                    col = 127 if d == 0 else 0
                    S.op("vector", lambda E, kv_=kv_, hc=hc, col=col: E.scalar_tensor_tensor(
                        out=S32[d][hc].t[:], in0=S32[d][hc].t[:], scalar=e1s[hc].t[:, col:col + 1], in1=kv_.t[:],
                        op0=ALU.mult, op1=ALU.add), reads=[S32[d][hc].b, e1s[hc].b, kv_.b], writes=[S32[d][hc].b])
                    S.op("gpsimd", lambda E, hc=hc: E.tensor_copy(out=Sbf[d][hc].t[:], in_=S32[d][hc].t[:]),
                         reads=[S32[d][hc].b], writes=[Sbf[d][hc].b])
                if d == 1:
                    o_ = osum.next()
                    S.op("vector", lambda E: E.tensor_tensor(out=o_.t[:], in0=po.t[:, :], in1=OF.t[:, tt, :], op=ALU.add),
                         reads=[po.b, OF.b], writes=[o_.b])
                    q2 = osq.next()
                    S.op("gpsimd", lambda E: E.tensor_tensor(out=q2.t[:], in0=o_.t[:], in1=o_.t[:], op=ALU.mult),
                         reads=[o_.b], writes=[q2.b])
                    s_ = ss.next()
                    S.op("vector", lambda E: E.tensor_reduce(out=s_.t[:], in_=q2.t[:].rearrange("p (h d) -> p h d", h=8),
                                                             axis=mybir.AxisListType.X, op=ALU.add), reads=[q2.b], writes=[s_.b])
                    S.op("scalar", lambda E: E.activation(out=s_.t[:], in_=s_.t[:], func=AF.Sqrt, scale=1.0 / 64, bias=EPS),
                         reads=[s_.b], writes=[s_.b])
                    S.op("vector", lambda E: E.reciprocal(out=s_.t[:], in_=s_.t[:]), reads=[s_.b], writes=[s_.b])
                    S.op("vector", lambda E: E.tensor_tensor(
                        out=o_.t[:].rearrange("p (h d) -> p h d", h=8), in0=o_.t[:].rearrange("p (h d) -> p h d", h=8),
                        in1=s_.t[:].unsqueeze(2).broadcast_to([128, 8, 64]), op=ALU.mult), reads=[o_.b, s_.b], writes=[o_.b])
                    S.op("gpsimd", lambda E: E.tensor_tensor(
                        out=o_.t[:].rearrange("p (h d) -> p h d", h=8), in0=o_.t[:].rearrange("p (h d) -> p h d", h=8),
                        in1=gng.t[:].unsqueeze(1).broadcast_to([128, 8, 64]), op=ALU.mult), reads=[o_.b, gng.b], writes=[o_.b])
                    gg_ = ggT.next()
                    S.dma("sync", gg_.t[:], dap(pT_d, 1024 * T + t0, [[T, 128], [128 * T, 4], [1, 128]]), writes=[gg_.b])
                    S.op("scalar", lambda E: E.activation(out=gg_.t[:], in_=gg_.t[:], func=AF.Silu), reads=[gg_.b], writes=[gg_.b])
                    pt_ = pX.next()
                    for c in range(4):
                        S.op("tensor", lambda E, c=c: E.transpose(out=pt_.t[:, c * 128:(c + 1) * 128],
                                                                   in_=o_.t[:, c * 128:(c + 1) * 128], identity=ident.t[:]),
                             reads=[o_.b, ident.b], writes=[pt_.b])
                    r_ = res.next()
                    S.op("vector", lambda E: E.tensor_tensor(out=r_.t[:], in0=pt_.t[:, :].rearrange("p (c t) -> p c t", c=4),
                                                             in1=gg_.t[:], op=ALU.mult), reads=[pt_.b, gg_.b], writes=[r_.b])
                    S.dma("sync", dap(oT_d, t0, [[T, 128], [128 * T, 4], [1, 128]]), r_.t[:], reads=[r_.b], writes=[ob])

            seqs = [(0, LS, True, -1)] + [(LS + i * LP, LP, False, i) for i in range(NPR)]
            for (tok0, Lq, samp, pi) in seqs:
                for d in range(2):
                    for hc in range(2):
                        if samp:
                            S.dma("sync", S32[d][hc].t[:],
                                  dap(self.state_gla, ((l * 2 + d) * 8 + hc * 4) * 32 * 64, [[64, 128], [1, 64]]),
                                  writes=[S32[d][hc].b])
                        else:
                            S.op("vector", lambda E, d=d, hc=hc: E.memset(S32[d][hc].t[:], 0.0), writes=[S32[d][hc].b])
                        S.op("gpsimd", lambda E, d=d, hc=hc: E.tensor_copy(out=Sbf[d][hc].t[:], in_=S32[d][hc].t[:]),
                             reads=[S32[d][hc].b], writes=[Sbf[d][hc].b])
                nt = Lq // 128
                for d in range(2):
                    for tt in (range(nt) if d == 0 else range(nt - 1, -1, -1)):
                        do_tile(tok0, tt, d, samp)
                if not samp:
                    for d in range(2):
                        for hc in range(2):
                            S.dma("sync", dap(self.ngla_d, (((pi * DEPTH + l) * 2 + d) * 8 + hc * 4) * 32 * 64, [[64, 128], [1, 64]]),
                                  S32[d][hc].t[:], reads=[S32[d][hc].b], writes=[Buf()])

    def ssd_phase(self, l, pT_d, oT_d):
        nc, S = self.nc, self.S
        ident = self.ident
        ZR, XR, DR = 3104, 3616, 4384
        with contextlib.ExitStack() as ea:
            def sb(name, shape, dt=F32):
                return Tile(ea.enter_context(nc.sbuf_tensor("s_s%s%d" % (name, l), shape, dt)))
            TRI = sb("tri", [128, 6, 128])
            ones = sb("ones", [128, 128])
            cw = sb("cw", [128, 6, 3])
            cb = sb("cb", [128, 6])
            gmask = sb("gmask", [128, 2])
            dtb = sb("dtb", [128, 16])
            abc = sb("abc", [128, 16])
            Dbc = sb("Dbc", [128, 8])
            ngb = sb("ngb", [128, 512])
            YF = sb("YF", [128, LS // 128, 512])
            ST32 = [[sb("ST32_%d%d" % (d, h), [128, 64]) for h in range(8)] for d in range(2)]
            STbf = [[sb("STbf_%d%d" % (d, h), [128, 64], BF16) for h in range(8)] for d in range(2)]
            xin = Ring(ea, nc, "sxin", 2, [128, 6, 130], F32)
            cv = Ring(ea, nc, "scv", 2, [128, 6, 128], F32)
            BTb = Ring(ea, nc, "sBT", 2, [128, 128], BF16)
            CTm = Ring(ea, nc, "sCTm", 2, [128, 2, 128], F32)
            xtok = Ring(ea, nc, "sxtok", 2, [128, 512], F32)
            xw = Ring(ea, nc, "sxw", 2, [128, 512], BF16)
            xtb = Ring(ea, nc, "sxtb", 2, [128, 512], BF16)
            Btok = Ring(ea, nc, "sBtok", 2, [128, 128], BF16)
            dtT = Ring(ea, nc, "sdtT", 2, [128, 128], F32)
            dt_ = Ring(ea, nc, "sdt", 2, [128, 16], F32)
            gg = Ring(ea, nc, "sg", 2, [128, 16], F32)
            cumc = Ring(ea, nc, "scum", 2, [128, 8], F32)
            ww = Ring(ea, nc, "sw", 2, [128, 8], F32)
            etot = Ring(ea, nc, "setot", 2, [128, 8], F32)
            gb = Ring(ea, nc, "sgb", 2, [128, 128], F32)
            Dm = Ring(ea, nc, "sDm", 2, [128, 128], F32)
            Eh = Ring(ea, nc, "sEh", 2, [128, 128], F32)
            Cd = Ring(ea, nc, "sCd", 2, [128, 128], BF16)
            at = Ring(ea, nc, "sat", 3, [128, 128], BF16)
            ysum = Ring(ea, nc, "sysum", 2, [128, 512], F32)
            ztok = Ring(ea, nc, "sztok", 2, [128, 512], F32)
            zT = Ring(ea, nc, "szT", 2, [128, 4, 128], F32)
            ysq = Ring(ea, nc, "sysq", 1, [128, 512], F32)
            s1 = Ring(ea, nc, "ss1", 2, [128, 1], F32)
            res = Ring(ea, nc, "sres", 2, [128, 4, 128], F32)
            sto = Ring(ea, nc, "ssto", 4, [64, 64], F32)
            pA = RingView(self.ps.tiles[0:3])
            pO = RingView(self.ps.tiles[3:5])
            pX = RingView([self.ps.tiles[5]] + self.psr.tiles)
            ob = Buf("oT_d")

            S.dma("sync", TRI.t[:], dap(self.ssd_tri, 0, [[128, 128], [128 * 128, 6], [1, 128]]), writes=[TRI.b])
            for t_ in dtT.tiles:
                S.op("vector", lambda E, t_=t_: E.memset(t_.t[:], 0.0), writes=[t_.b])
            S.op("vector", lambda E: E.memset(ones.t[:], 1.0), writes=[ones.b])
            S.dma("sync", gmask.t[:], self.ssd_gmask.ap(), writes=[gmask.b])
            for k in range(3):
                S.dma("sync", cw.t[:, :, k], dap(self.ssd_conv_w, (l * 3 + k) * 768, [[1, 128], [128, 6]]), writes=[cw.b], **NCD)
            S.dma("sync", cb.t[:], dap(self.ssd_conv_b, l * 768, [[1, 128], [128, 6]]), writes=[cb.b], **NCD)
            S.dma("sync", dtb.t[:], dap(self.ssd_dt_bias, l * 16, [[0, 128], [1, 16]]), writes=[dtb.b])
            S.dma("sync", abc.t[:], dap(self.ssd_a_log, l * 16, [[0, 128], [1, 16]]), writes=[abc.b])
            S.dma("sync", Dbc.t[:], dap(self.ssd_d, l * 8, [[0, 128], [1, 8]]), writes=[Dbc.b])
            S.dma("sync", ngb.t[:], dap(self.ssd_norm_g, l * 512, [[0, 128], [1, 512]]), writes=[ngb.b])
            S.op("scalar", lambda E: E.activation(out=abc.t[:], in_=abc.t[:], func=AF.Exp), reads=[abc.b], writes=[abc.b])
            S.op("vector", lambda E: E.tensor_scalar(out=abc.t[:], in0=abc.t[:], scalar1=-1.0, scalar2=None, op0=ALU.mult),
                 reads=[abc.b], writes=[abc.b])

            def do_tile(tok0, Lq, tt, d):
                t0 = tok0 + tt * 128
                nt = Lq // 128
                lo = 0 if tt > 0 else 1
                hi = 130 if tt < nt - 1 else 129
                x = xin.next()
                if lo == 1:
                    S.op("vector", lambda E: E.memset(x.t[:, :, 0:1], 0.0), writes=[x.b])
                if hi == 129:
                    S.op("vector", lambda E: E.memset(x.t[:, :, 129:130], 0.0), writes=[x.b])
                S.dma("sync", x.t[:, :, lo:hi], dap(pT_d, XR * T + t0 - 1 + lo, [[T, 128], [128 * T, 6], [1, hi - lo]]),
                      writes=[x.b])
                c_ = cv.next()
                for ch in range(6):
                    S.op("vector", lambda E, ch=ch: E.tensor_scalar(out=c_.t[:, ch, :], in0=x.t[:, ch, 0:128],
                                                                    scalar1=cw.t[:, ch, 0:1], scalar2=None, op0=ALU.mult),
                         reads=[x.b, cw.b], writes=[c_.b])
                    for k in (1, 2):
                        S.op("vector", lambda E, ch=ch, k=k: E.scalar_tensor_tensor(
                            out=c_.t[:, ch, :], in0=x.t[:, ch, k:k + 128], scalar=cw.t[:, ch, k:k + 1], in1=c_.t[:, ch, :],
                            op0=ALU.mult, op1=ALU.add), reads=[x.b, cw.b, c_.b], writes=[c_.b])
                    S.op("scalar", lambda E, ch=ch: E.activation(out=c_.t[:, ch, :], in_=c_.t[:, ch, :], func=AF.Silu,
                                                                 bias=cb.t[:, ch:ch + 1]), reads=[c_.b, cb.b], writes=[c_.b])
                bt = BTb.next()
                S.op("gpsimd", lambda E: E.tensor_copy(out=bt.t[:], in_=c_.t[:, 4, :]), reads=[c_.b], writes=[bt.b])
                cm = CTm.next()
                for g in range(2):
                    S.op("gpsimd", lambda E, g=g: E.tensor_scalar(out=cm.t[:, g, :], in0=c_.t[:, 5, :], scalar1=gmask.t[:, g:g + 1],
                                                                  scalar2=None, op0=ALU.mult), reads=[c_.b, gmask.b], writes=[cm.b])
                cmb = Cd.next()
                p = pX.next()
                for c in range(4):
                    S.op("tensor", lambda E, c=c: E.transpose(out=p.t[:, c * 128:(c + 1) * 128], in_=c_.t[:, c, :],
                                                               identity=ident.t[:]), reads=[c_.b, ident.b], writes=[p.b])
                xt = xtok.next()
                S.op("scalar", lambda E: E.activation(out=xt.t[:], in_=p.t[:, :], func=AF.Copy), reads=[p.b], writes=[xt.b])
                xb = xtb.next()
                S.op("gpsimd", lambda E: E.tensor_copy(out=xb.t[:], in_=xt.t[:]), reads=[xt.b], writes=[xb.b])
                p2 = pX.next()
                S.op("tensor", lambda E: E.transpose(out=p2.t[:, 0:128], in_=c_.t[:, 4, :], identity=ident.t[:]),
                     reads=[c_.b, ident.b], writes=[p2.b])
                btk = Btok.next()
                S.op("vector", lambda E: E.tensor_copy(out=btk.t[:], in_=p2.t[:, 0:128]), reads=[p2.b], writes=[btk.b])
                dT = dtT.next()
                S.dma("sync", dT.t[0:16, :], dap(pT_d, DR * T + t0, [[T, 16], [1, 128]]), writes=[dT.b])
                p3 = pX.next()
                S.op("tensor", lambda E: E.transpose(out=p3.t[:, 0:128], in_=dT.t[:, :], identity=ident.t[:]),
                     reads=[dT.b, ident.b], writes=[p3.b])
                dt = dt_.next()
                S.op("vector", lambda E: E.tensor_tensor(out=dt.t[:], in0=p3.t[:, 0:16], in1=dtb.t[:], op=ALU.add),
                     reads=[p3.b, dtb.b], writes=[dt.b])
                S.op("scalar", lambda E: E.activation(out=dt.t[:], in_=dt.t[:], func=AF.Exp), reads=[dt.b], writes=[dt.b])
                S.op("scalar", lambda E: E.activation(out=dt.t[:], in_=dt.t[:], func=AF.Ln, bias=1.0), reads=[dt.b], writes=[dt.b])
                g_ = gg.next()
                S.op("vector", lambda E: E.tensor_tensor(out=g_.t[:], in0=dt.t[:], in1=abc.t[:], op=ALU.mult),
                     reads=[dt.b, abc.b], writes=[g_.b])
                pc = pX.next()
                S.op("tensor", lambda E: E.matmul(pc.t[:, 0:8], TRI.t[:, d, :], g_.t[:, d * 8:(d + 1) * 8], start=True, stop=True),
                     reads=[TRI.b, g_.b], writes=[pc.b])
                S.op("tensor", lambda E: E.matmul(pc.t[:, 8:16], TRI.t[:, 2 + d, :], g_.t[:, d * 8:(d + 1) * 8], start=True, stop=True),
                     reads=[TRI.b, g_.b], writes=[pc.b])
                S.op("tensor", lambda E: E.matmul(pc.t[:, 16:24], ones.t[:], g_.t[:, d * 8:(d + 1) * 8], start=True, stop=True),
                     reads=[ones.b, g_.b], writes=[pc.b])
                ncum = cumc.next()
                S.op("vector", lambda E: E.tensor_scalar(out=ncum.t[:], in0=pc.t[:, 0:8], scalar1=-1.0, scalar2=None, op0=ALU.mult),
                     reads=[pc.b], writes=[ncum.b])
                w_ = ww.next()
                S.op("scalar", lambda E: E.activation(out=w_.t[:], in_=pc.t[:, 8:16], func=AF.Exp), reads=[pc.b], writes=[w_.b])
                S.op("vector", lambda E: E.tensor_tensor(out=w_.t[:], in0=w_.t[:], in1=dt.t[:, d * 8:(d + 1) * 8], op=ALU.mult),
                     reads=[w_.b, dt.b], writes=[w_.b])
                et = etot.next()
                S.op("scalar", lambda E: E.activation(out=et.t[:], in_=pc.t[:, 16:24], func=AF.Exp), reads=[pc.b], writes=[et.b])
                xw_ = xw.next()
                S.op("vector", lambda E: E.tensor_tensor(
                    out=xw_.t[:].rearrange("p (h d) -> p h d", h=8), in0=xt.t[:].rearrange("p (h d) -> p h d", h=8),
                    in1=w_.t[:].unsqueeze(2).broadcast_to([128, 8, 64]), op=ALU.mult), reads=[xt.b, w_.b], writes=[xw_.b])
                pG = []
                cmbf = at.next()
                for g in range(2):
                    pg = pA.next()
                    cgb = Cd.next()
                    S.op("gpsimd", lambda E, g=g, cgb=cgb: E.tensor_copy(out=cgb.t[:], in_=cm.t[:, g, :]), reads=[cm.b], writes=[cgb.b])
                    S.op("tensor", lambda E, pg=pg, cgb=cgb: E.matmul(pg.t[:, 0:128], bt.t[:], cgb.t[:], start=True, stop=True),
                         reads=[bt.b, cgb.b], writes=[pg.b])
                    pG.append(pg)
                po = pO.next()
                for h in range(8):
                    g = h // 4
                    gb_ = gb.next()
                    S.op("gpsimd", lambda E, gb_=gb_, h=h: E.tensor_copy(
                        out=gb_.t[:], in_=g_.t[:, d * 8 + h:d * 8 + h + 1].broadcast_to([128, 128])),
                        reads=[g_.b], writes=[gb_.b])
                    pd = pX.next()
                    S.op("tensor", lambda E, pd=pd, gb_=gb_: E.matmul(pd.t[:, 0:128], gb_.t[:], TRI.t[:, d, :], start=True, stop=True),
                         reads=[gb_.b, TRI.b], writes=[pd.b])
                    S.op("tensor", lambda E, pd=pd, gb_=gb_: E.matmul(pd.t[:, 128:256], gb_.t[:], TRI.t[:, d, :], start=True, stop=False),
                         reads=[gb_.b, TRI.b], writes=[pd.b])
                    S.op("tensor", lambda E, pd=pd: E.matmul(pd.t[:, 128:256], ident.t[:], TRI.t[:, 4 + d, :], start=False, stop=True),
                         reads=[ident.b, TRI.b], writes=[pd.b])
                    e_ = Eh.next()
                    S.op("scalar", lambda E, pd=pd, e_=e_: E.activation(out=e_.t[:], in_=pd.t[:, 0:128], func=AF.Exp),
                         reads=[pd.b], writes=[e_.b])
                    dm = Dm.next()
                    S.op("scalar", lambda E, pd=pd, dm=dm, h=h: E.activation(out=dm.t[:], in_=pd.t[:, 128:256], func=AF.Exp,
                                                                             bias=ncum.t[:, h:h + 1]),
                         reads=[pd.b, ncum.b], writes=[dm.b])
                    cd = Cd.next()
                    S.op("gpsimd", lambda E, cd=cd, e_=e_, g=g: E.tensor_tensor(out=cd.t[:], in0=cm.t[:, g, :], in1=e_.t[:], op=ALU.mult),
                         reads=[cm.b, e_.b], writes=[cd.b])
                    a_ = at.next()
                    S.op("vector", lambda E, a_=a_, dm=dm, g=g, h=h: E.scalar_tensor_tensor(
                        out=a_.t[:], in0=pG[g].t[:, 0:128], scalar=dt.t[:, d * 8 + h:d * 8 + h + 1], in1=dm.t[:],
                        op0=ALU.mult, op1=ALU.mult), reads=[pG[g].b, dt.b, dm.b], writes=[a_.b])
                    S.op("tensor", lambda E, a_=a_, h=h: E.matmul(po.t[:, h * 64:(h + 1) * 64], a_.t[:], xb.t[:, h * 64:(h + 1) * 64],
                                                                  start=(h == 0), stop=False), reads=[a_.b, xb.b], writes=[po.b])
                    S.op("tensor", lambda E, cd=cd, h=h: E.matmul(po.t[:, h * 64:(h + 1) * 64], cd.t[:], STbf[d][h].t[:],
                                                                  start=False, stop=True), reads=[cd.b, STbf[d][h].b], writes=[po.b])
                if d == 0:
                    S.op("scalar", lambda E: E.activation(out=YF.t[:, tt, :], in_=po.t[:, :], func=AF.Copy),
                         reads=[po.b], writes=[YF.b])
                for h in range(8):
                    ps_ = pX.next()
                    S.op("tensor", lambda E, ps_=ps_, h=h: E.matmul(ps_.t[:, 0:64], btk.t[:], xw_.t[:, h * 64:(h + 1) * 64],
                                                                    start=True, stop=True), reads=[btk.b, xw_.b], writes=[ps_.b])
                    S.op("vector", lambda E, ps_=ps_, h=h: E.scalar_tensor_tensor(
                        out=ST32[d][h].t[:], in0=ST32[d][h].t[:], scalar=et.t[:, h:h + 1], in1=ps_.t[:, 0:64],
                        op0=ALU.mult, op1=ALU.add), reads=[ST32[d][h].b, et.b, ps_.b], writes=[ST32[d][h].b])
                    S.op("gpsimd", lambda E, h=h: E.tensor_copy(out=STbf[d][h].t[:], in_=ST32[d][h].t[:]),
                         reads=[ST32[d][h].b], writes=[STbf[d][h].b])
                if d == 1:
                    y_ = ysum.next()
                    S.op("vector", lambda E: E.tensor_tensor(out=y_.t[:], in0=po.t[:, :], in1=YF.t[:, tt, :], op=ALU.add),
                         reads=[po.b, YF.b], writes=[y_.b])
                    q2 = ysq.next()
                    S.op("gpsimd", lambda E: E.tensor_tensor(
                        out=q2.t[:].rearrange("p (h d) -> p h d", h=8), in0=xt.t[:].rearrange("p (h d) -> p h d", h=8),
                        in1=Dbc.t[:].unsqueeze(2).broadcast_to([128, 8, 64]), op=ALU.mult), reads=[xt.b, Dbc.b], writes=[q2.b])
                    S.op("vector", lambda E: E.tensor_tensor(out=y_.t[:], in0=y_.t[:], in1=q2.t[:], op=ALU.add),
                         reads=[y_.b, q2.b], writes=[y_.b])
                    z_ = zT.next()
                    S.dma("sync", z_.t[:], dap(pT_d, ZR * T + t0, [[T, 128], [128 * T, 4], [1, 128]]), writes=[z_.b])
                    pz = pX.next()
                    for c in range(4):
                        S.op("tensor", lambda E, c=c: E.transpose(out=pz.t[:, c * 128:(c + 1) * 128], in_=z_.t[:, c, :],
                                                                   identity=ident.t[:]), reads=[z_.b, ident.b], writes=[pz.b])
                    zt = ztok.next()
                    S.op("scalar", lambda E: E.activation(out=zt.t[:], in_=pz.t[:, :], func=AF.Silu), reads=[pz.b], writes=[zt.b])
                    S.op("vector", lambda E: E.tensor_tensor(out=y_.t[:], in0=y_.t[:], in1=zt.t[:], op=ALU.mult),
                         reads=[y_.b, zt.b], writes=[y_.b])
                    S.op("gpsimd", lambda E: E.tensor_tensor(out=q2.t[:], in0=y_.t[:], in1=y_.t[:], op=ALU.mult),
                         reads=[y_.b], writes=[q2.b])
                    r1 = s1.next()
                    S.op("vector", lambda E: E.tensor_reduce(out=r1.t[:], in_=q2.t[:], axis=mybir.AxisListType.X, op=ALU.add),
                         reads=[q2.b], writes=[r1.b])
                    S.op("scalar", lambda E: E.activation(out=r1.t[:], in_=r1.t[:], func=AF.Sqrt, scale=1.0 / 512, bias=EPS),
                         reads=[r1.b], writes=[r1.b])
                    S.op("vector", lambda E: E.reciprocal(out=r1.t[:], in_=r1.t[:]), reads=[r1.b], writes=[r1.b])
                    S.op("vector", lambda E: E.scalar_tensor_tensor(out=y_.t[:], in0=y_.t[:], scalar=r1.t[:, 0:1], in1=ngb.t[:],
                                                                    op0=ALU.mult, op1=ALU.mult),
                         reads=[y_.b, r1.b, ngb.b], writes=[y_.b])
                    pt_ = pX.next()
                    for c in range(4):
                        S.op("tensor", lambda E, c=c: E.transpose(out=pt_.t[:, c * 128:(c + 1) * 128],
                                                                   in_=y_.t[:, c * 128:(c + 1) * 128], identity=ident.t[:]),
                             reads=[y_.b, ident.b], writes=[pt_.b])
                    r_ = res.next()
                    S.op("scalar", lambda E: E.activation(out=r_.t[:], in_=pt_.t[:, :].rearrange("p (c t) -> p c t", c=4), func=AF.Copy),
                         reads=[pt_.b], writes=[r_.b])
                    S.dma("sync", dap(oT_d, 1024 * T + t0, [[T, 128], [128 * T, 4], [1, 128]]), r_.t[:], reads=[r_.b], writes=[ob])

            seqs = [(0, LS, True, -1)] + [(LS + i * LP, LP, False, i) for i in range(NPR)]
            for (tok0, Lq, samp, pi) in seqs:
                for d in range(2):
                    for h in range(8):
                        g = h // 4
                        S.op("vector", lambda E, d=d, h=h: E.memset(ST32[d][h].t[:], 0.0), writes=[ST32[d][h].b])
                        if samp:
                            si = sto.next()
                            S.dma("sync", si.t[:], dap(self.state_ssd, ((l * 2 + d) * 8 + h) * 4096, [[64, 64], [1, 64]]),
                                  writes=[si.b])
                            pp = pX.next()
                            S.op("tensor", lambda E, pp=pp, si=si: E.transpose(out=pp.t[0:64, 0:64], in_=si.t[:],
                                                                                identity=ident.t[0:64, 0:64]),
                                 reads=[si.b, ident.b], writes=[pp.b])
                            so = sto.next()
                            S.op("vector", lambda E, pp=pp, so=so: E.tensor_copy(out=so.t[:], in_=pp.t[0:64, 0:64]),
                                 reads=[pp.b], writes=[so.b])
                            S.dma("sync", ST32[d][h].t[g * 64:(g + 1) * 64, :], so.t[:], reads=[so.b], writes=[ST32[d][h].b])
                        S.op("gpsimd", lambda E, d=d, h=h: E.tensor_copy(out=STbf[d][h].t[:], in_=ST32[d][h].t[:]),
                             reads=[ST32[d][h].b], writes=[STbf[d][h].b])
                nt = Lq // 128
                for d in range(2):
                    for tt in (range(nt) if d == 0 else range(nt - 1, -1, -1)):
                        do_tile(tok0, Lq, tt, d)
                if not samp:
                    for d in range(2):
                        for h in range(8):
                            g = h // 4
                            si = sto.next()
                            S.dma("sync", si.t[:], ST32[d][h].t[g * 64:(g + 1) * 64, :], reads=[ST32[d][h].b], writes=[si.b])
                            pp = pX.next()
                            S.op("tensor", lambda E, pp=pp, si=si: E.transpose(out=pp.t[0:64, 0:64], in_=si.t[:],
                                                                                identity=ident.t[0:64, 0:64]),
                                 reads=[si.b, ident.b], writes=[pp.b])
                            so = sto.next()
                            S.op("vector", lambda E, pp=pp, so=so: E.tensor_copy(out=so.t[:], in_=pp.t[0:64, 0:64]),
                                 reads=[pp.b], writes=[so.b])
                            S.dma("sync", dap(self.nssd_d, (((pi * DEPTH + l) * 2 + d) * 8 + h) * 4096, [[64, 64], [1, 64]]),
                                  so.t[:], reads=[so.b], writes=[Buf()])

    def hyena_phase(self, l, pT_d, oT_d):
        nc, S = self.nc, self.S
        HR = 4400
        with contextlib.ExitStack() as ea:
            def sb(name, shape, dt=F32):
                return Tile(ea.enter_context(nc.sbuf_tensor("s_h%s%d" % (name, l), shape, dt)))
            W1 = sb("W1", [128, 64]); W2 = sb("W2", [128, 64]); W3 = sb("W3", [128, 128])
            cols = sb("cols", [128, 8])
            hcw = sb("hcw", [128, 3, 3]); hcb = sb("hcb", [128, 3])
            hdn = sb("hdn", [128, LS])
            trow = sb("trow", [128, LS])
            filt = [[sb("filt%d%d" % (o, d), [128, LS]) for d in range(2)] for o in range(2)]
            u = sb("u", [128, LS]); accF = sb("accF", [128, LS]); accB = sb("accB", [128, LS])
            tmp = Ring(ea, nc, "htmp", 2, [128, LS + 2], F32)
            pX = RingView(self.ps.tiles[0:6])
            ob = Buf("oT_d")
            S.dma("sync", W1.t[0:33, :], dap(self.hy_f_w1, l * 33 * 64, [[64, 33], [1, 64]]), writes=[W1.b])
            S.dma("sync", W2.t[0:64, :], dap(self.hy_f_w2, l * 64 * 64, [[64, 64], [1, 64]]), writes=[W2.b])
            S.dma("sync", cols.t[0:64, 0:1], dap(self.hy_f_b1, l * 64, [[1, 64], [1, 1]]), writes=[cols.b])
            S.dma("sync", cols.t[0:64, 1:2], dap(self.hy_f_b2, l * 64, [[1, 64], [1, 1]]), writes=[cols.b])
            S.dma("sync", cols.t[0:64, 2:3], dap(self.hy_sin_w, l * 64, [[1, 64], [1, 1]]), writes=[cols.b])

            def my_sin(A, L):
                s_ = tmp.next(); c_ = tmp.next()
                sv, cv, tv, av = s_.t[0:64, 0:L], c_.t[0:64, 0:L], accF.t[0:64, 0:L], A.t[0:64, 0:L]
                S.op("scalar", lambda E: E.activation(out=cv, in_=av, func=AF.Abs), reads=[A.b], writes=[c_.b])
                S.op("scalar", lambda E: E.activation(out=sv, in_=av, func=AF.Sin, scale=0.125), reads=[A.b], writes=[s_.b])
                S.op("vector", lambda E: E.tensor_scalar(out=cv, in0=cv, scalar1=-0.125, scalar2=math.pi / 2, op0=ALU.mult,
                                                         op1=ALU.add), reads=[c_.b], writes=[c_.b])
                S.op("scalar", lambda E: E.activation(out=cv, in_=cv, func=AF.Sin), reads=[c_.b], writes=[c_.b])
                for it in range(3):
                    S.op("gpsimd", lambda E: E.tensor_tensor(out=tv, in0=sv, in1=sv, op=ALU.mult), reads=[s_.b], writes=[accF.b])
                    S.op("vector", lambda E: E.scalar_tensor_tensor(out=sv, in0=sv, scalar=2.0, in1=cv, op0=ALU.mult, op1=ALU.mult),
                         reads=[s_.b, c_.b, accF.b], writes=[s_.b])
                    S.op("vector", lambda E: E.tensor_scalar(out=cv, in0=tv, scalar1=-2.0, scalar2=1.0, op0=ALU.mult, op1=ALU.add),
                         reads=[accF.b, s_.b], writes=[c_.b])
                S.op("gpsimd", lambda E: E.tensor_copy(out=av, in_=sv), reads=[s_.b], writes=[A.b])

            def build_hidden(L, z_d):
                z = tmp.next()
                S.dma("sync", z.t[0:33, 0:L], z_d.ap(), writes=[z.b])
                S.dma("sync", trow.t[:, 0:L], dap(z_d, 0, [[0, 128], [1, L]]), writes=[trow.b])
                blk = min(512, L)
                for b0 in range(0, L, blk):
                    p = pX.next()
                    S.op("tensor", lambda E, p=p, b0=b0: E.matmul(p.t[0:64, 0:blk], W1.t[0:33, :], z.t[0:33, b0:b0 + blk],
                                                                   start=True, stop=True), reads=[W1.b, z.b], writes=[p.b])
                    S.op("vector", lambda E, p=p, b0=b0: E.tensor_scalar(
                        out=u.t[0:64, b0:b0 + blk], in0=p.t[0:64, 0:blk], scalar1=cols.t[0:64, 0:1], scalar2=cols.t[0:64, 2:3],
                        op0=ALU.add, op1=ALU.mult), reads=[p.b, cols.b], writes=[u.b])
                my_sin(u, L)
                for b0 in range(0, L, blk):
                    p = pX.next()
                    S.op("tensor", lambda E, p=p, b0=b0: E.matmul(p.t[0:64, 0:blk], W2.t[0:64, :], u.t[0:64, b0:b0 + blk],
                                                                   start=True, stop=True), reads=[W2.b, u.b], writes=[p.b])
                    S.op("vector", lambda E, p=p, b0=b0: E.tensor_scalar(
                        out=hdn.t[0:64, b0:b0 + blk], in0=p.t[0:64, 0:blk], scalar1=cols.t[0:64, 1:2], scalar2=cols.t[0:64, 2:3],
                        op0=ALU.add, op1=ALU.mult), reads=[p.b, cols.b], writes=[hdn.b])
                my_sin(hdn, L)

            def build_filters(L, cc):
                blk = min(512, L)
                for o in range(2):
                    for d in range(2):
                        c0 = (o * 2 + d) * 512 + cc * 128
                        f = filt[o][d]
                        S.dma("sync", W3.t[0:64, :], dap(self.hy_f_w3, l * 64 * 2048 + c0, [[2048, 64], [1, 128]]), writes=[W3.b])
                        S.dma("sync", cols.t[:, 3:4], dap(self.hy_f_b3, l * 2048 + c0, [[1, 128], [1, 1]]), writes=[cols.b])
                        S.dma("sync", cols.t[:, 4:5], dap(self.hy_decay, l * 2048 + c0, [[1, 128], [1, 1]]), writes=[cols.b])
                        S.op("scalar", lambda E: E.activation(out=cols.t[:, 4:5], in_=cols.t[:, 4:5], func=AF.Abs),
                             reads=[cols.b], writes=[cols.b])
                        S.op("vector", lambda E: E.tensor_scalar(out=cols.t[:, 4:5], in0=cols.t[:, 4:5], scalar1=-1.0, scalar2=None,
                                                                 op0=ALU.mult), reads=[cols.b], writes=[cols.b])
                        for b0 in range(0, L, blk):
                            p = pX.next()
                            S.op("tensor", lambda E, p=p, b0=b0: E.matmul(p.t[:, 0:blk], W3.t[0:64, :], hdn.t[0:64, b0:b0 + blk],
                                                                           start=True, stop=True), reads=[W3.b, hdn.b], writes=[p.b])
                            S.op("scalar", lambda E, p=p, b0=b0, f=f: E.activation(
                                out=f.t[:, b0:b0 + blk], in_=p.t[:, 0:blk], func=AF.Identity, bias=cols.t[:, 3:4]),
                                reads=[p.b, cols.b], writes=[f.b])
                        e_ = tmp.next()
                        S.op("scalar", lambda E, e_=e_: E.activation(out=e_.t[:, 0:L], in_=trow.t[:, 0:L], func=AF.Exp,
                                                                     scale=cols.t[:, 4:5]), reads=[trow.b, cols.b], writes=[e_.b])
                        S.op("vector", lambda E, e_=e_, f=f: E.tensor_tensor(out=f.t[:, 0:L], in0=f.t[:, 0:L], in1=e_.t[:, 0:L],
                                                                             op=ALU.mult), reads=[f.b, e_.b], writes=[f.b])
                        S.op("scalar", lambda E, e_=e_, f=f: E.activation(out=e_.t[:, 0:L], in_=f.t[:, 0:L], func=AF.Abs),
                             reads=[f.b], writes=[e_.b])
                        S.op("vector", lambda E, e_=e_, d=d: E.tensor_reduce(out=cols.t[:, 6 + d:7 + d], in_=e_.t[:, 0:L],
                                                                             axis=mybir.AxisListType.X, op=ALU.add),
                             reads=[e_.b], writes=[cols.b])
                    S.op("vector", lambda E: E.scalar_tensor_tensor(out=cols.t[:, 6:7], in0=cols.t[:, 6:7], scalar=EPS, in1=cols.t[:, 7:8],
                                                                    op0=ALU.add, op1=ALU.add), reads=[cols.b], writes=[cols.b])
                    S.op("vector", lambda E: E.reciprocal(out=cols.t[:, 6:7], in_=cols.t[:, 6:7]), reads=[cols.b], writes=[cols.b])
                    for d in range(2):
                        S.op("vector" if d == 0 else "gpsimd", lambda E, o=o, d=d: E.tensor_scalar(
                            out=filt[o][d].t[:, 0:L], in0=filt[o][d].t[:, 0:L], scalar1=cols.t[:, 6:7], scalar2=None, op0=ALU.mult),
                            reads=[filt[o][d].b, cols.b], writes=[filt[o][d].b])
                    S.dma("sync", cols.t[:, 5:6], dap(self.hy_skip, (l * 2 + o) * 512 + cc * 128, [[1, 128], [1, 1]]), writes=[cols.b])
                    S.op("vector", lambda E, o=o: E.tensor_tensor(out=filt[o][0].t[:, 0:1], in0=filt[o][0].t[:, 0:1], in1=cols.t[:, 5:6],
                                                                  op=ALU.add), reads=[filt[o][0].b, cols.b], writes=[filt[o][0].b])

            def short_conv(tok0, L, cc, j, dst_ap, dst_b):
                x = tmp.next()
                S.op("vector", lambda E: E.memset(x.t[:, 0:1], 0.0), writes=[x.b])
                S.op("vector", lambda E: E.memset(x.t[:, L + 1:L + 2], 0.0), writes=[x.b])
                S.dma("sync", x.t[:, 1:L + 1], dap(pT_d, (HR + j * 512 + cc * 128) * T + tok0, [[T, 128], [1, L]]), writes=[x.b])
                S.op("vector", lambda E: E.tensor_scalar(out=dst_ap, in0=x.t[:, 0:L], scalar1=hcw.t[:, j, 0:1], scalar2=None,
                                                         op0=ALU.mult), reads=[x.b, hcw.b], writes=[dst_b])
                for k in (1, 2):
                    S.op("vector", lambda E, k=k: E.scalar_tensor_tensor(out=dst_ap, in0=x.t[:, k:k + L], scalar=hcw.t[:, j, k:k + 1],
                                                                         in1=dst_ap, op0=ALU.mult, op1=ALU.add),
                         reads=[x.b, hcw.b, dst_b], writes=[dst_b])
                S.op("scalar", lambda E: E.activation(out=dst_ap, in_=dst_ap, func=AF.Identity, bias=hcb.t[:, j:j + 1]),
                     reads=[dst_b, hcb.b], writes=[dst_b])

            def long_conv(L, o):
                hf, hb = filt[o][0], filt[o][1]
                S.op("vector", lambda E: E.tensor_scalar(out=accF.t[:, 0:L], in0=u.t[:, 0:L], scalar1=hf.t[:, 0:1], scalar2=None,
                                                         op0=ALU.mult), reads=[u.b, hf.b], writes=[accF.b])
                S.op("gpsimd", lambda E: E.memset(accB.t[:, 0:L], 0.0), writes=[accB.b])
                for tau in range(1, L):
                    S.op("vector", lambda E, tau=tau: E.scalar_tensor_tensor(
                        out=accF.t[:, tau:L], in0=u.t[:, 0:L - tau], scalar=hf.t[:, tau:tau + 1], in1=accF.t[:, tau:L],
                        op0=ALU.mult, op1=ALU.add), reads=[u.b, hf.b, accF.b], writes=[accF.b])
                    t_ = tmp.next()
                    S.op("scalar", lambda E, tau=tau, t_=t_: E.activation(out=t_.t[:, 0:L - tau], in_=u.t[:, tau:L], func=AF.Copy,
                                                                          scale=hb.t[:, tau:tau + 1]), reads=[u.b, hb.b], writes=[t_.b])
                    S.op("gpsimd", lambda E, tau=tau, t_=t_: E.tensor_tensor(out=accB.t[:, 0:L - tau], in0=accB.t[:, 0:L - tau],
                                                                             in1=t_.t[:, 0:L - tau], op=ALU.add),
                         reads=[accB.b, t_.b], writes=[accB.b])
                S.op("vector", lambda E: E.tensor_tensor(out=accF.t[:, 0:L], in0=accF.t[:, 0:L], in1=accB.t[:, 0:L], op=ALU.add),
                     reads=[accF.b, accB.b], writes=[accF.b])

            def do_seq(tok0, L, cc):
                short_conv(tok0, L, cc, 0, u.t[:, 0:L], u.b)
                long_conv(L, 0)
                g1 = tmp.next()
                short_conv(tok0, L, cc, 1, g1.t[:, 0:L], g1.b)
                S.op("vector", lambda E: E.tensor_tensor(out=u.t[:, 0:L], in0=g1.t[:, 0:L], in1=accF.t[:, 0:L], op=ALU.mult),
                     reads=[g1.b, accF.b], writes=[u.b])
                long_conv(L, 1)
                g2 = tmp.next()
                short_conv(tok0, L, cc, 2, g2.t[:, 0:L], g2.b)
                S.op("vector", lambda E: E.tensor_tensor(out=g2.t[:, 0:L], in0=g2.t[:, 0:L], in1=accF.t[:, 0:L], op=ALU.mult),
                     reads=[g2.b, accF.b], writes=[g2.b])
                S.dma("sync", dap(oT_d, (1536 + cc * 128) * T + tok0, [[T, 128], [1, L]]), g2.t[:, 0:L], reads=[g2.b], writes=[ob])

            for (L, z_d, seqs) in ((LS, self.hy_z_s, [0]), (LP, self.hy_z_p, [LS + i * LP for i in range(NPR)])):
                build_hidden(L, z_d)
                for cc in range(4):
                    for k in range(3):
                        S.dma("sync", hcw.t[:, :, k], dap(self.hy_conv_w, (l * 3 + k) * 1536 + cc * 128, [[1, 128], [512, 3]]),
                              writes=[hcw.b], **NCD)
                    S.dma("sync", hcb.t[:], dap(self.hy_conv_b, l * 1536 + cc * 128, [[1, 128], [512, 3]]), writes=[hcb.b], **NCD)
                    build_filters(L, cc)
                    for tok0 in seqs:
                        do_seq(tok0, L, cc)


_CACHE = {}


def _rope_consts():
    t = np.arange(LS)
    r = (t // 64).astype(np.float32)
    col = (t % 64).astype(np.float32)
    inv = (10000.0 ** (-np.arange(8, dtype=np.float32) / 8)).astype(np.float32)
    ang = np.concatenate([r[:, None] * inv, col[:, None] * inv], axis=-1).astype(np.float32)
    idx = (np.arange(128) % 32) % 16
    ropec = np.ascontiguousarray(np.cos(ang)[:, idx].T.astype(np.float32))
    ropes = np.ascontiguousarray(np.sin(ang)[:, idx].T.astype(np.float32))
    R = np.zeros((128, 128), np.float32)
    for p in range(128):
        if p % 32 < 16:
            R[p + 16, p] = -1.0
        else:
            R[p - 16, p] = 1.0
    rowmask = np.zeros((128, 4), np.float32)
    for p in range(128):
        rowmask[p, p // 32] = 1.0
    i = np.arange(128)
    le = (i[:, None] <= i[None, :]).astype(np.float32)
    ge = (i[:, None] >= i[None, :]).astype(np.float32)
    gt = (i[:, None] > i[None, :]).astype(np.float32)
    lt = (i[:, None] < i[None, :]).astype(np.float32)
    c = np.float32(-1.0 / 16.0)
    gla_tri = np.stack([c * le, c * ge, c * gt, c * lt, le, ge], axis=0).astype(np.float32)
    bm = (np.arange(128)[:, None] // 32 == np.arange(256)[None, :] // 64).astype(np.float32)
    def zfeat(L):
        t = (np.arange(L, dtype=np.float32) / np.float32(L)).astype(np.float32)[:, None]
        bands = np.arange(1, 17, dtype=np.float32)
        ang = (np.float32(2.0 * math.pi) * t * bands).astype(np.float32)
        return np.ascontiguousarray(np.concatenate([t, np.cos(ang), np.sin(ang)], axis=-1).T.astype(np.float32))
    hyc = dict(hy_z_s=zfeat(LS), hy_z_p=zfeat(LP))
    neg = np.float32(-30000.0)
    ssd_tri = np.stack([le, ge, gt, lt, neg * (1.0 - le), neg * (1.0 - ge)], axis=0).astype(np.float32)
    gmask = np.zeros((128, 2), np.float32)
    gmask[:64, 0] = 1.0
    gmask[64:, 1] = 1.0
    return dict(ropec=ropec, ropes=ropes, roper=R, rowmask=rowmask, gla_tri=np.ascontiguousarray(gla_tri),
                gla_bmask=np.ascontiguousarray(bm), ssd_tri=np.ascontiguousarray(ssd_tri), ssd_gmask=gmask, **hyc)


def _get_prog(**kw):
    key = tuple(sorted(kw.items()))
    if key not in _CACHE:
        p = Prog(**kw)
        p.build()
        _CACHE[key] = p
    return _CACHE[key]


def kernel(x_prompt, x_sample, cache_diff_k, cache_diff_v, state_gla, state_ssd, c, c_ctx,
           ada_w, ada_b, norm1_g, norm2_g, w_in, w_out,
           gla_gate_w, gla_gate_b, gla_norm_g, diff_lambda, diff_norm_g,
           ssd_conv_w, ssd_conv_b, ssd_dt_bias, ssd_a_log, ssd_d, ssd_norm_g,
           hy_conv_w, hy_conv_b, hy_f_w1, hy_f_b1, hy_f_w2, hy_f_b2, hy_f_w3, hy_f_b3,
           hy_sin_w, hy_decay, hy_skip, mlp_w1, mlp_w2, final_g, _prog_kw=None):
    f = lambda a: np.ascontiguousarray(np.asarray(a, dtype=np.float32))
    prog = _get_prog(**(_prog_kw or {}))
    shared = dict(ada_w=f(ada_w), ada_b=f(ada_b), norm1_g=f(norm1_g), norm2_g=f(norm2_g), w_in=f(w_in),
                  w_out=f(w_out), mlp_w1=f(mlp_w1), mlp_w2=f(mlp_w2), final_g=f(final_g),
                  ident=np.eye(128, dtype=np.float32), diff_lambda=f(diff_lambda), diff_norm_g=f(diff_norm_g),
                  gla_gate_w=f(gla_gate_w), gla_gate_b=f(gla_gate_b), gla_norm_g=f(gla_norm_g),
                  ssd_conv_w=f(ssd_conv_w), ssd_conv_b=f(ssd_conv_b), ssd_dt_bias=f(ssd_dt_bias), ssd_a_log=f(ssd_a_log),
                  ssd_d=f(ssd_d), ssd_norm_g=f(ssd_norm_g),
                  hy_conv_w=f(hy_conv_w), hy_conv_b=f(hy_conv_b), hy_f_w1=f(hy_f_w1), hy_f_b1=f(hy_f_b1), hy_f_w2=f(hy_f_w2),
                  hy_f_b2=f(hy_f_b2), hy_f_w3=f(hy_f_w3), hy_f_b3=f(hy_f_b3), hy_sin_w=f(hy_sin_w), hy_decay=f(hy_decay),
                  hy_skip=f(hy_skip))
    state_ssd = f(state_ssd)
    state_gla = f(state_gla)
    shared.update(_rope_consts())
    cache_diff_k = f(cache_diff_k)
    cache_diff_v = f(cache_diff_v)
    x_prompt = f(x_prompt)
    x_sample = f(x_sample)
    c = f(c)
    c_ctx = f(c_ctx)
    in_maps = []
    for i in range(NCORES):
        xin = np.concatenate([x_sample[i], x_prompt[NPR * i:NPR * (i + 1)].reshape(NPR * LP, D)], axis=0)
        m = dict(shared)
        m["xin"] = np.ascontiguousarray(xin)
        m["c2"] = np.ascontiguousarray(np.stack([c[i], c_ctx], axis=0))
        m["cache_k"] = cache_diff_k[i].reshape(DEPTH, 512, 512)
        m["cache_v"] = cache_diff_v[i].reshape(DEPTH, 512, 512)
        m["state_gla"] = state_gla[i]
        m["state_ssd"] = state_ssd[i]
        in_maps.append({k: v for k, v in m.items() if k in prog.inp})
    res = run_bass_kernel_spmd(prog.nc, in_maps, core_ids=list(range(NCORES)))
    if prog.debug:
        global DBG
        DBG = res.results
    ys = [r["y"] for r in res.results]
    y_sample = np.stack([y[:LS] for y in ys], axis=0)
    y_prompt = np.concatenate([y[LS:].reshape(NPR, LP, D) for y in ys], axis=0)
    if not prog.mixers:
        return y_prompt, y_sample
    B = NCORES * NPR
    new_k = np.concatenate([r["nk"] for r in res.results], axis=0).reshape(B, DEPTH, LP, 8, 64)
    new_v = np.concatenate([r["nv"] for r in res.results], axis=0).reshape(B, DEPTH, LP, 8, 64)
    new_gla = np.concatenate([r["ngla"] for r in res.results], axis=0)
    if ENABLE_SSD:
        new_ssd = np.concatenate([r["nssd"] for r in res.results], axis=0)
    else:
        new_ssd = np.zeros((B, DEPTH, 2, 8, 64, 64), np.float32)
    return y_prompt, y_sample, new_k, new_v, new_gla, new_ssd
```

```python
import contextlib
import math
import numpy as np
import concourse.bass as bass
import concourse.mybir as mybir
from concourse.bass_utils import run_bass_kernel_spmd

F32 = mybir.dt.float32
BF16 = mybir.dt.bfloat16
AF = mybir.ActivationFunctionType
ALU = mybir.AluOpType

NCORES = 8
D = 2048
DEPTH = 2
LS = 4096
LP = 256
NPR = 4
T = LS + NPR * LP
G = 512
NG = T // G
IN_W = 5936
D_FF = 8192
EPS = 1e-6

SAME_ENGINE_SYNC = True
NCD = dict(allow_slow_non_contiguous=True)
ENABLE_SSD = True
ENABLE_HY = True


class Buf:
    __slots__ = ("name", "w", "r")

    def __init__(self, name=""):
        self.name = name
        self.w = None
        self.r = []


class Sched:
    ENGS = ("tensor", "vector", "scalar", "gpsimd", "sync")

    def __init__(self, nc, n_dma_sems=12):
        self.nc = nc
        self.prog = {e: [] for e in self.ENGS}
        self.cnt = {e: 0 for e in self.ENGS}
        self.known = {e: {} for e in self.ENGS}
        self.sems = {}
        self._ctx = []
        for e in self.ENGS:
            self._mk("c_" + e)
        self.dma_ring = {}
        for q in ("sync", "gpsimd", "scalar"):
            keys = []
            for i in range(n_dma_sems):
                k = "d_%s_%d" % (q, i)
                self._mk(k)
                keys.append(k)
            self.dma_ring[q] = [keys, 0, {k: 0 for k in keys}]

    def _mk(self, key):
        cm = self.nc.semaphore(key)
        h = cm.__enter__()
        self._ctx.append(cm)
        self.sems[key] = h

    def _need(self, eng, tok, waits, force=False):
        if tok is None:
            return
        k, v = tok
        if (not force) and k == "c_" + eng and (eng == "tensor" or not SAME_ENGINE_SYNC):
            return
        if self.known[eng].get(k, 0) >= v:
            return
        waits[k] = max(waits.get(k, 0), v)

    def _deps(self, eng, reads, writes, force=False):
        waits = {}
        for b in reads:
            self._need(eng, b.w, waits, force)
        for b in writes:
            self._need(eng, b.w, waits, force)
            for t in b.r:
                self._need(eng, t, waits, force)
        return waits

    def _emit_waits(self, eng, waits):
        for k, v in waits.items():
            self.known[eng][k] = v
            h = self.sems[k]
            self.prog[eng].append(lambda E, h=h, v=v: E.wait_ge(h, v))

    def _mark(self, tok, reads, writes):
        for b in reads:
            b.r.append(tok)
            if len(b.r) > 64:
                b.r = b.r[-64:] if False else b.r
        for b in writes:
            b.w = tok
            b.r = []

    def op(self, eng, fn, reads=(), writes=(), sreads=()):
        waits = self._deps(eng, reads, writes)
        for k, v in self._deps(eng, sreads, (), force=True).items():
            waits[k] = max(waits.get(k, 0), v)
        reads = list(reads) + list(sreads)
        self._emit_waits(eng, waits)
        self.cnt[eng] += 1
        v = self.cnt[eng]
        h = self.sems["c_" + eng]
        self.prog[eng].append(lambda E, fn=fn, h=h: fn(E).then_inc(h, 1))
        self._mark(("c_" + eng, v), reads, writes)

    def dma(self, q, out, in_, reads=(), writes=(), **kw):
        keys, pos, vals = self.dma_ring[q]
        k = keys[pos % len(keys)]
        self.dma_ring[q][1] = pos + 1
        waits = self._deps(q, reads, writes, force=True)
        if vals[k] > 0 and self.known[q].get(k, 0) < vals[k]:
            waits[k] = max(waits.get(k, 0), vals[k])
        self._emit_waits(q, waits)
        vals[k] += 16
        v = vals[k]
        h = self.sems[k]
        self.prog[q].append(
            lambda E, out=out, in_=in_, h=h, kw=kw: E.dma_start(out=out, in_=in_, **kw).then_inc(h, 16))
        self._mark((k, v), reads, writes)

    def barrier(self):
        for e in self.ENGS:
            waits = {}
            for o in self.ENGS:
                if o != e and self.cnt[o] > 0:
                    self._need(e, ("c_" + o, self.cnt[o]), waits)
            for q in self.dma_ring:
                for k, v in self.dma_ring[q][2].items():
                    if v > 0:
                        self._need(e, (k, v), waits)
            self._emit_waits(e, waits)

    def emit(self):
        nc = self.nc
        with nc.Block() as block:
            for e in self.ENGS:
                lst = self.prog[e]
                if not lst:
                    continue

                def body(E, lst=lst):
                    for f in lst:
                        f(E)
                getattr(block, e)(body)
        for cm in reversed(self._ctx):
            cm.__exit__(None, None, None)


class Tile:
    def __init__(self, t, name=""):
        self.t = t
        self.b = Buf(name)


_UID = [0]


class Ring:
    def __init__(self, es, nc, name, n, shape, dtype, psum=False):
        self.tiles = []
        _UID[0] += 1
        name = "%s_%d_" % (name, _UID[0])
        for i in range(n):
            if psum:
                t = es.enter_context(nc.psum_tensor("p_%s%d" % (name, i), shape, dtype))
            else:
                t = es.enter_context(nc.sbuf_tensor("s_%s%d" % (name, i), shape, dtype))
            self.tiles.append(Tile(t, "%s%d" % (name, i)))
        self.i = 0

    def next(self):
        t = self.tiles[self.i % len(self.tiles)]
        self.i += 1
        return t


class RingView:
    def __init__(self, tiles):
        self.tiles = list(tiles)
        self.i = 0

    def next(self):
        t = self.tiles[self.i % len(self.tiles)]
        self.i += 1
        return t


def dap(t, offset, dims):
    return bass.AP(t, offset, [list(d) for d in dims])


class Prog:
    def __init__(self, n_layers=DEPTH, mixers=True, debug=False):
        self.debug = debug
        self.n_layers = n_layers
        self.mixers = mixers
        nc = self.nc = bass.Bass("TRN2", target_bir_lowering=False)
        self.S = Sched(nc)
        self.inp = {}
        self.out = {}

    def din(self, name, shape):
        t = self.nc.dram_tensor(name, list(shape), F32, kind="ExternalInput")
        self.inp[name] = t
        return t

    def dout(self, name, shape):
        t = self.nc.dram_tensor(name, list(shape), F32, kind="ExternalOutput")
        self.out[name] = t
        return t

    def dscr(self, name, shape, dtype=F32):
        return self.nc.dram_tensor(name, list(shape), dtype, kind="Internal")

    def wtile(self, W, base, ld, k0, n0, ncols, nk=16):
        wt = self.wring.next()
        src = dap(W, base + k0 * ld + n0, [[ld, 128], [128 * ld, nk], [1, ncols]])
        self.S.dma("gpsimd", wt.t[:, 0:nk, 0:ncols], src, writes=[wt.b])
        return wt

    def evac(self, i, out, in_, reads, writes):
        if i % 2 == 0:
            self.S.op("scalar", lambda E: E.activation(out=out, in_=in_, func=AF.Copy), reads=reads, writes=writes)
        else:
            self.S.op("vector", lambda E: E.tensor_copy(out=out, in_=in_), reads=reads, writes=writes)

    def build(self):
        nc, S = self.nc, self.S
        L = self.n_layers
        xin = self.din("xin", [T, D])
        c2 = self.din("c2", [2, D])
        ada_w = self.din("ada_w", [DEPTH, D, 6 * D])
        ada_b = self.din("ada_b", [DEPTH, 6 * D])
        norm1_g = self.din("norm1_g", [DEPTH, D])
        norm2_g = self.din("norm2_g", [DEPTH, D])
        w_in = self.din("w_in", [DEPTH, D, IN_W])
        w_out = self.din("w_out", [DEPTH, D, D])
        mlp_w1 = self.din("mlp_w1", [DEPTH, D, D_FF])
        mlp_w2 = self.din("mlp_w2", [DEPTH, D_FF, D])
        final_g = self.din("final_g", [D])
        ident_d = self.din("ident", [128, 128])
        if self.mixers:
            self.cache_k = self.din("cache_k", [DEPTH, 512, 512])
            self.cache_v = self.din("cache_v", [DEPTH, 512, 512])
            self.diff_lambda = self.din("diff_lambda", [DEPTH, 4, 32])
            self.diff_norm_g = self.din("diff_norm_g", [DEPTH, 64])
            self.ropec = self.din("ropec", [128, LS])
            self.ropes = self.din("ropes", [128, LS])
            self.roper = self.din("roper", [128, 128])
            self.rowmask = self.din("rowmask", [128, 4])
            self.gla_tri = self.din("gla_tri", [6, 128, 128])
            self.gla_bmask = self.din("gla_bmask", [128, 256])
            self.gla_gate_w = self.din("gla_gate_w", [DEPTH, 2, 16, 256])
            self.gla_gate_b = self.din("gla_gate_b", [DEPTH, 2, 256])
            self.gla_norm_g = self.din("gla_norm_g", [DEPTH, 64])
            self.state_gla = self.din("state_gla", [DEPTH, 2, 8, 32, 64])
            self.ngla_d = self.dout("ngla", [NPR, DEPTH, 2, 8, 32, 64])
            if ENABLE_SSD:
              self.ssd_tri = self.din("ssd_tri", [6, 128, 128])
              self.ssd_gmask = self.din("ssd_gmask", [128, 2])
              self.ssd_conv_w = self.din("ssd_conv_w", [DEPTH, 3, 768])
              self.ssd_conv_b = self.din("ssd_conv_b", [DEPTH, 768])
              self.ssd_dt_bias = self.din("ssd_dt_bias", [DEPTH, 2, 8])
              self.ssd_a_log = self.din("ssd_a_log", [DEPTH, 2, 8])
              self.ssd_d = self.din("ssd_d", [DEPTH, 8])
              self.ssd_norm_g = self.din("ssd_norm_g", [DEPTH, 512])
              self.state_ssd = self.din("state_ssd", [DEPTH, 2, 8, 64, 64])
              self.nssd_d = self.dout("nssd", [NPR, DEPTH, 2, 8, 64, 64])
            if ENABLE_HY:
                self.hy_z_s = self.din("hy_z_s", [33, LS])
                self.hy_z_p = self.din("hy_z_p", [33, LP])
                self.hy_conv_w = self.din("hy_conv_w", [DEPTH, 3, 1536])
                self.hy_conv_b = self.din("hy_conv_b", [DEPTH, 1536])
                self.hy_f_w1 = self.din("hy_f_w1", [DEPTH, 33, 64])
                self.hy_f_b1 = self.din("hy_f_b1", [DEPTH, 64])
                self.hy_f_w2 = self.din("hy_f_w2", [DEPTH, 64, 64])
                self.hy_f_b2 = self.din("hy_f_b2", [DEPTH, 64])
                self.hy_f_w3 = self.din("hy_f_w3", [DEPTH, 64, 2048])
                self.hy_f_b3 = self.din("hy_f_b3", [DEPTH, 2048])
                self.hy_sin_w = self.din("hy_sin_w", [DEPTH, 64])
                self.hy_decay = self.din("hy_decay", [DEPTH, 2048])
                self.hy_skip = self.din("hy_skip", [DEPTH, 2, 512])
            self.nk_d = self.dout("nk", [NPR, DEPTH, LP, 512])
            self.nv_d = self.dout("nv", [NPR, DEPTH, LP, 512])
        self.ident = None
        y_d = self.dout("y", [T, D])
        xT_d = self.dscr("xT", [D, T])
        pT_d = self.dscr("pT", [IN_W, T])
        oT_d = self.dscr("oT", [D, T])
        self.obufs = []

        with contextlib.ExitStack() as es:
            ident = Tile(es.enter_context(nc.sbuf_tensor("s_ident", [128, 128], F32)))
            onesb = Tile(es.enter_context(nc.sbuf_tensor("s_onesb", [128, 128], BF16)))
            modT = Tile(es.enter_context(nc.sbuf_tensor("s_modT", [128, 96, 2], F32)))
            sc1 = Tile(es.enter_context(nc.sbuf_tensor("s_sc1", [128, 16, 2], F32)))
            sc2 = Tile(es.enter_context(nc.sbuf_tensor("s_sc2", [128, 16, 2], F32)))
            cT = Tile(es.enter_context(nc.sbuf_tensor("s_cT", [128, 16, 2], BF16)))
            cTf = Tile(es.enter_context(nc.sbuf_tensor("s_cTf", [128, 16, 2], F32)))
            gvec = Tile(es.enter_context(nc.sbuf_tensor("s_gvec", [128, 16], F32)))
            adab = Tile(es.enter_context(nc.sbuf_tensor("s_adab", [128, 96], F32)))
            fing = Tile(es.enter_context(nc.sbuf_tensor("s_fing", [128, 16], F32)))
            self.ps = Ring(es, nc, "ps", 6, [128, 512], F32, psum=True)
            psr = Ring(es, nc, "psr", 2, [128, 512], F32, psum=True)
            self.psr = psr

            S.dma("sync", ident.t[:], ident_d.ap(), writes=[ident.b])
            self.ident = ident
            S.op("vector", lambda E: E.memset(onesb.t[:], 1.0), writes=[onesb.b])
            for cv in range(2):
                S.dma("sync", cTf.t[:, :, cv], dap(c2, cv * D, [[1, 128], [128, 16]]), writes=[cTf.b], **NCD)
            S.dma("sync", fing.t[:], dap(final_g, 0, [[1, 128], [128, 16]]), writes=[fing.b], **NCD)
            S.op("scalar", lambda E: E.activation(out=cT.t[:], in_=cTf.t[:], func=AF.Silu), reads=[cTf.b], writes=[cT.b])

            with contextlib.ExitStack() as e0:
                xtok = Ring(e0, nc, "xtok", 2, [128, D], F32)
                xTt = Ring(e0, nc, "xTt", 2, [128, 16, G], F32)
                xTb = Buf("xT_d")
                for g in range(NG):
                    dst = xTt.next()
                    for s in range(4):
                        tt = g * 4 + s
                        xt = xtok.next()
                        S.dma("sync", xt.t[:], xin.ap()[tt * 128:(tt + 1) * 128, :], writes=[xt.b])
                        for q in range(4):
                            p = self.ps.next()
                            for j in range(4):
                                fc = q * 4 + j
                                S.op("tensor", lambda E, p=p, xt=xt, fc=fc, j=j: E.transpose(
                                    out=p.t[:, j * 128:(j + 1) * 128], in_=xt.t[:, fc * 128:(fc + 1) * 128],
                                    identity=ident.t[:]), reads=[xt.b, ident.b], writes=[p.b])
                            self.evac(q, dst.t[:, q * 4:(q + 1) * 4, s * 128:(s + 1) * 128],
                                      p.t[:, :].rearrange("p (a b) -> p a b", a=4), [p.b], [dst.b])
                    S.dma("sync", dap(xT_d, g * G, [[T, 128], [128 * T, 16], [1, G]]), dst.t[:],
                          reads=[dst.b], writes=[xTb])
            S.barrier()

            for l in range(L):
                ewa = contextlib.ExitStack()
                self.wring = Ring(ewa, nc, "wt", 3, [128, 16, 512], BF16)
                S.dma("sync", adab.t[:], dap(ada_b, l * 6 * D, [[1, 128], [128, 96]]), writes=[adab.b], **NCD)
                for nb in range(24):
                    wt = self.wtile(ada_w, l * D * 6 * D, 6 * D, 0, nb * 512, 512)
                    p = self.ps.next()
                    for j in range(4):
                        for kc in range(16):
                            S.op("tensor", lambda E, p=p, wt=wt, j=j, kc=kc: E.matmul(
                                p.t[:, j * 2:(j + 1) * 2], wt.t[:, kc, j * 128:(j + 1) * 128], cT.t[:, kc, :],
                                start=(kc == 0), stop=(kc == 15)), reads=[wt.b, cT.b], writes=[p.b])
                    S.op("vector", lambda E, p=p, nb=nb: E.tensor_tensor(
                        out=modT.t[:, nb * 4:(nb + 1) * 4, :],
                        in0=p.t[:, 0:8].rearrange("p (a b) -> p a b", a=4),
                        in1=adab.t[:, nb * 4:(nb + 1) * 4].unsqueeze(2).broadcast_to([128, 4, 2]),
                        op=ALU.add), reads=[p.b, adab.b], writes=[modT.b])
                for (sct, ng, j6) in ((sc1, norm1_g, 1), (sc2, norm2_g, 4)):
                    S.dma("sync", gvec.t[:], dap(ng, l * D, [[1, 128], [128, 16]]), writes=[gvec.b], **NCD)
                    S.op("vector", lambda E, sct=sct, j6=j6: E.tensor_scalar(
                        out=sct.t[:], in0=modT.t[:, j6 * 16:(j6 + 1) * 16, :], scalar1=1.0, scalar2=math.sqrt(D),
                        op0=ALU.add, op1=ALU.mult), reads=[modT.b], writes=[sct.b])
                    S.op("vector", lambda E, sct=sct: E.tensor_tensor(
                        out=sct.t[:], in0=sct.t[:], in1=gvec.t[:].unsqueeze(2).broadcast_to([128, 16, 2]),
                        op=ALU.mult), reads=[gvec.b, sct.b], writes=[sct.b])
                S.barrier()

                with contextlib.ExitStack() as eb:
                    xg = Ring(eb, nc, "xg", 2, [128, 16, G], F32)
                    uT = Ring(eb, nc, "uT", 2, [128, 16, G], BF16)
                    sq = Ring(eb, nc, "sq", 2, [128, G], BF16)
                    tmp = Ring(eb, nc, "tmp", 2, [128, G], F32)
                    rstd = Ring(eb, nc, "rstd", 2, [128, G], F32)
                    stg = Ring(eb, nc, "stg", 3, [128, G], F32)
                    pTb = Buf("pT_d")
                    for g in range(NG):
                        cv = 0 if g < LS // G else 1
                        x = xg.next()
                        S.dma("sync", x.t[:], dap(xT_d, g * G, [[T, 128], [128 * T, 16], [1, G]]), writes=[x.b])
                        u = uT.next()
                        self.norm_mod(x, u, sq, tmp, rstd, psr, onesb, sc1, modT, 0, cv)
                        ei = 0
                        for nb in range(12):
                            ncols = min(512, IN_W - nb * 512)
                            wt = self.wtile(w_in, l * D * IN_W, IN_W, 0, nb * 512, ncols)
                            for j in range((ncols + 127) // 128):
                                M = min(128, ncols - j * 128)
                                p = self.ps.next()
                                for kc in range(16):
                                    S.op("tensor", lambda E, p=p, wt=wt, u=u, j=j, kc=kc, M=M: E.matmul(
                                        p.t[0:M, :], wt.t[:, kc, j * 128:j * 128 + M], u.t[:, kc, :],
                                        start=(kc == 0), stop=(kc == 15)), reads=[wt.b, u.b], writes=[p.b])
                                st = stg.next()
                                self.evac(ei, st.t[0:M, :], p.t[0:M, :], [p.b], [st.b])
                                ei += 1
                                r0 = nb * 512 + j * 128
                                S.dma("sync", pT_d.ap()[r0:r0 + M, g * G:(g + 1) * G], st.t[0:M, :],
                                      reads=[st.b], writes=[pTb])
                        if self.mixers and cv == 1:
                            for (c0, od) in ((2080, self.nk_d), (2592, self.nv_d)):
                                wt = self.wtile(w_in, l * D * IN_W, IN_W, 0, c0, 512)
                                for s4 in range(4):
                                    p = self.ps.next()
                                    for kc in range(16):
                                        S.op("tensor", lambda E, p=p, wt=wt, u=u, s4=s4, kc=kc: E.matmul(
                                            p.t[:, :], u.t[:, kc, s4 * 128:(s4 + 1) * 128], wt.t[:, kc, :],
                                            start=(kc == 0), stop=(kc == 15)), reads=[wt.b, u.b], writes=[p.b])
                                    st = stg.next()
                                    self.evac(ei, st.t[:, :], p.t[:, :], [p.b], [st.b])
                                    ei += 1
                                    tokp = (g - LS // G) * G + s4 * 128
                                    pi, pos = tokp // LP, tokp % LP
                                    kvb = Buf("kv_out")
                                    S.dma("sync", dap(od, ((pi * DEPTH + l) * LP + pos) * 512, [[512, 128], [1, 512]]),
                                          st.t[:, :], reads=[st.b], writes=[kvb])
                S.barrier()
                ewa.close()

                self.mixer_phase(l, pT_d, oT_d)
                S.barrier()

                with contextlib.ExitStack() as ee:
                    self.wring = Ring(ee, nc, "wt", 3, [128, 16, 512], BF16)
                    xg = Ring(ee, nc, "xg", 1, [128, 16, G], F32)
                    oT = Ring(ee, nc, "oTt", 1, [128, 16, G], BF16)
                    uT = Ring(ee, nc, "uT", 1, [128, 16, G], BF16)
                    hT = Ring(ee, nc, "hT", 1, [128, 64, G], BF16)
                    sq = Ring(ee, nc, "sq", 2, [128, G], BF16)
                    tmp = Ring(ee, nc, "tmp", 2, [128, G], F32)
                    rstd = Ring(ee, nc, "rstd", 1, [128, G], F32)
                    rl = Ring(ee, nc, "rl", 3, [128, G], BF16)
                    ytok = Ring(ee, nc, "ytok", 2, [128, D], F32)
                    xTb = Buf("xT_d")
                    last = (l == L - 1)
                    for g in range(NG):
                        cv = 0 if g < LS // G else 1
                        x = xg.next()
                        S.dma("sync", x.t[:], dap(xT_d, g * G, [[T, 128], [128 * T, 16], [1, G]]),
                              reads=[], writes=[x.b])
                        o = oT.next()
                        S.dma("gpsimd", o.t[:], dap(oT_d, g * G, [[T, 128], [128 * T, 16], [1, G]]), writes=[o.b])
                        for nb in range(4):
                            wt = self.wtile(w_out, l * D * D, D, 0, nb * 512, 512)
                            for j in range(4):
                                fc = nb * 4 + j
                                p = self.ps.next()
                                for kc in range(16):
                                    S.op("tensor", lambda E, p=p, wt=wt, o=o, j=j, kc=kc: E.matmul(
                                        p.t[:, :], wt.t[:, kc, j * 128:(j + 1) * 128], o.t[:, kc, :],
                                        start=(kc == 0), stop=(kc == 15)), reads=[wt.b, o.b], writes=[p.b])
                                S.op("vector", lambda E, p=p, x=x, fc=fc, cv=cv: E.scalar_tensor_tensor(
                                    out=x.t[:, fc, :], in0=p.t[:, :], scalar=modT.t[:, 2 * 16 + fc, cv:cv + 1],
                                    in1=x.t[:, fc, :], op0=ALU.mult, op1=ALU.add),
                                    reads=[p.b, x.b, modT.b], writes=[x.b])
                        u = uT.next()
                        self.norm_mod(x, u, sq, tmp, rstd, psr, onesb, sc2, modT, 3, cv)
                        h = hT.next()
                        for nb in range(16):
                            wt = self.wtile(mlp_w1, l * D * D_FF, D_FF, 0, nb * 512, 512)
                            for j in range(4):
                                ffc = nb * 4 + j
                                p = self.ps.next()
                                for kc in range(16):
                                    S.op("tensor", lambda E, p=p, wt=wt, u=u, j=j, kc=kc: E.matmul(
                                        p.t[:, :], wt.t[:, kc, j * 128:(j + 1) * 128], u.t[:, kc, :],
                                        start=(kc == 0), stop=(kc == 15)), reads=[wt.b, u.b], writes=[p.b])
                                r = rl.next()
                                S.op("scalar", lambda E, p=p, r=r: E.activation(out=r.t[:], in_=p.t[:, :], func=AF.Relu),
                                     reads=[p.b], writes=[r.b])
                                S.op("gpsimd", lambda E, r=r, h=h, ffc=ffc: E.tensor_tensor(
                                    out=h.t[:, ffc, :], in0=r.t[:], in1=r.t[:], op=ALU.mult),
                                    reads=[r.b], writes=[h.b])
                        for nb in range(4):
                            pp = [self.ps.next() for _ in range(4)]
                            for kq in range(4):
                                wt = self.wtile(mlp_w2, l * D_FF * D, D, kq * 2048, nb * 512, 512)
                                for j in range(4):
                                    p = pp[j]
                                    for kc in range(16):
                                        S.op("tensor", lambda E, p=p, wt=wt, h=h, j=j, kc=kc, kq=kq: E.matmul(
                                            p.t[:, :], wt.t[:, kc, j * 128:(j + 1) * 128], h.t[:, kq * 16 + kc, :],
                                            start=(kq == 0 and kc == 0), stop=(kq == 3 and kc == 15)),
                                            reads=[wt.b, h.b], writes=[p.b])
                            for j in range(4):
                                fc = nb * 4 + j
                                p = pp[j]
                                S.op("vector", lambda E, p=p, x=x, fc=fc, cv=cv: E.scalar_tensor_tensor(
                                    out=x.t[:, fc, :], in0=p.t[:, :], scalar=modT.t[:, 5 * 16 + fc, cv:cv + 1],
                                    in1=x.t[:, fc, :], op0=ALU.mult, op1=ALU.add),
                                    reads=[p.b, x.b, modT.b], writes=[x.b])
                        if not last:
                            S.dma("sync", dap(xT_d, g * G, [[T, 128], [128 * T, 16], [1, G]]), x.t[:],
                                  reads=[x.b], writes=[xTb])
                        else:
                            self.final_norm_store(x, u, sq, tmp, rstd, psr, onesb, fing, ident, ytok, y_d, g)
                S.barrier()

            S.emit()
        return nc

    def sumsq_bcast(self, x, sq, psr, onesb):
        S = self.S
        pr = psr.next()
        for fc in range(16):
            s = sq.next()
            S.op("scalar", lambda E, s=s, x=x, fc=fc: E.activation(out=s.t[:], in_=x.t[:, fc, :], func=AF.Square),
                 reads=[x.b], writes=[s.b])
            S.op("tensor", lambda E, pr=pr, s=s, fc=fc: E.matmul(pr.t[:, :], onesb.t[:], s.t[:],
                                                                 start=(fc == 0), stop=(fc == 15)),
                 reads=[s.b, onesb.b], writes=[pr.b])
        return pr

    def rsqrt(self, rs, pr, scale, bias):
        S = self.S
        S.op("scalar", lambda E: E.activation(out=rs.t[:], in_=pr.t[:, :], func=AF.Sqrt, scale=scale, bias=bias),
             reads=[pr.b], writes=[rs.b])
        S.op("vector", lambda E: E.reciprocal(out=rs.t[:], in_=rs.t[:]), reads=[rs.b], writes=[rs.b])

    def norm_mod(self, x, u, sq, tmp, rstd, psr, onesb, sct, modT, jsh, cv):
        S = self.S
        pr = self.sumsq_bcast(x, sq, psr, onesb)
        rs = rstd.next()
        self.rsqrt(rs, pr, 1.0, D * EPS)
        for fc in range(16):
            t = tmp.next()
            S.op("vector", lambda E, t=t, fc=fc: E.tensor_tensor(out=t.t[:], in0=x.t[:, fc, :], in1=rs.t[:], op=ALU.mult),
                 reads=[x.b, rs.b], writes=[t.b])
            S.op("scalar", lambda E, t=t, fc=fc: E.activation(
                out=u.t[:, fc, :], in_=t.t[:], func=AF.Identity,
                scale=sct.t[:, fc, cv:cv + 1], bias=modT.t[:, jsh * 16 + fc, cv:cv + 1]),
                reads=[t.b, sct.b, modT.b], writes=[u.b])

    def final_norm_store(self, x, u, sq, tmp, rstd, psr, onesb, fing, ident, ytok, y_d, g):
        S = self.S
        pr = self.sumsq_bcast(x, sq, psr, onesb)
        rs = rstd.next()
        self.rsqrt(rs, pr, 1.0 / D, EPS)
        for fc in range(16):
            S.op("vector", lambda E, fc=fc: E.scalar_tensor_tensor(
                out=x.t[:, fc, :], in0=x.t[:, fc, :], scalar=fing.t[:, fc:fc + 1], in1=rs.t[:],
                op0=ALU.mult, op1=ALU.mult), reads=[x.b, rs.b, fing.b], writes=[x.b])
        ybuf = Buf("y_d")
        self.obufs.append(ybuf)
        for s in range(4):
            yt = ytok.next()
            for q in range(4):
                p = self.ps.next()
                for j in range(4):
                    fc = q * 4 + j
                    S.op("tensor", lambda E, p=p, fc=fc, j=j, s=s: E.transpose(
                        out=p.t[:, j * 128:(j + 1) * 128], in_=x.t[:, fc, s * 128:(s + 1) * 128],
                        identity=ident.t[:]), reads=[x.b, ident.b], writes=[p.b])
                self.evac(q, yt.t[:, q * 512:(q + 1) * 512], p.t[:, :], [p.b], [yt.b])
            r0 = g * G + s * 128
            S.dma("sync", y_d.ap()[r0:r0 + 128, :], yt.t[:], reads=[yt.b], writes=[ybuf])

    def mixer_phase(self, l, pT_d, oT_d):
        nc, S = self.nc, self.S
        if not self.mixers:
            with contextlib.ExitStack() as em:
                cp = Ring(em, nc, "cp", 2, [128, 16, G], F32)
                ob = Buf("oT_d")
                for g in range(NG):
                    t = cp.next()
                    S.dma("sync", t.t[:], dap(pT_d, g * G, [[T, 128], [128 * T, 16], [1, G]]), writes=[t.b])
                    S.dma("sync", dap(oT_d, g * G, [[T, 128], [128 * T, 16], [1, G]]), t.t[:], reads=[t.b], writes=[ob])
            return
        if l == 0:
            with contextlib.ExitStack() as em:
                z = Tile(em.enter_context(nc.sbuf_tensor("s_zero_fill", [128, 16, G], F32)))
                ob = Buf("oT_d")
                S.op("vector", lambda E: E.memset(z.t[:], 0.0), writes=[z.b])
                for g in range(NG):
                    S.dma("sync", dap(oT_d, g * G, [[T, 128], [128 * T, 16], [1, G]]), z.t[:], reads=[z.b], writes=[ob])
                S.barrier()
        self.attn_phase(l, pT_d, oT_d)
        S.barrier()
        self.gla_phase(l, pT_d, oT_d)
        S.barrier()
        if ENABLE_SSD:
            self.ssd_phase(l, pT_d, oT_d)
            S.barrier()
        if ENABLE_HY:
            self.hyena_phase(l, pT_d, oT_d)
            S.barrier()
        if self.debug and l == 0:
            dbg = self.dout("dbg_oT", [D, T])
            dbp = self.dout("dbg_pT", [IN_W, T])
            db = Buf("dbg")
            for r in range(16):
                S.dma("sync", dbg.ap()[r * 128:(r + 1) * 128, :], oT_d.ap()[r * 128:(r + 1) * 128, :], writes=[db])
            for r in range(0, IN_W, 128):
                e = min(IN_W, r + 128)
                S.dma("sync", dbp.ap()[r:e, :], pT_d.ap()[r:e, :], writes=[db])
            S.barrier()

    def attn_phase(self, l, pT_d, oT_d):
        nc, S = self.nc, self.S
        QR, KR, VR = 1568, 2080, 2592
        lam_init = 0.8 - 0.6 * math.exp(-0.3 * l)
        with contextlib.ExitStack() as ea:
            NKT = (LS + 512) // 128
            KT = Tile(ea.enter_context(nc.sbuf_tensor("s_KT%d" % l, [128, 4, LS + 512], BF16)))
            VA = Tile(ea.enter_context(nc.sbuf_tensor("s_VA%d" % l, [128, NKT, 8, 65], BF16)))
            cosT = Tile(ea.enter_context(nc.sbuf_tensor("s_cosT%d" % l, [128, LS], F32)))
            sinT = Tile(ea.enter_context(nc.sbuf_tensor("s_sinT%d" % l, [128, LS], F32)))
            Rm = Tile(ea.enter_context(nc.sbuf_tensor("s_Rm%d" % l, [128, 128], F32)))
            lamb = Tile(ea.enter_context(nc.sbuf_tensor("s_lamb%d" % l, [128, 4, 32], F32)))
            lamp = Tile(ea.enter_context(nc.sbuf_tensor("s_lamp%d" % l, [128, 2, 32], F32)))
            lam2 = Tile(ea.enter_context(nc.sbuf_tensor("s_lam2%d" % l, [128, 2], F32)))
            lam = Tile(ea.enter_context(nc.sbuf_tensor("s_lam%d" % l, [128, 1], F32)))
            gnb = Tile(ea.enter_context(nc.sbuf_tensor("s_gnb%d" % l, [128, 64], F32)))
            QT = Ring(ea, nc, "QT", 2, [128, 4, G], BF16)
            ld = Ring(ea, nc, "ld", 2, [128, G], F32)
            t1 = Ring(ea, nc, "t1", 1, [128, G], F32)
            t2 = Ring(ea, nc, "t2", 1, [128, G], F32)
            PT = Ring(ea, nc, "PT", 3, [128, G], BF16)
            osb = Ring(ea, nc, "osb", 1, [128, 4, 8, 64], F32)
            osq = Ring(ea, nc, "osq", 1, [128, 4, 8, 64], F32)
            rr = Ring(ea, nc, "rr", 4, [128, 2, 4], F32)
            tt = Ring(ea, nc, "tt", 4, [128, 4, 64], F32)
            ssum = Ring(ea, nc, "ssum", 2, [128, 32], F32)
            ckl = Ring(ea, nc, "ckl", 2, [128, 512], F32)
            stg = Ring(ea, nc, "astg", 2, [128, G], F32)
            ident = self.ident
            ob = Buf("oT_d")
            accr = RingView(self.ps.tiles[0:4])
            QM = Ring(ea, nc, "QM", 1, [128, 4, 4, G], BF16)
            rmask = Tile(ea.enter_context(nc.sbuf_tensor("s_rmask%d" % l, [128, 4], F32)))
            S.dma("sync", rmask.t[:], self.rowmask.ap(), writes=[rmask.b])
            scr = RingView(self.ps.tiles[4:6] + self.psr.tiles)

            S.dma("sync", cosT.t[:], self.ropec.ap(), writes=[cosT.b])
            S.dma("sync", sinT.t[:], self.ropes.ap(), writes=[sinT.b])
            S.dma("sync", Rm.t[:], self.roper.ap(), writes=[Rm.b])
            S.dma("sync", lamb.t[:], dap(self.diff_lambda, l * 128, [[0, 128], [32, 4], [1, 32]]), writes=[lamb.b])
            S.dma("sync", gnb.t[:], dap(self.diff_norm_g, l * 64, [[0, 128], [1, 64]]), writes=[gnb.b])
            S.op("vector", lambda E: E.tensor_tensor(
                out=lamp.t[:], in0=lamb.t[:].rearrange("p (a b) d -> p a b d", b=2)[:, :, 0, :],
                in1=lamb.t[:].rearrange("p (a b) d -> p a b d", b=2)[:, :, 1, :], op=ALU.mult),
                reads=[lamb.b], writes=[lamp.b])
            S.op("vector", lambda E: E.tensor_reduce(out=lam2.t[:], in_=lamp.t[:], axis=mybir.AxisListType.X, op=ALU.add),
                 reads=[lamp.b], writes=[lam2.b])
            S.op("scalar", lambda E: E.activation(out=lam2.t[:], in_=lam2.t[:], func=AF.Exp), reads=[lam2.b], writes=[lam2.b])
            S.op("vector", lambda E: E.tensor_tensor(out=lam.t[:], in0=lam2.t[:, 0:1], in1=lam2.t[:, 1:2], op=ALU.subtract),
                 reads=[lam2.b], writes=[lam.b])
            S.op("vector", lambda E: E.tensor_scalar(out=lam.t[:], in0=lam.t[:], scalar1=lam_init, scalar2=None, op0=ALU.add),
                 reads=[lam.b], writes=[lam.b])
            S.op("vector", lambda E: E.tensor_scalar(out=gnb.t[:], in0=gnb.t[:], scalar1=1.0 - lam_init, scalar2=None,
                                                     op0=ALU.mult), reads=[gnb.b], writes=[gnb.b])
            S.op("gpsimd", lambda E: E.memset(VA.t[:, :, :, 64:65], 1.0), writes=[VA.b])

            def load_rot(row0, tok0, pos0, W, rope, dst_ap, dst_b):
                x = ld.next()
                S.dma("sync", x.t[:, 0:W], pT_d.ap()[row0:row0 + 128, tok0:tok0 + W], writes=[x.b])
                if not rope:
                    S.op("gpsimd", lambda E: E.tensor_copy(out=dst_ap, in_=x.t[:, 0:W]), reads=[x.b], writes=[dst_b])
                    return
                p = scr.next()
                S.op("tensor", lambda E: E.matmul(p.t[:, 0:W], Rm.t[:], x.t[:, 0:W], start=True, stop=True),
                     reads=[x.b, Rm.b], writes=[p.b])
                a = t1.next()
                b = t2.next()
                S.op("gpsimd", lambda E: E.tensor_tensor(out=a.t[:, 0:W], in0=x.t[:, 0:W], in1=cosT.t[:, pos0:pos0 + W],
                                                         op=ALU.mult), reads=[x.b, cosT.b], writes=[a.b])
                S.op("vector", lambda E: E.tensor_tensor(out=b.t[:, 0:W], in0=p.t[:, 0:W], in1=sinT.t[:, pos0:pos0 + W],
                                                         op=ALU.mult), reads=[p.b, sinT.b], writes=[b.b])
                S.op("vector", lambda E: E.tensor_tensor(out=dst_ap, in0=a.t[:, 0:W], in1=b.t[:, 0:W], op=ALU.add),
                     reads=[a.b, b.b], writes=[dst_b])

            seqs = [(0, LS, True)] + [(LS + i * LP, LP, False) for i in range(NPR)]
            def do_seq(tok0, Lq, samp):
                W = min(G, Lq)
                nkt = Lq // 128 + (4 if samp else 0)
                for kg in range(Lq // W):
                    for ch in range(4):
                        load_rot(KR + ch * 128, tok0 + kg * W, kg * W, W, samp,
                                 KT.t[:, ch, kg * W:(kg + 1) * W], KT.b)
                    for ch in range(4):
                        x = ld.next()
                        S.dma("sync", x.t[:, 0:W], pT_d.ap()[VR + ch * 128:VR + (ch + 1) * 128,
                                                            tok0 + kg * W:tok0 + (kg + 1) * W], writes=[x.b])
                        p = scr.next()
                        for s4 in range(W // 128):
                            S.op("tensor", lambda E, p=p, x=x, s4=s4: E.transpose(
                                out=p.t[:, s4 * 128:(s4 + 1) * 128], in_=x.t[:, s4 * 128:(s4 + 1) * 128],
                                identity=ident.t[:]), reads=[x.b, ident.b], writes=[p.b])
                        for s4 in range(W // 128):
                            kt = kg * (W // 128) + s4
                            self.evac(s4, VA.t[:, kt, 2 * ch:2 * ch + 2, 0:64],
                                      p.t[:, s4 * 128:(s4 + 1) * 128].rearrange("p (a b) -> p a b", a=2),
                                      [p.b], [VA.b])
                if samp:
                    for k4 in range(4):
                        ck = ckl.next()
                        S.dma("sync", ck.t[:], dap(self.cache_k, (l * 512 + k4 * 128) * 512, [[512, 128], [1, 512]]),
                              writes=[ck.b])
                        p = scr.next()
                        for ch in range(4):
                            S.op("tensor", lambda E, p=p, ck=ck, ch=ch: E.transpose(
                                out=p.t[:, ch * 128:(ch + 1) * 128], in_=ck.t[:, ch * 128:(ch + 1) * 128],
                                identity=ident.t[:]), reads=[ck.b, ident.b], writes=[p.b])
                        self.evac(k4, KT.t[:, :, LS + k4 * 128:LS + (k4 + 1) * 128],
                                  p.t[:, :].rearrange("p (a b) -> p a b", a=4), [p.b], [KT.b])
                        S.dma("gpsimd", VA.t[:, LS // 128 + k4, :, 0:64],
                              dap(self.cache_v, (l * 512 + k4 * 128) * 512, [[512, 128], [64, 8], [1, 64]]),
                              writes=[VA.b])
                def do_qg(qg):
                    q = QT.next()
                    for ch in range(4):
                        load_rot(QR + ch * 128, tok0 + qg * W, qg * W, W, samp, q.t[:, ch, 0:W], q.b)
                    qm = QM.next()
                    for ch in range(4):
                        for grp in range(4):
                            S.op("gpsimd" if (ch + grp) % 2 else "vector", lambda E, qm=qm, q=q, ch=ch, grp=grp: E.tensor_scalar(
                                out=qm.t[:, ch, grp, 0:W], in0=q.t[:, ch, 0:W], scalar1=rmask.t[:, grp:grp + 1], scalar2=None,
                                op0=ALU.mult), reads=[q.b, rmask.b], writes=[qm.b])
                    o = osb.next()
                    ns = W // 128
                    def do_head(h):
                        acc = [accr.next(), accr.next()]
                        ch = h // 2
                        for c in range(2):
                            r0 = ((h % 2) * 2 + c) * 32
                            for kt in range(nkt):
                                kcol = kt * 128 if kt < Lq // 128 else LS + (kt - Lq // 128) * 128
                                pS = scr.next()
                                S.op("tensor", lambda E, pS=pS, qm=qm, r0=r0, ch=ch, kcol=kcol: E.matmul(
                                    pS.t[:, 0:W], KT.t[:, ch, kcol:kcol + 128], qm.t[:, ch, r0 // 32, 0:W],
                                    start=True, stop=True),
                                    reads=[KT.b, qm.b], writes=[pS.b])
                                pt = PT.next()
                                S.op("scalar", lambda E, pS=pS, pt=pt: E.activation(
                                    out=pt.t[:, 0:W], in_=pS.t[:, 0:W], func=AF.Exp, scale=32.0 ** -0.5),
                                    reads=[pS.b], writes=[pt.b])
                                vkt = kt if kt < Lq // 128 else LS // 128 + (kt - Lq // 128)
                                for s4 in range(ns):
                                    S.op("tensor", lambda E, a=acc[c], pt=pt, s4=s4, vkt=vkt, h=h, kt=kt: E.matmul(
                                        a.t[:, s4 * 65:(s4 + 1) * 65], pt.t[:, s4 * 128:(s4 + 1) * 128], VA.t[:, vkt, h, :],
                                        start=(kt == 0 and s4 == 0), stop=(kt == nkt - 1)),
                                        reads=[pt.b, VA.b], writes=[acc[c].b])
                        r = rr.next()
                        for c in range(2):
                            S.op("vector", lambda E, r=r, c=c, a=acc[c]: E.reciprocal(
                                out=r.t[:, c, 0:ns], in_=a.t[:, 0:ns * 65].rearrange("p (s e) -> p s e", e=65)[:, :, 64]),
                                reads=[acc[c].b], writes=[r.b])
                        S.op("vector", lambda E, r=r: E.tensor_scalar(out=r.t[:, 1, 0:ns], in0=r.t[:, 1, 0:ns],
                                                                      scalar1=lam.t[:, 0:1], scalar2=None, op0=ALU.mult),
                             reads=[r.b, lam.b], writes=[r.b])
                        ta = tt.next()
                        tb = tt.next()
                        a0v = acc[0].t[:, 0:ns * 65].rearrange("p (s e) -> p s e", e=65)[:, :, 0:64]
                        a1v = acc[1].t[:, 0:ns * 65].rearrange("p (s e) -> p s e", e=65)[:, :, 0:64]
                        S.op("vector", lambda E, ta=ta, r=r, a0v=a0v: E.tensor_tensor(
                            out=ta.t[:, 0:ns, :], in0=a0v, in1=r.t[:, 0, 0:ns].unsqueeze(2).broadcast_to([128, ns, 64]),
                            op=ALU.mult), reads=[acc[0].b, r.b], writes=[ta.b])
                        S.op("vector", lambda E, tb=tb, r=r, a1v=a1v: E.tensor_tensor(
                            out=tb.t[:, 0:ns, :], in0=a1v, in1=r.t[:, 1, 0:ns].unsqueeze(2).broadcast_to([128, ns, 64]),
                            op=ALU.mult), reads=[acc[1].b, r.b], writes=[tb.b])
                        S.op("gpsimd", lambda E, ta=ta, tb=tb, o=o, h=h: E.tensor_tensor(
                            out=o.t[:, 0:ns, h, :], in0=ta.t[:, 0:ns, :], in1=tb.t[:, 0:ns, :], op=ALU.subtract),
                            reads=[ta.b, tb.b], writes=[o.b])
                    for h in range(8):
                        do_head(h)
                    dbgon = self.debug and l == 0 and tok0 == LS and qg == 0
                    if dbgon:
                        d1 = self.dout("dbg_o_pre", [128, 4 * 8 * 64])
                        d2 = self.dout("dbg_sm", [128, 32])
                        d3 = self.dout("dbg_o_post", [128, 4 * 8 * 64])
                        d4 = self.dout("dbg_sm2", [128, 32])
                        S.dma("sync", d1.ap(), o.t[:].rearrange("p a b c -> p (a b c)"), reads=[o.b], writes=[Buf()])
                    oq = osq.next()
                    S.op("gpsimd", lambda E, oq=oq, o=o: E.tensor_tensor(out=oq.t[:, 0:ns], in0=o.t[:, 0:ns], in1=o.t[:, 0:ns],
                                                                         op=ALU.mult), reads=[o.b], writes=[oq.b])
                    sm = ssum.next()
                    S.op("vector", lambda E, sm=sm, oq=oq: E.tensor_reduce(
                        out=sm.t[:, 0:ns * 8], in_=oq.t[:, 0:ns].rearrange("p s h d -> p (s h) d"),
                        axis=mybir.AxisListType.X, op=ALU.add), reads=[oq.b], writes=[sm.b])
                    if dbgon:
                        S.dma("sync", d2.ap(), sm.t[:], reads=[sm.b], writes=[Buf()])
                    S.op("scalar", lambda E, sm=sm: E.activation(out=sm.t[:, 0:ns * 8], in_=sm.t[:, 0:ns * 8], func=AF.Sqrt,
                                                                 scale=1.0 / 64, bias=EPS), reads=[sm.b], writes=[sm.b])
                    S.op("vector", lambda E, sm=sm: E.reciprocal(out=sm.t[:, 0:ns * 8], in_=sm.t[:, 0:ns * 8]),
                         reads=[sm.b], writes=[sm.b])
                    S.op("vector", lambda E, sm=sm, o=o: E.tensor_tensor(
                        out=o.t[:, 0:ns].rearrange("p s h d -> p (s h) d"),
                        in0=o.t[:, 0:ns].rearrange("p s h d -> p (s h) d"),
                        in1=sm.t[:, 0:ns * 8].unsqueeze(2).broadcast_to([128, ns * 8, 64]), op=ALU.mult),
                        reads=[o.b, sm.b], writes=[o.b])
                    S.op("gpsimd", lambda E, o=o: E.tensor_tensor(
                        out=o.t[:, 0:ns].rearrange("p s h d -> p (s h) d"),
                        in0=o.t[:, 0:ns].rearrange("p s h d -> p (s h) d"),
                        in1=gnb.t[:].unsqueeze(1).broadcast_to([128, ns * 8, 64]), op=ALU.mult),
                        reads=[o.b, gnb.b], writes=[o.b])
                    if dbgon:
                        S.dma("sync", d4.ap(), sm.t[:], reads=[sm.b], writes=[Buf()])
                        S.dma("sync", d3.ap(), o.t[:].rearrange("p a b c -> p (a b c)"), reads=[o.b], writes=[Buf()])
                    for hp in range(4):
                        p = scr.next()
                        for s4 in range(ns):
                            S.op("tensor", lambda E, p=p, o=o, s4=s4, hp=hp: E.transpose(
                                out=p.t[:, s4 * 128:(s4 + 1) * 128],
                                in_=o.t[:, s4, 2 * hp:2 * hp + 2, :].rearrange("p h d -> p (h d)"),
                                identity=ident.t[:]), reads=[o.b, ident.b], writes=[p.b])
                        st = stg.next()
                        self.evac(hp, st.t[:, 0:W], p.t[:, 0:W], [p.b], [st.b])
                        r0 = 512 + hp * 128
                        S.dma("sync", oT_d.ap()[r0:r0 + 128, tok0 + qg * W:tok0 + (qg + 1) * W], st.t[:, 0:W],
                              reads=[st.b], writes=[ob])
                for qg in range(Lq // W):
                    do_qg(qg)
            for (tok0, Lq, samp) in seqs:
                do_seq(tok0, Lq, samp)

    def gla_phase(self, l, pT_d, oT_d):
        nc, S = self.nc, self.S
        ident = self.ident
        with contextlib.ExitStack() as ea:
            def sb(name, shape, dt=F32):
                return Tile(ea.enter_context(nc.sbuf_tensor("s_g%s%d" % (name, l), shape, dt)))
            TRI = sb("tri", [128, 6, 128])
            GW = sb("gw", [128, 512])
            gng = sb("gng", [128, 64])
            bmask = sb("bmask", [128, 256])
            rmask = sb("rmask", [128, 4])
            OF = sb("OF", [128, LS // 128, 512])
            S32 = [[sb("S32_%d%d" % (d, hc), [128, 64]) for hc in range(2)] for d in range(2)]
            Sbf = [[sb("Sbf_%d%d" % (d, hc), [128, 64], BF16) for hc in range(2)] for d in range(2)]
            grT = Ring(ea, nc, "grT", 2, [128, 128], F32)
            qk = Ring(ea, nc, "gqk", 2, [128, 4, 128], F32)
            vT = Ring(ea, nc, "gvT", 2, [128, 4, 128], F32)
            ktok = Ring(ea, nc, "gktok", 2, [128, 256], F32)
            vtok = Ring(ea, nc, "gvtok", 2, [128, 512], BF16)
            ex = Ring(ea, nc, "gex", 2, [128, 256], F32)
            la = Ring(ea, nc, "gla", 2, [128, 256], F32)
            E1 = Ring(ea, nc, "gE1", 4, [128, 128], F32)
            E2 = Ring(ea, nc, "gE2", 2, [128, 128], F32)
            qd = Ring(ea, nc, "gqd", 2, [128, 128], F32)
            qm = Ring(ea, nc, "gqm", 4, [128, 4, 128], BF16)
            kd = Ring(ea, nc, "gkd", 4, [128, 128], BF16)
            ek = Ring(ea, nc, "gek", 2, [128, 256], F32)
            khat = Ring(ea, nc, "gkhat", 2, [128, 256], BF16)
            at = Ring(ea, nc, "gat", 3, [128, 128], BF16)
            km = Ring(ea, nc, "gkm", 2, [128, 256], F32)
            kvd = Ring(ea, nc, "gkvd", 2, [128, 64], F32)
            osum = Ring(ea, nc, "gosum", 2, [128, 512], F32)
            osq = Ring(ea, nc, "gosq", 1, [128, 512], F32)
            ss = Ring(ea, nc, "gss", 2, [128, 8], F32)
            ggT = Ring(ea, nc, "gggT", 2, [128, 4, 128], F32)
            res = Ring(ea, nc, "gres", 2, [128, 4, 128], F32)
            pA = RingView(self.ps.tiles[0:3])
            pO = RingView(self.ps.tiles[3:5])
            pX = RingView([self.ps.tiles[5]] + self.psr.tiles)
            ob = Buf("oT_d")

            S.dma("sync", TRI.t[:], dap(self.gla_tri, 0, [[128, 128], [128 * 128, 6], [1, 128]]), writes=[TRI.b])
            S.dma("sync", bmask.t[:], self.gla_bmask.ap(), writes=[bmask.b])
            S.dma("sync", rmask.t[:], self.rowmask.ap(), writes=[rmask.b])
            S.dma("sync", gng.t[:], dap(self.gla_norm_g, l * 64, [[0, 128], [1, 64]]), writes=[gng.b])
            S.op("vector", lambda E: E.memset(GW.t[:], 0.0), writes=[GW.b])
            for d in range(2):
                S.dma("sync", GW.t[d * 16:(d + 1) * 16, d * 256:(d + 1) * 256],
                      dap(self.gla_gate_w, (l * 2 + d) * 16 * 256, [[256, 16], [1, 256]]), writes=[GW.b])
            S.dma("sync", GW.t[32:33, :], dap(self.gla_gate_b, l * 512, [[512, 1], [1, 512]]), writes=[GW.b])
            for g_ in grT.tiles:
                S.op("vector", lambda E, g_=g_: E.memset(g_.t[32:33, :], 1.0), writes=[g_.b])

            def do_tile(tok0, tt, d, samp):
                t0 = tok0 + tt * 128
                x = qk.next()
                S.dma("sync", x.t[:], dap(pT_d, t0, [[T, 128], [128 * T, 4], [1, 128]]), writes=[x.b])
                v = vT.next()
                S.dma("sync", v.t[:], dap(pT_d, 512 * T + t0, [[T, 128], [128 * T, 4], [1, 128]]), writes=[v.b])
                g = grT.next()
                S.dma("sync", g.t[0:32, :], dap(pT_d, 1536 * T + t0, [[T, 32], [1, 128]]), writes=[g.b])
                p = pX.next()
                for c in range(2):
                    S.op("tensor", lambda E, c=c: E.transpose(out=p.t[:, c * 128:(c + 1) * 128], in_=x.t[:, 2 + c, :],
                                                               identity=ident.t[:]), reads=[x.b, ident.b], writes=[p.b])
                kt_ = ktok.next()
                S.op("vector", lambda E: E.tensor_copy(out=kt_.t[:], in_=p.t[:, 0:256]), reads=[p.b], writes=[kt_.b])
                p2 = pX.next()
                for c in range(4):
                    S.op("tensor", lambda E, c=c: E.transpose(out=p2.t[:, c * 128:(c + 1) * 128], in_=v.t[:, c, :],
                                                               identity=ident.t[:]), reads=[v.b, ident.b], writes=[p2.b])
                vt_ = vtok.next()
                S.op("scalar", lambda E: E.activation(out=vt_.t[:], in_=p2.t[:, :], func=AF.Copy), reads=[p2.b], writes=[vt_.b])
                pl = pX.next()
                S.op("tensor", lambda E: E.matmul(pl.t[:, 0:256], g.t[0:33, :], GW.t[0:33, d * 256:(d + 1) * 256],
                                                  start=True, stop=True), reads=[g.b, GW.b], writes=[pl.b])
                e_ = ex.next()
                S.op("scalar", lambda E: E.activation(out=e_.t[:], in_=pl.t[:, 0:256], func=AF.Exp, scale=-1.0),
                     reads=[pl.b], writes=[e_.b])
                la_ = la.next()
                S.op("scalar", lambda E: E.activation(out=la_.t[:], in_=e_.t[:], func=AF.Ln, bias=1.0),
                     reads=[e_.b], writes=[la_.b])
                qms, kds, e1s = [], [], []
                for hc in range(2):
                    pb = pX.next()
                    S.op("tensor", lambda E, hc=hc, pb=pb: E.matmul(pb.t[:, 0:128], la_.t[:, hc * 128:(hc + 1) * 128],
                                                                    TRI.t[:, d, :], start=True, stop=True),
                         reads=[la_.b, TRI.b], writes=[pb.b])
                    e1 = E1.next()
                    e2 = E2.next()
                    S.op("scalar", lambda E, e1=e1, pb=pb: E.activation(out=e1.t[:], in_=pb.t[:, 0:128], func=AF.Exp),
                         reads=[pb.b], writes=[e1.b])
                    S.op("scalar", lambda E, e2=e2, pb=pb: E.activation(out=e2.t[:], in_=pb.t[:, 0:128], func=AF.Exp, scale=-1.0),
                         reads=[pb.b], writes=[e2.b])
                    q_ = qd.next()
                    S.op("vector", lambda E, q_=q_, e1=e1, hc=hc: E.scalar_tensor_tensor(
                        out=q_.t[:], in0=x.t[:, hc, :], scalar=32.0 ** -0.5, in1=e1.t[:], op0=ALU.mult, op1=ALU.mult),
                        reads=[x.b, e1.b], writes=[q_.b])
                    m_ = qm.next()
                    for grp in range(4):
                        S.op("gpsimd" if grp % 2 else "vector", lambda E, m_=m_, q_=q_, grp=grp: E.tensor_scalar(
                            out=m_.t[:, grp, :], in0=q_.t[:], scalar1=rmask.t[:, grp:grp + 1], scalar2=None, op0=ALU.mult),
                            reads=[q_.b, rmask.b], writes=[m_.b])
                    k_ = kd.next()
                    S.op("gpsimd", lambda E, k_=k_, e2=e2, hc=hc: E.tensor_tensor(out=k_.t[:], in0=x.t[:, 2 + hc, :], in1=e2.t[:],
                                                                                  op=ALU.mult), reads=[x.b, e2.b], writes=[k_.b])
                    qms.append(m_)
                    kds.append(k_)
                    e1s.append(e1)
                pk = pX.next()
                S.op("tensor", lambda E: E.matmul(pk.t[:, 0:256], TRI.t[:, 2 + d, :], la_.t[:], start=True, stop=True),
                     reads=[la_.b, TRI.b], writes=[pk.b])
                ek_ = ek.next()
                S.op("scalar", lambda E: E.activation(out=ek_.t[:], in_=pk.t[:, 0:256], func=AF.Exp), reads=[pk.b], writes=[ek_.b])
                kh = khat.next()
                S.op("vector", lambda E: E.tensor_tensor(out=kh.t[:], in0=kt_.t[:], in1=ek_.t[:], op=ALU.mult),
                     reads=[kt_.b, ek_.b], writes=[kh.b])
                po = pO.next()
                for h in range(8):
                    hc, grp = h // 4, h % 4
                    pa = pA.next()
                    S.op("tensor", lambda E, pa=pa, hc=hc, grp=grp: E.matmul(
                        pa.t[:, 0:128], kds[hc].t[:], qms[hc].t[:, grp, :], start=True, stop=True),
                        reads=[kds[hc].b, qms[hc].b], writes=[pa.b])
                    a_ = at.next()
                    S.op("vector", lambda E, pa=pa, a_=a_: E.tensor_tensor(out=a_.t[:], in0=pa.t[:, 0:128], in1=TRI.t[:, 4 + d, :],
                                                                           op=ALU.mult), reads=[pa.b, TRI.b], writes=[a_.b])
                    S.op("tensor", lambda E, a_=a_, h=h: E.matmul(
                        po.t[:, h * 64:(h + 1) * 64], a_.t[:], vt_.t[:, h * 64:(h + 1) * 64], start=(h == 0), stop=False),
                        reads=[a_.b, vt_.b], writes=[po.b])
                    S.op("tensor", lambda E, h=h, hc=hc, grp=grp: E.matmul(
                        po.t[:, h * 64:(h + 1) * 64], qms[hc].t[:, grp, :], Sbf[d][hc].t[:], start=False, stop=True),
                        reads=[qms[hc].b, Sbf[d][hc].b], writes=[po.b])
                if d == 0:
                    S.op("scalar", lambda E: E.activation(out=OF.t[:, tt, :], in_=po.t[:, :], func=AF.Copy),
                         reads=[po.b], writes=[OF.b])
                for hc in range(2):
                    ps_ = pX.next()
                    S.op("tensor", lambda E, ps_=ps_, hc=hc: E.matmul(
                        ps_.t[:, 0:256], kh.t[:, hc * 128:(hc + 1) * 128], vt_.t[:, hc * 256:(hc + 1) * 256],
                        start=True, stop=True), reads=[kh.b, vt_.b], writes=[ps_.b])
                    km_ = km.next()
                    S.op("vector", lambda E, ps_=ps_, km_=km_: E.tensor_tensor(out=km_.t[:], in0=ps_.t[:, 0:256], in1=bmask.t[:],
                                                                               op=ALU.mult), reads=[ps_.b, bmask.b], writes=[km_.b])
                    kv_ = kvd.next()
                    S.op("vector", lambda E, km_=km_, kv_=kv_: E.tensor_reduce(
                        out=kv_.t[:], in_=km_.t[:].rearrange("p (h d) -> p d h", h=4), axis=mybir.AxisListType.X, op=ALU.add),
                        reads=[km_.b], writes=[kv_.b])
                    col = 127 if d == 0 else 0
                    S.op("vector", lambda E, kv_=kv_, hc=hc, col=col: E.scalar_tensor_tensor(
                        out=S32[d][hc].t[:], in0=S32[d][hc].t[:], scalar=e1s[hc].t[:, col:col + 1], in1=kv_.t[:],
                        op0=ALU.mult, op1=ALU.add), reads=[S32[d][hc].b, e1s[hc].b, kv_.b], writes=[S32[d][hc].b])
                    S.op("gpsimd", lambda E, hc=hc: E.tensor_copy(out=Sbf[d][hc].t[:], in_=S32[d][hc].t[:]),
                         reads=[S32[d][hc].b], writes=[Sbf[d][hc].b])
                if d == 1:
                    o_ = osum.next()
                    S.op("vector", lambda E: E.tensor_tensor(out=o_.t[:], in0=po.t[:, :], in1=OF.t[:, tt, :], op=ALU.add),
                         reads=[po.b, OF.b], writes=[o_.b])
                    q2 = osq.next()
                    S.op("gpsimd", lambda E: E.tensor_tensor(out=q2.t[:], in0=o_.t[:], in1=o_.t[:], op=ALU.mult),
                         reads=[o_.b], writes=[q2.b])
                    s_ = ss.next()
                    S.op("vector", lambda E: E.tensor_reduce(out=s_.t[:], in_=q2.t[:].rearrange("p (h d) -> p h d", h=8),
                                                             axis=mybir.AxisListType.X, op=ALU.add), reads=[q2.b], writes=[s_.b])
                    S.op("scalar", lambda E: E.activation(out=s_.t[:], in_=s_.t[:], func=AF.Sqrt, scale=1.0 / 64, bias=EPS),
                         reads=[s_.b], writes=[s_.b])
                    S.op("vector", lambda E: E.reciprocal(out=s_.t[:], in_=s_.t[:]), reads=[s_.b], writes=[s_.b])
                    S.op("vector", lambda E: E.tensor_tensor(
                        out=o_.t[:].rearrange("p (h d) -> p h d", h=8), in0=o_.t[:].rearrange("p (h d) -> p h d", h=8),
                        in1=s_.t[:].unsqueeze(2).broadcast_to([128, 8, 64]), op=ALU.mult), reads=[o_.b, s_.b], writes=[o_.b])
                    S.op("gpsimd", lambda E: E.tensor_tensor(
                        out=o_.t[:].rearrange("p (h d) -> p h d", h=8), in0=o_.t[:].rearrange("p (h d) -> p h d", h=8),
                        in1=gng.t[:].unsqueeze(1).broadcast_to([128, 8, 64]), op=ALU.mult), reads=[o_.b, gng.b], writes=[o_.b])
                    gg_ = ggT.next()
                    S.dma("sync", gg_.t[:], dap(pT_d, 1024 * T + t0, [[T, 128], [128 * T, 4], [1, 128]]), writes=[gg_.b])
                    S.op("scalar", lambda E: E.activation(out=gg_.t[:], in_=gg_.t[:], func=AF.Silu), reads=[gg_.b], writes=[gg_.b])
                    pt_ = pX.next()
                    for c in range(4):
                        S.op("tensor", lambda E, c=c: E.transpose(out=pt_.t[:, c * 128:(c + 1) * 128],
                                                                   in_=o_.t[:, c * 128:(c + 1) * 128], identity=ident.t[:]),
                             reads=[o_.b, ident.b], writes=[pt_.b])
                    r_ = res.next()
                    S.op("vector", lambda E: E.tensor_tensor(out=r_.t[:], in0=pt_.t[:, :].rearrange("p (c t) -> p c t", c=4),
                                                             in1=gg_.t[:], op=ALU.mult), reads=[pt_.b, gg_.b], writes=[r_.b])
                    S.dma("sync", dap(oT_d, t0, [[T, 128], [128 * T, 4], [1, 128]]), r_.t[:], reads=[r_.b], writes=[ob])

            seqs = [(0, LS, True, -1)] + [(LS + i * LP, LP, False, i) for i in range(NPR)]
            for (tok0, Lq, samp, pi) in seqs:
                for d in range(2):
                    for hc in range(2):
                        if samp:
                            S.dma("sync", S32[d][hc].t[:],
                                  dap(self.state_gla, ((l * 2 + d) * 8 + hc * 4) * 32 * 64, [[64, 128], [1, 64]]),
                                  writes=[S32[d][hc].b])
                        else:
                            S.op("vector", lambda E, d=d, hc=hc: E.memset(S32[d][hc].t[:], 0.0), writes=[S32[d][hc].b])
                        S.op("gpsimd", lambda E, d=d, hc=hc: E.tensor_copy(out=Sbf[d][hc].t[:], in_=S32[d][hc].t[:]),
                             reads=[S32[d][hc].b], writes=[Sbf[d][hc].b])
                nt = Lq // 128
                for d in range(2):
                    for tt in (range(nt) if d == 0 else range(nt - 1, -1, -1)):
                        do_tile(tok0, tt, d, samp)
                if not samp:
                    for d in range(2):
                        for hc in range(2):
                            S.dma("sync", dap(self.ngla_d, (((pi * DEPTH + l) * 2 + d) * 8 + hc * 4) * 32 * 64, [[64, 128], [1, 64]]),
                                  S32[d][hc].t[:], reads=[S32[d][hc].b], writes=[Buf()])

    def ssd_phase(self, l, pT_d, oT_d):
        nc, S = self.nc, self.S
        ident = self.ident
        ZR, XR, DR = 3104, 3616, 4384
        with contextlib.ExitStack() as ea:
            def sb(name, shape, dt=F32):
                return Tile(ea.enter_context(nc.sbuf_tensor("s_s%s%d" % (name, l), shape, dt)))
            TRI = sb("tri", [128, 6, 128])
            ones = sb("ones", [128, 128])
            cw = sb("cw", [128, 6, 3])
            cb = sb("cb", [128, 6])
            gmask = sb("gmask", [128, 2])
            dtb = sb("dtb", [128, 16])
            abc = sb("abc", [128, 16])
            Dbc = sb("Dbc", [128, 8])
            ngb = sb("ngb", [128, 512])
            YF = sb("YF", [128, LS // 128, 512])
            ST32 = [[sb("ST32_%d%d" % (d, h), [128, 64]) for h in range(8)] for d in range(2)]
            STbf = [[sb("STbf_%d%d" % (d, h), [128, 64], BF16) for h in range(8)] for d in range(2)]
            xin = Ring(ea, nc, "sxin", 2, [128, 6, 130], F32)
            cv = Ring(ea, nc, "scv", 2, [128, 6, 128], F32)
            BTb = Ring(ea, nc, "sBT", 2, [128, 128], BF16)
            CTm = Ring(ea, nc, "sCTm", 2, [128, 2, 128], F32)
            xtok = Ring(ea, nc, "sxtok", 2, [128, 512], F32)
            xw = Ring(ea, nc, "sxw", 2, [128, 512], BF16)
            xtb = Ring(ea, nc, "sxtb", 2, [128, 512], BF16)
            Btok = Ring(ea, nc, "sBtok", 2, [128, 128], BF16)
            dtT = Ring(ea, nc, "sdtT", 2, [128, 128], F32)
            dt_ = Ring(ea, nc, "sdt", 2, [128, 16], F32)
            gg = Ring(ea, nc, "sg", 2, [128, 16], F32)
            cumc = Ring(ea, nc, "scum", 2, [128, 8], F32)
            ww = Ring(ea, nc, "sw", 2, [128, 8], F32)
            etot = Ring(ea, nc, "setot", 2, [128, 8], F32)
            gb = Ring(ea, nc, "sgb", 2, [128, 128], F32)
            Dm = Ring(ea, nc, "sDm", 2, [128, 128], F32)
            Eh = Ring(ea, nc, "sEh", 2, [128, 128], F32)
            Cd = Ring(ea, nc, "sCd", 2, [128, 128], BF16)
            at = Ring(ea, nc, "sat", 3, [128, 128], BF16)
            ysum = Ring(ea, nc, "sysum", 2, [128, 512], F32)
            ztok = Ring(ea, nc, "sztok", 2, [128, 512], F32)
            zT = Ring(ea, nc, "szT", 2, [128, 4, 128], F32)
            ysq = Ring(ea, nc, "sysq", 1, [128, 512], F32)
            s1 = Ring(ea, nc, "ss1", 2, [128, 1], F32)
            res = Ring(ea, nc, "sres", 2, [128, 4, 128], F32)
            sto = Ring(ea, nc, "ssto", 4, [64, 64], F32)
            pA = RingView(self.ps.tiles[0:3])
            pO = RingView(self.ps.tiles[3:5])
            pX = RingView([self.ps.tiles[5]] + self.psr.tiles)
            ob = Buf("oT_d")

            S.dma("sync", TRI.t[:], dap(self.ssd_tri, 0, [[128, 128], [128 * 128, 6], [1, 128]]), writes=[TRI.b])
            for t_ in dtT.tiles:
                S.op("vector", lambda E, t_=t_: E.memset(t_.t[:], 0.0), writes=[t_.b])
            S.op("vector", lambda E: E.memset(ones.t[:], 1.0), writes=[ones.b])
            S.dma("sync", gmask.t[:], self.ssd_gmask.ap(), writes=[gmask.b])
            for k in range(3):
                S.dma("sync", cw.t[:, :, k], dap(self.ssd_conv_w, (l * 3 + k) * 768, [[1, 128], [128, 6]]), writes=[cw.b], **NCD)
            S.dma("sync", cb.t[:], dap(self.ssd_conv_b, l * 768, [[1, 128], [128, 6]]), writes=[cb.b], **NCD)
            S.dma("sync", dtb.t[:], dap(self.ssd_dt_bias, l * 16, [[0, 128], [1, 16]]), writes=[dtb.b])
            S.dma("sync", abc.t[:], dap(self.ssd_a_log, l * 16, [[0, 128], [1, 16]]), writes=[abc.b])
            S.dma("sync", Dbc.t[:], dap(self.ssd_d, l * 8, [[0, 128], [1, 8]]), writes=[Dbc.b])
            S.dma("sync", ngb.t[:], dap(self.ssd_norm_g, l * 512, [[0, 128], [1, 512]]), writes=[ngb.b])
            S.op("scalar", lambda E: E.activation(out=abc.t[:], in_=abc.t[:], func=AF.Exp), reads=[abc.b], writes=[abc.b])
            S.op("vector", lambda E: E.tensor_scalar(out=abc.t[:], in0=abc.t[:], scalar1=-1.0, scalar2=None, op0=ALU.mult),
                 reads=[abc.b], writes=[abc.b])

            def do_tile(tok0, Lq, tt, d):
                t0 = tok0 + tt * 128
                nt = Lq // 128
                lo = 0 if tt > 0 else 1
                hi = 130 if tt < nt - 1 else 129
                x = xin.next()
                if lo == 1:
                    S.op("vector", lambda E: E.memset(x.t[:, :, 0:1], 0.0), writes=[x.b])
                if hi == 129:
                    S.op("vector", lambda E: E.memset(x.t[:, :, 129:130], 0.0), writes=[x.b])
                S.dma("sync", x.t[:, :, lo:hi], dap(pT_d, XR * T + t0 - 1 + lo, [[T, 128], [128 * T, 6], [1, hi - lo]]),
                      writes=[x.b])
                c_ = cv.next()
                for ch in range(6):
                    S.op("vector", lambda E, ch=ch: E.tensor_scalar(out=c_.t[:, ch, :], in0=x.t[:, ch, 0:128],
                                                                    scalar1=cw.t[:, ch, 0:1], scalar2=None, op0=ALU.mult),
                         reads=[x.b, cw.b], writes=[c_.b])
                    for k in (1, 2):
                        S.op("vector", lambda E, ch=ch, k=k: E.scalar_tensor_tensor(
                            out=c_.t[:, ch, :], in0=x.t[:, ch, k:k + 128], scalar=cw.t[:, ch, k:k + 1], in1=c_.t[:, ch, :],
                            op0=ALU.mult, op1=ALU.add), reads=[x.b, cw.b, c_.b], writes=[c_.b])
                    S.op("scalar", lambda E, ch=ch: E.activation(out=c_.t[:, ch, :], in_=c_.t[:, ch, :], func=AF.Silu,
                                                                 bias=cb.t[:, ch:ch + 1]), reads=[c_.b, cb.b], writes=[c_.b])
                bt = BTb.next()
                S.op("gpsimd", lambda E: E.tensor_copy(out=bt.t[:], in_=c_.t[:, 4, :]), reads=[c_.b], writes=[bt.b])
                cm = CTm.next()
                for g in range(2):
                    S.op("gpsimd", lambda E, g=g: E.tensor_scalar(out=cm.t[:, g, :], in0=c_.t[:, 5, :], scalar1=gmask.t[:, g:g + 1],
                                                                  scalar2=None, op0=ALU.mult), reads=[c_.b, gmask.b], writes=[cm.b])
                cmb = Cd.next()
                p = pX.next()
                for c in range(4):
                    S.op("tensor", lambda E, c=c: E.transpose(out=p.t[:, c * 128:(c + 1) * 128], in_=c_.t[:, c, :],
                                                               identity=ident.t[:]), reads=[c_.b, ident.b], writes=[p.b])
                xt = xtok.next()
                S.op("scalar", lambda E: E.activation(out=xt.t[:], in_=p.t[:, :], func=AF.Copy), reads=[p.b], writes=[xt.b])
                xb = xtb.next()
                S.op("gpsimd", lambda E: E.tensor_copy(out=xb.t[:], in_=xt.t[:]), reads=[xt.b], writes=[xb.b])
                p2 = pX.next()
                S.op("tensor", lambda E: E.transpose(out=p2.t[:, 0:128], in_=c_.t[:, 4, :], identity=ident.t[:]),
                     reads=[c_.b, ident.b], writes=[p2.b])
                btk = Btok.next()
                S.op("vector", lambda E: E.tensor_copy(out=btk.t[:], in_=p2.t[:, 0:128]), reads=[p2.b], writes=[btk.b])
                dT = dtT.next()
                S.dma("sync", dT.t[0:16, :], dap(pT_d, DR * T + t0, [[T, 16], [1, 128]]), writes=[dT.b])
                p3 = pX.next()
                S.op("tensor", lambda E: E.transpose(out=p3.t[:, 0:128], in_=dT.t[:, :], identity=ident.t[:]),
                     reads=[dT.b, ident.b], writes=[p3.b])
                dt = dt_.next()
                S.op("vector", lambda E: E.tensor_tensor(out=dt.t[:], in0=p3.t[:, 0:16], in1=dtb.t[:], op=ALU.add),
                     reads=[p3.b, dtb.b], writes=[dt.b])
                S.op("scalar", lambda E: E.activation(out=dt.t[:], in_=dt.t[:], func=AF.Exp), reads=[dt.b], writes=[dt.b])
                S.op("scalar", lambda E: E.activation(out=dt.t[:], in_=dt.t[:], func=AF.Ln, bias=1.0), reads=[dt.b], writes=[dt.b])
                g_ = gg.next()
                S.op("vector", lambda E: E.tensor_tensor(out=g_.t[:], in0=dt.t[:], in1=abc.t[:], op=ALU.mult),
                     reads=[dt.b, abc.b], writes=[g_.b])
                pc = pX.next()
                S.op("tensor", lambda E: E.matmul(pc.t[:, 0:8], TRI.t[:, d, :], g_.t[:, d * 8:(d + 1) * 8], start=True, stop=True),
                     reads=[TRI.b, g_.b], writes=[pc.b])
                S.op("tensor", lambda E: E.matmul(pc.t[:, 8:16], TRI.t[:, 2 + d, :], g_.t[:, d * 8:(d + 1) * 8], start=True, stop=True),
                     reads=[TRI.b, g_.b], writes=[pc.b])
                S.op("tensor", lambda E: E.matmul(pc.t[:, 16:24], ones.t[:], g_.t[:, d * 8:(d + 1) * 8], start=True, stop=True),
                     reads=[ones.b, g_.b], writes=[pc.b])
                ncum = cumc.next()
                S.op("vector", lambda E: E.tensor_scalar(out=ncum.t[:], in0=pc.t[:, 0:8], scalar1=-1.0, scalar2=None, op0=ALU.mult),
                     reads=[pc.b], writes=[ncum.b])
                w_ = ww.next()
                S.op("scalar", lambda E: E.activation(out=w_.t[:], in_=pc.t[:, 8:16], func=AF.Exp), reads=[pc.b], writes=[w_.b])
                S.op("vector", lambda E: E.tensor_tensor(out=w_.t[:], in0=w_.t[:], in1=dt.t[:, d * 8:(d + 1) * 8], op=ALU.mult),
                     reads=[w_.b, dt.b], writes=[w_.b])
                et = etot.next()
                S.op("scalar", lambda E: E.activation(out=et.t[:], in_=pc.t[:, 16:24], func=AF.Exp), reads=[pc.b], writes=[et.b])
                xw_ = xw.next()
                S.op("vector", lambda E: E.tensor_tensor(
                    out=xw_.t[:].rearrange("p (h d) -> p h d", h=8), in0=xt.t[:].rearrange("p (h d) -> p h d", h=8),
                    in1=w_.t[:].unsqueeze(2).broadcast_to([128, 8, 64]), op=ALU.mult), reads=[xt.b, w_.b], writes=[xw_.b])
                pG = []
                cmbf = at.next()
                for g in range(2):
                    pg = pA.next()
                    cgb = Cd.next()
                    S.op("gpsimd", lambda E, g=g, cgb=cgb: E.tensor_copy(out=cgb.t[:], in_=cm.t[:, g, :]), reads=[cm.b], writes=[cgb.b])
                    S.op("tensor", lambda E, pg=pg, cgb=cgb: E.matmul(pg.t[:, 0:128], bt.t[:], cgb.t[:], start=True, stop=True),
                         reads=[bt.b, cgb.b], writes=[pg.b])
                    pG.append(pg)
                po = pO.next()
                for h in range(8):
                    g = h // 4
                    gb_ = gb.next()
                    S.op("gpsimd", lambda E, gb_=gb_, h=h: E.tensor_copy(
                        out=gb_.t[:], in_=g_.t[:, d * 8 + h:d * 8 + h + 1].broadcast_to([128, 128])),
                        reads=[g_.b], writes=[gb_.b])
                    pd = pX.next()
                    S.op("tensor", lambda E, pd=pd, gb_=gb_: E.matmul(pd.t[:, 0:128], gb_.t[:], TRI.t[:, d, :], start=True, stop=True),
                         reads=[gb_.b, TRI.b], writes=[pd.b])
                    S.op("tensor", lambda E, pd=pd, gb_=gb_: E.matmul(pd.t[:, 128:256], gb_.t[:], TRI.t[:, d, :], start=True, stop=False),
                         reads=[gb_.b, TRI.b], writes=[pd.b])
                    S.op("tensor", lambda E, pd=pd: E.matmul(pd.t[:, 128:256], ident.t[:], TRI.t[:, 4 + d, :], start=False, stop=True),
                         reads=[ident.b, TRI.b], writes=[pd.b])
                    e_ = Eh.next()
                    S.op("scalar", lambda E, pd=pd, e_=e_: E.activation(out=e_.t[:], in_=pd.t[:, 0:128], func=AF.Exp),
                         reads=[pd.b], writes=[e_.b])
                    dm = Dm.next()
                    S.op("scalar", lambda E, pd=pd, dm=dm, h=h: E.activation(out=dm.t[:], in_=pd.t[:, 128:256], func=AF.Exp,
                                                                             bias=ncum.t[:, h:h + 1]),
                         reads=[pd.b, ncum.b], writes=[dm.b])
                    cd = Cd.next()
                    S.op("gpsimd", lambda E, cd=cd, e_=e_, g=g: E.tensor_tensor(out=cd.t[:], in0=cm.t[:, g, :], in1=e_.t[:], op=ALU.mult),
                         reads=[cm.b, e_.b], writes=[cd.b])
                    a_ = at.next()
                    S.op("vector", lambda E, a_=a_, dm=dm, g=g, h=h: E.scalar_tensor_tensor(
                        out=a_.t[:], in0=pG[g].t[:, 0:128], scalar=dt.t[:, d * 8 + h:d * 8 + h + 1], in1=dm.t[:],
                        op0=ALU.mult, op1=ALU.mult), reads=[pG[g].b, dt.b, dm.b], writes=[a_.b])
                    S.op("tensor", lambda E, a_=a_, h=h: E.matmul(po.t[:, h * 64:(h + 1) * 64], a_.t[:], xb.t[:, h * 64:(h + 1) * 64],
                                                                  start=(h == 0), stop=False), reads=[a_.b, xb.b], writes=[po.b])
                    S.op("tensor", lambda E, cd=cd, h=h: E.matmul(po.t[:, h * 64:(h + 1) * 64], cd.t[:], STbf[d][h].t[:],
                                                                  start=False, stop=True), reads=[cd.b, STbf[d][h].b], writes=[po.b])
                if d == 0:
                    S.op("scalar", lambda E: E.activation(out=YF.t[:, tt, :], in_=po.t[:, :], func=AF.Copy),
                         reads=[po.b], writes=[YF.b])
                for h in range(8):
                    ps_ = pX.next()
                    S.op("tensor", lambda E, ps_=ps_, h=h: E.matmul(ps_.t[:, 0:64], btk.t[:], xw_.t[:, h * 64:(h + 1) * 64],
                                                                    start=True, stop=True), reads=[btk.b, xw_.b], writes=[ps_.b])
                    S.op("vector", lambda E, ps_=ps_, h=h: E.scalar_tensor_tensor(
                        out=ST32[d][h].t[:], in0=ST32[d][h].t[:], scalar=et.t[:, h:h + 1], in1=ps_.t[:, 0:64],
                        op0=ALU.mult, op1=ALU.add), reads=[ST32[d][h].b, et.b, ps_.b], writes=[ST32[d][h].b])
                    S.op("gpsimd", lambda E, h=h: E.tensor_copy(out=STbf[d][h].t[:], in_=ST32[d][h].t[:]),
                         reads=[ST32[d][h].b], writes=[STbf[d][h].b])
                if d == 1:
                    y_ = ysum.next()
                    S.op("vector", lambda E: E.tensor_tensor(out=y_.t[:], in0=po.t[:, :], in1=YF.t[:, tt, :], op=ALU.add),
                         reads=[po.b, YF.b], writes=[y_.b])
                    q2 = ysq.next()
                    S.op("gpsimd", lambda E: E.tensor_tensor(
                        out=q2.t[:].rearrange("p (h d) -> p h d", h=8), in0=xt.t[:].rearrange("p (h d) -> p h d", h=8),
                        in1=Dbc.t[:].unsqueeze(2).broadcast_to([128, 8, 64]), op=ALU.mult), reads=[xt.b, Dbc.b], writes=[q2.b])
                    S.op("vector", lambda E: E.tensor_tensor(out=y_.t[:], in0=y_.t[:], in1=q2.t[:], op=ALU.add),
                         reads=[y_.b, q2.b], writes=[y_.b])
                    z_ = zT.next()
                    S.dma("sync", z_.t[:], dap(pT_d, ZR * T + t0, [[T, 128], [128 * T, 4], [1, 128]]), writes=[z_.b])
                    pz = pX.next()
                    for c in range(4):
                        S.op("tensor", lambda E, c=c: E.transpose(out=pz.t[:, c * 128:(c + 1) * 128], in_=z_.t[:, c, :],
                                                                   identity=ident.t[:]), reads=[z_.b, ident.b], writes=[pz.b])
                    zt = ztok.next()
                    S.op("scalar", lambda E: E.activation(out=zt.t[:], in_=pz.t[:, :], func=AF.Silu), reads=[pz.b], writes=[zt.b])
                    S.op("vector", lambda E: E.tensor_tensor(out=y_.t[:], in0=y_.t[:], in1=zt.t[:], op=ALU.mult),
                         reads=[y_.b, zt.b], writes=[y_.b])
                    S.op("gpsimd", lambda E: E.tensor_tensor(out=q2.t[:], in0=y_.t[:], in1=y_.t[:], op=ALU.mult),
                         reads=[y_.b], writes=[q2.b])
                    r1 = s1.next()
                    S.op("vector", lambda E: E.tensor_reduce(out=r1.t[:], in_=q2.t[:], axis=mybir.AxisListType.X, op=ALU.add),
                         reads=[q2.b], writes=[r1.b])
                    S.op("scalar", lambda E: E.activation(out=r1.t[:], in_=r1.t[:], func=AF.Sqrt, scale=1.0 / 512, bias=EPS),
                         reads=[r1.b], writes=[r1.b])
                    S.op("vector", lambda E: E.reciprocal(out=r1.t[:], in_=r1.t[:]), reads=[r1.b], writes=[r1.b])
                    S.op("vector", lambda E: E.scalar_tensor_tensor(out=y_.t[:], in0=y_.t[:], scalar=r1.t[:, 0:1], in1=ngb.t[:],
                                                                    op0=ALU.mult, op1=ALU.mult),
                         reads=[y_.b, r1.b, ngb.b], writes=[y_.b])
                    pt_ = pX.next()
                    for c in range(4):
                        S.op("tensor", lambda E, c=c: E.transpose(out=pt_.t[:, c * 128:(c + 1) * 128],
                                                                   in_=y_.t[:, c * 128:(c + 1) * 128], identity=ident.t[:]),
                             reads=[y_.b, ident.b], writes=[pt_.b])
                    r_ = res.next()
                    S.op("scalar", lambda E: E.activation(out=r_.t[:], in_=pt_.t[:, :].rearrange("p (c t) -> p c t", c=4), func=AF.Copy),
                         reads=[pt_.b], writes=[r_.b])
                    S.dma("sync", dap(oT_d, 1024 * T + t0, [[T, 128], [128 * T, 4], [1, 128]]), r_.t[:], reads=[r_.b], writes=[ob])

            seqs = [(0, LS, True, -1)] + [(LS + i * LP, LP, False, i) for i in range(NPR)]
            for (tok0, Lq, samp, pi) in seqs:
                for d in range(2):
                    for h in range(8):
                        g = h // 4
                        S.op("vector", lambda E, d=d, h=h: E.memset(ST32[d][h].t[:], 0.0), writes=[ST32[d][h].b])
                        if samp:
                            si = sto.next()
                            S.dma("sync", si.t[:], dap(self.state_ssd, ((l * 2 + d) * 8 + h) * 4096, [[64, 64], [1, 64]]),
                                  writes=[si.b])
                            pp = pX.next()
                            S.op("tensor", lambda E, pp=pp, si=si: E.transpose(out=pp.t[0:64, 0:64], in_=si.t[:],
                                                                                identity=ident.t[0:64, 0:64]),
                                 reads=[si.b, ident.b], writes=[pp.b])
                            so = sto.next()
                            S.op("vector", lambda E, pp=pp, so=so: E.tensor_copy(out=so.t[:], in_=pp.t[0:64, 0:64]),
                                 reads=[pp.b], writes=[so.b])
                            S.dma("sync", ST32[d][h].t[g * 64:(g + 1) * 64, :], so.t[:], reads=[so.b], writes=[ST32[d][h].b])
                        S.op("gpsimd", lambda E, d=d, h=h: E.tensor_copy(out=STbf[d][h].t[:], in_=ST32[d][h].t[:]),
                             reads=[ST32[d][h].b], writes=[STbf[d][h].b])
                nt = Lq // 128
                for d in range(2):
                    for tt in (range(nt) if d == 0 else range(nt - 1, -1, -1)):
                        do_tile(tok0, Lq, tt, d)
                if not samp:
                    for d in range(2):
                        for h in range(8):
                            g = h // 4
                            si = sto.next()
                            S.dma("sync", si.t[:], ST32[d][h].t[g * 64:(g + 1) * 64, :], reads=[ST32[d][h].b], writes=[si.b])
                            pp = pX.next()
                            S.op("tensor", lambda E, pp=pp, si=si: E.transpose(out=pp.t[0:64, 0:64], in_=si.t[:],
                                                                                identity=ident.t[0:64, 0:64]),
                                 reads=[si.b, ident.b], writes=[pp.b])
                            so = sto.next()
                            S.op("vector", lambda E, pp=pp, so=so: E.tensor_copy(out=so.t[:], in_=pp.t[0:64, 0:64]),
                                 reads=[pp.b], writes=[so.b])
                            S.dma("sync", dap(self.nssd_d, (((pi * DEPTH + l) * 2 + d) * 8 + h) * 4096, [[64, 64], [1, 64]]),
                                  so.t[:], reads=[so.b], writes=[Buf()])

    def hyena_phase(self, l, pT_d, oT_d):
        nc, S = self.nc, self.S
        HR = 4400
        with contextlib.ExitStack() as ea:
            def sb(name, shape, dt=F32):
                return Tile(ea.enter_context(nc.sbuf_tensor("s_h%s%d" % (name, l), shape, dt)))
            W1 = sb("W1", [128, 64]); W2 = sb("W2", [128, 64]); W3 = sb("W3", [128, 128])
            cols = sb("cols", [128, 8])
            hcw = sb("hcw", [128, 3, 3]); hcb = sb("hcb", [128, 3])
            hdn = sb("hdn", [128, LS])
            trow = sb("trow", [128, LS])
            filt = [[sb("filt%d%d" % (o, d), [128, LS]) for d in range(2)] for o in range(2)]
            u = sb("u", [128, LS]); accF = sb("accF", [128, LS]); accB = sb("accB", [128, LS])
            tmp = Ring(ea, nc, "htmp", 2, [128, LS + 2], F32)
            tbf = Ring(ea, nc, "htbf", 2, [128, LS], BF16)
            ub = sb("ub", [128, LS], BF16)
            pX = RingView(self.ps.tiles[0:6])
            ob = Buf("oT_d")
            S.dma("sync", W1.t[0:33, :], dap(self.hy_f_w1, l * 33 * 64, [[64, 33], [1, 64]]), writes=[W1.b])
            S.dma("sync", W2.t[0:64, :], dap(self.hy_f_w2, l * 64 * 64, [[64, 64], [1, 64]]), writes=[W2.b])
            S.dma("sync", cols.t[0:64, 0:1], dap(self.hy_f_b1, l * 64, [[1, 64], [1, 1]]), writes=[cols.b])
            S.dma("sync", cols.t[0:64, 1:2], dap(self.hy_f_b2, l * 64, [[1, 64], [1, 1]]), writes=[cols.b])
            S.dma("sync", cols.t[0:64, 2:3], dap(self.hy_sin_w, l * 64, [[1, 64], [1, 1]]), writes=[cols.b])

            def my_sin(A, L):
                s_ = tmp.next(); c_ = tmp.next()
                sv, cv, tv, av = s_.t[0:64, 0:L], c_.t[0:64, 0:L], accF.t[0:64, 0:L], A.t[0:64, 0:L]
                S.op("scalar", lambda E: E.activation(out=cv, in_=av, func=AF.Abs), reads=[A.b], writes=[c_.b])
                S.op("scalar", lambda E: E.activation(out=sv, in_=av, func=AF.Sin, scale=0.125), reads=[A.b], writes=[s_.b])
                S.op("vector", lambda E: E.tensor_scalar(out=cv, in0=cv, scalar1=-0.125, scalar2=math.pi / 2, op0=ALU.mult,
                                                         op1=ALU.add), reads=[c_.b], writes=[c_.b])
                S.op("scalar", lambda E: E.activation(out=cv, in_=cv, func=AF.Sin), reads=[c_.b], writes=[c_.b])
                for it in range(3):
                    S.op("gpsimd", lambda E: E.tensor_tensor(out=tv, in0=sv, in1=sv, op=ALU.mult), reads=[s_.b], writes=[accF.b])
                    S.op("vector", lambda E: E.scalar_tensor_tensor(out=sv, in0=sv, scalar=2.0, in1=cv, op0=ALU.mult, op1=ALU.mult),
                         reads=[s_.b, c_.b, accF.b], writes=[s_.b])
                    S.op("vector", lambda E: E.tensor_scalar(out=cv, in0=tv, scalar1=-2.0, scalar2=1.0, op0=ALU.mult, op1=ALU.add),
                         reads=[accF.b, s_.b], writes=[c_.b])
                S.op("gpsimd", lambda E: E.tensor_copy(out=av, in_=sv), reads=[s_.b], writes=[A.b])

            def build_hidden(L, z_d):
                z = tmp.next()
                S.dma("sync", z.t[0:33, 0:L], z_d.ap(), writes=[z.b])
                S.dma("sync", trow.t[:, 0:L], dap(z_d, 0, [[0, 128], [1, L]]), writes=[trow.b])
                blk = min(512, L)
                for b0 in range(0, L, blk):
                    p = pX.next()
                    S.op("tensor", lambda E, p=p, b0=b0: E.matmul(p.t[0:64, 0:blk], W1.t[0:33, :], z.t[0:33, b0:b0 + blk],
                                                                   start=True, stop=True), reads=[W1.b, z.b], writes=[p.b])
                    S.op("vector", lambda E, p=p, b0=b0: E.tensor_scalar(
                        out=u.t[0:64, b0:b0 + blk], in0=p.t[0:64, 0:blk], scalar1=cols.t[0:64, 0:1], scalar2=cols.t[0:64, 2:3],
                        op0=ALU.add, op1=ALU.mult), reads=[p.b, cols.b], writes=[u.b])
                my_sin(u, L)
                for b0 in range(0, L, blk):
                    p = pX.next()
                    S.op("tensor", lambda E, p=p, b0=b0: E.matmul(p.t[0:64, 0:blk], W2.t[0:64, :], u.t[0:64, b0:b0 + blk],
                                                                   start=True, stop=True), reads=[W2.b, u.b], writes=[p.b])
                    S.op("vector", lambda E, p=p, b0=b0: E.tensor_scalar(
                        out=hdn.t[0:64, b0:b0 + blk], in0=p.t[0:64, 0:blk], scalar1=cols.t[0:64, 1:2], scalar2=cols.t[0:64, 2:3],
                        op0=ALU.add, op1=ALU.mult), reads=[p.b, cols.b], writes=[hdn.b])
                my_sin(hdn, L)

            def build_filters(L, cc):
                blk = min(512, L)
                for o in range(2):
                    for d in range(2):
                        c0 = (o * 2 + d) * 512 + cc * 128
                        f = filt[o][d]
                        S.dma("sync", W3.t[0:64, :], dap(self.hy_f_w3, l * 64 * 2048 + c0, [[2048, 64], [1, 128]]), writes=[W3.b])
                        S.dma("sync", cols.t[:, 3:4], dap(self.hy_f_b3, l * 2048 + c0, [[1, 128], [1, 1]]), writes=[cols.b])
                        S.dma("sync", cols.t[:, 4:5], dap(self.hy_decay, l * 2048 + c0, [[1, 128], [1, 1]]), writes=[cols.b])
                        S.op("scalar", lambda E: E.activation(out=cols.t[:, 4:5], in_=cols.t[:, 4:5], func=AF.Abs),
                             reads=[cols.b], writes=[cols.b])
                        S.op("vector", lambda E: E.tensor_scalar(out=cols.t[:, 4:5], in0=cols.t[:, 4:5], scalar1=-1.0, scalar2=None,
                                                                 op0=ALU.mult), reads=[cols.b], writes=[cols.b])
                        for b0 in range(0, L, blk):
                            p = pX.next()
                            S.op("tensor", lambda E, p=p, b0=b0: E.matmul(p.t[:, 0:blk], W3.t[0:64, :], hdn.t[0:64, b0:b0 + blk],
                                                                           start=True, stop=True), reads=[W3.b, hdn.b], writes=[p.b])
                            S.op("scalar", lambda E, p=p, b0=b0, f=f: E.activation(
                                out=f.t[:, b0:b0 + blk], in_=p.t[:, 0:blk], func=AF.Identity, bias=cols.t[:, 3:4]),
                                reads=[p.b, cols.b], writes=[f.b])
                        e_ = tmp.next()
                        S.op("scalar", lambda E, e_=e_: E.activation(out=e_.t[:, 0:L], in_=trow.t[:, 0:L], func=AF.Exp,
                                                                     scale=cols.t[:, 4:5]), reads=[trow.b, cols.b], writes=[e_.b])
                        S.op("vector", lambda E, e_=e_, f=f: E.tensor_tensor(out=f.t[:, 0:L], in0=f.t[:, 0:L], in1=e_.t[:, 0:L],
                                                                             op=ALU.mult), reads=[f.b, e_.b], writes=[f.b])
                        S.op("scalar", lambda E, e_=e_, f=f: E.activation(out=e_.t[:, 0:L], in_=f.t[:, 0:L], func=AF.Abs),
                             reads=[f.b], writes=[e_.b])
                        S.op("vector", lambda E, e_=e_, d=d: E.tensor_reduce(out=cols.t[:, 6 + d:7 + d], in_=e_.t[:, 0:L],
                                                                             axis=mybir.AxisListType.X, op=ALU.add),
                             reads=[e_.b], writes=[cols.b])
                    S.op("vector", lambda E: E.scalar_tensor_tensor(out=cols.t[:, 6:7], in0=cols.t[:, 6:7], scalar=EPS, in1=cols.t[:, 7:8],
                                                                    op0=ALU.add, op1=ALU.add), reads=[cols.b], writes=[cols.b])
                    S.op("vector", lambda E: E.reciprocal(out=cols.t[:, 6:7], in_=cols.t[:, 6:7]), reads=[cols.b], writes=[cols.b])
                    for d in range(2):
                        S.op("vector" if d == 0 else "gpsimd", lambda E, o=o, d=d: E.tensor_scalar(
                            out=filt[o][d].t[:, 0:L], in0=filt[o][d].t[:, 0:L], scalar1=cols.t[:, 6:7], scalar2=None, op0=ALU.mult),
                            reads=[filt[o][d].b, cols.b], writes=[filt[o][d].b])
                    S.dma("sync", cols.t[:, 5:6], dap(self.hy_skip, (l * 2 + o) * 512 + cc * 128, [[1, 128], [1, 1]]), writes=[cols.b])
                    S.op("vector", lambda E, o=o: E.tensor_tensor(out=filt[o][0].t[:, 0:1], in0=filt[o][0].t[:, 0:1], in1=cols.t[:, 5:6],
                                                                  op=ALU.add), reads=[filt[o][0].b, cols.b], writes=[filt[o][0].b])

            def short_conv(tok0, L, cc, j, dst_ap, dst_b):
                x = tmp.next()
                S.op("vector", lambda E: E.memset(x.t[:, 0:1], 0.0), writes=[x.b])
                S.op("vector", lambda E: E.memset(x.t[:, L + 1:L + 2], 0.0), writes=[x.b])
                S.dma("sync", x.t[:, 1:L + 1], dap(pT_d, (HR + j * 512 + cc * 128) * T + tok0, [[T, 128], [1, L]]), writes=[x.b])
                S.op("vector", lambda E: E.tensor_scalar(out=dst_ap, in0=x.t[:, 0:L], scalar1=hcw.t[:, j, 0:1], scalar2=None,
                                                         op0=ALU.mult), reads=[x.b, hcw.b], writes=[dst_b])
                for k in (1, 2):
                    S.op("vector", lambda E, k=k: E.scalar_tensor_tensor(out=dst_ap, in0=x.t[:, k:k + L], scalar=hcw.t[:, j, k:k + 1],
                                                                         in1=dst_ap, op0=ALU.mult, op1=ALU.add),
                         reads=[x.b, hcw.b, dst_b], writes=[dst_b])
                S.op("scalar", lambda E: E.activation(out=dst_ap, in_=dst_ap, func=AF.Identity, bias=hcb.t[:, j:j + 1]),
                     reads=[dst_b, hcb.b], writes=[dst_b])

            def long_conv(L, o):
                hf, hb = filt[o][0], filt[o][1]
                S.op("gpsimd", lambda E: E.tensor_copy(out=ub.t[:, 0:L], in_=u.t[:, 0:L]), reads=[u.b], writes=[ub.b])
                S.op("vector", lambda E: E.tensor_scalar(out=accF.t[:, 0:L], in0=u.t[:, 0:L], scalar1=hf.t[:, 0:1], scalar2=None,
                                                         op0=ALU.mult), reads=[u.b, hf.b], writes=[accF.b])
                S.op("gpsimd", lambda E: E.memset(accB.t[:, 0:L], 0.0), writes=[accB.b])
                for tau in range(1, L):
                    S.op("vector", lambda E, tau=tau: E.scalar_tensor_tensor(
                        out=accF.t[:, tau:L], in0=ub.t[:, 0:L - tau], scalar=hf.t[:, tau:tau + 1], in1=accF.t[:, tau:L],
                        op0=ALU.mult, op1=ALU.add), reads=[ub.b, hf.b, accF.b], writes=[accF.b])
                    t_ = tbf.next()
                    S.op("scalar", lambda E, tau=tau, t_=t_: E.activation(out=t_.t[:, 0:L - tau], in_=u.t[:, tau:L], func=AF.Copy,
                                                                          scale=hb.t[:, tau:tau + 1]), reads=[u.b, hb.b], writes=[t_.b])
                    S.op("gpsimd", lambda E, tau=tau, t_=t_: E.tensor_tensor(out=accB.t[:, 0:L - tau], in0=accB.t[:, 0:L - tau],
                                                                             in1=t_.t[:, 0:L - tau], op=ALU.add),
                         reads=[accB.b, t_.b], writes=[accB.b])
                S.op("vector", lambda E: E.tensor_tensor(out=accF.t[:, 0:L], in0=accF.t[:, 0:L], in1=accB.t[:, 0:L], op=ALU.add),
                     reads=[accF.b, accB.b], writes=[accF.b])

            def do_seq(tok0, L, cc):
                short_conv(tok0, L, cc, 0, u.t[:, 0:L], u.b)
                long_conv(L, 0)
                g1 = tmp.next()
                short_conv(tok0, L, cc, 1, g1.t[:, 0:L], g1.b)
                S.op("vector", lambda E: E.tensor_tensor(out=u.t[:, 0:L], in0=g1.t[:, 0:L], in1=accF.t[:, 0:L], op=ALU.mult),
                     reads=[g1.b, accF.b], writes=[u.b])
                long_conv(L, 1)
                g2 = tmp.next()
                short_conv(tok0, L, cc, 2, g2.t[:, 0:L], g2.b)
                S.op("vector", lambda E: E.tensor_tensor(out=g2.t[:, 0:L], in0=g2.t[:, 0:L], in1=accF.t[:, 0:L], op=ALU.mult),
                     reads=[g2.b, accF.b], writes=[g2.b])
                S.dma("sync", dap(oT_d, (1536 + cc * 128) * T + tok0, [[T, 128], [1, L]]), g2.t[:, 0:L], reads=[g2.b], writes=[ob])

            for (L, z_d, seqs) in ((LS, self.hy_z_s, [0]), (LP, self.hy_z_p, [LS + i * LP for i in range(NPR)])):
                build_hidden(L, z_d)
                for cc in range(4):
                    for k in range(3):
                        S.dma("sync", hcw.t[:, :, k], dap(self.hy_conv_w, (l * 3 + k) * 1536 + cc * 128, [[1, 128], [512, 3]]),
                              writes=[hcw.b], **NCD)
                    S.dma("sync", hcb.t[:], dap(self.hy_conv_b, l * 1536 + cc * 128, [[1, 128], [512, 3]]), writes=[hcb.b], **NCD)
                    build_filters(L, cc)
                    for tok0 in seqs:
                        do_seq(tok0, L, cc)


_CACHE = {}


def _rope_consts():
    t = np.arange(LS)
    r = (t // 64).astype(np.float32)
    col = (t % 64).astype(np.float32)
    inv = (10000.0 ** (-np.arange(8, dtype=np.float32) / 8)).astype(np.float32)
    ang = np.concatenate([r[:, None] * inv, col[:, None] * inv], axis=-1).astype(np.float32)
    idx = (np.arange(128) % 32) % 16
    ropec = np.ascontiguousarray(np.cos(ang)[:, idx].T.astype(np.float32))
    ropes = np.ascontiguousarray(np.sin(ang)[:, idx].T.astype(np.float32))
    R = np.zeros((128, 128), np.float32)
    for p in range(128):
        if p % 32 < 16:
            R[p + 16, p] = -1.0
        else:
            R[p - 16, p] = 1.0
    rowmask = np.zeros((128, 4), np.float32)
    for p in range(128):
        rowmask[p, p // 32] = 1.0
    i = np.arange(128)
    le = (i[:, None] <= i[None, :]).astype(np.float32)
    ge = (i[:, None] >= i[None, :]).astype(np.float32)
    gt = (i[:, None] > i[None, :]).astype(np.float32)
    lt = (i[:, None] < i[None, :]).astype(np.float32)
    c = np.float32(-1.0 / 16.0)
    gla_tri = np.stack([c * le, c * ge, c * gt, c * lt, le, ge], axis=0).astype(np.float32)
    bm = (np.arange(128)[:, None] // 32 == np.arange(256)[None, :] // 64).astype(np.float32)
    def zfeat(L):
        t = (np.arange(L, dtype=np.float32) / np.float32(L)).astype(np.float32)[:, None]
        bands = np.arange(1, 17, dtype=np.float32)
        ang = (np.float32(2.0 * math.pi) * t * bands).astype(np.float32)
        return np.ascontiguousarray(np.concatenate([t, np.cos(ang), np.sin(ang)], axis=-1).T.astype(np.float32))
    hyc = dict(hy_z_s=zfeat(LS), hy_z_p=zfeat(LP))
    neg = np.float32(-30000.0)
    ssd_tri = np.stack([le, ge, gt, lt, neg * (1.0 - le), neg * (1.0 - ge)], axis=0).astype(np.float32)
    gmask = np.zeros((128, 2), np.float32)
    gmask[:64, 0] = 1.0
    gmask[64:, 1] = 1.0
    return dict(ropec=ropec, ropes=ropes, roper=R, rowmask=rowmask, gla_tri=np.ascontiguousarray(gla_tri),
                gla_bmask=np.ascontiguousarray(bm), ssd_tri=np.ascontiguousarray(ssd_tri), ssd_gmask=gmask, **hyc)


def _get_prog(**kw):
    key = tuple(sorted(kw.items()))
    if key not in _CACHE:
        p = Prog(**kw)
        p.build()
        _CACHE[key] = p
    return _CACHE[key]


def kernel(x_prompt, x_sample, cache_diff_k, cache_diff_v, state_gla, state_ssd, c, c_ctx,
           ada_w, ada_b, norm1_g, norm2_g, w_in, w_out,
           gla_gate_w, gla_gate_b, gla_norm_g, diff_lambda, diff_norm_g,
           ssd_conv_w, ssd_conv_b, ssd_dt_bias, ssd_a_log, ssd_d, ssd_norm_g,
           hy_conv_w, hy_conv_b, hy_f_w1, hy_f_b1, hy_f_w2, hy_f_b2, hy_f_w3, hy_f_b3,
           hy_sin_w, hy_decay, hy_skip, mlp_w1, mlp_w2, final_g, _prog_kw=None):
    f = lambda a: np.ascontiguousarray(np.asarray(a, dtype=np.float32))
    prog = _get_prog(**(_prog_kw or {}))
    shared = dict(ada_w=f(ada_w), ada_b=f(ada_b), norm1_g=f(norm1_g), norm2_g=f(norm2_g), w_in=f(w_in),
                  w_out=f(w_out), mlp_w1=f(mlp_w1), mlp_w2=f(mlp_w2), final_g=f(final_g),
                  ident=np.eye(128, dtype=np.float32), diff_lambda=f(diff_lambda), diff_norm_g=f(diff_norm_g),
                  gla_gate_w=f(gla_gate_w), gla_gate_b=f(gla_gate_b), gla_norm_g=f(gla_norm_g),
                  ssd_conv_w=f(ssd_conv_w), ssd_conv_b=f(ssd_conv_b), ssd_dt_bias=f(ssd_dt_bias), ssd_a_log=f(ssd_a_log),
                  ssd_d=f(ssd_d), ssd_norm_g=f(ssd_norm_g),
                  hy_conv_w=f(hy_conv_w), hy_conv_b=f(hy_conv_b), hy_f_w1=f(hy_f_w1), hy_f_b1=f(hy_f_b1), hy_f_w2=f(hy_f_w2),
                  hy_f_b2=f(hy_f_b2), hy_f_w3=f(hy_f_w3), hy_f_b3=f(hy_f_b3), hy_sin_w=f(hy_sin_w), hy_decay=f(hy_decay),
                  hy_skip=f(hy_skip))
    state_ssd = f(state_ssd)
    state_gla = f(state_gla)
    shared.update(_rope_consts())
    cache_diff_k = f(cache_diff_k)
    cache_diff_v = f(cache_diff_v)
    x_prompt = f(x_prompt)
    x_sample = f(x_sample)
    c = f(c)
    c_ctx = f(c_ctx)
    in_maps = []
    for i in range(NCORES):
        xin = np.concatenate([x_sample[i], x_prompt[NPR * i:NPR * (i + 1)].reshape(NPR * LP, D)], axis=0)
        m = dict(shared)
        m["xin"] = np.ascontiguousarray(xin)
        m["c2"] = np.ascontiguousarray(np.stack([c[i], c_ctx], axis=0))
        m["cache_k"] = cache_diff_k[i].reshape(DEPTH, 512, 512)
        m["cache_v"] = cache_diff_v[i].reshape(DEPTH, 512, 512)
        m["state_gla"] = state_gla[i]
        m["state_ssd"] = state_ssd[i]
        in_maps.append({k: v for k, v in m.items() if k in prog.inp})
    res = run_bass_kernel_spmd(prog.nc, in_maps, core_ids=list(range(NCORES)))
    if prog.debug:
        global DBG
        DBG = res.results
    ys = [r["y"] for r in res.results]
    y_sample = np.stack([y[:LS] for y in ys], axis=0)
    y_prompt = np.concatenate([y[LS:].reshape(NPR, LP, D) for y in ys], axis=0)
    if not prog.mixers:
        return y_prompt, y_sample
    B = NCORES * NPR
    new_k = np.concatenate([r["nk"] for r in res.results], axis=0).reshape(B, DEPTH, LP, 8, 64)
    new_v = np.concatenate([r["nv"] for r in res.results], axis=0).reshape(B, DEPTH, LP, 8, 64)
    new_gla = np.concatenate([r["ngla"] for r in res.results], axis=0)
    if ENABLE_SSD:
        new_ssd = np.concatenate([r["nssd"] for r in res.results], axis=0)
    else:
        new_ssd = np.zeros((B, DEPTH, 2, 8, 64, 64), np.float32)
    return y_prompt, y_sample, new_k, new_v, new_gla, new_ssd
```
